# Optimizing a Trainium2 kernel written in Bass

```python
import math
import jax, jax.numpy as jnp
from jax import lax
import numpy as np

D_MODEL = 1024
BATCH = 4
SEQ = 4096
DEPTH = 1
DEC_BATCH = 16
DEC_SEQ = 2048
PAST_LEN = 128

HEAD_DIM = 64
DIL_PATTERNS = ((128, 1), (512, 4), (2048, 16))
N_DIL_GROUPS = len(DIL_PATTERNS)
A_HEADS_PER_GROUP = 4
A_HEADS = N_DIL_GROUPS * A_HEADS_PER_GROUP
A_WIDTH = A_HEADS * HEAD_DIM
A_OUT_WIDTH = A_HEADS_PER_GROUP * HEAD_DIM
B_Q_HEADS = 8
B_KV_HEADS = 2
B_GROUP = B_Q_HEADS // B_KV_HEADS
B_Q_WIDTH = B_Q_HEADS * HEAD_DIM
B_KV_WIDTH = B_KV_HEADS * HEAD_DIM
B_RADIUS = 128
BLOCK = 128
N_HEADS_TOTAL = A_HEADS + B_Q_HEADS
N_BUCKETS = 32
MAX_DISTANCE = 1024
N_BRANCH = 2
IN_WIDTH = 3 * A_WIDTH + B_Q_WIDTH + 2 * B_KV_WIDTH + N_BRANCH * D_MODEL
D_FF = 2816
CONV_WIDTH = 3
EPS = 1e-6
NEG = -1e30

kernel_name = 'hybrid_dilated_window_encoder'


def rmsnorm(x, g):
    xf = x.astype(jnp.float32)
    y = xf * lax.rsqrt(jnp.mean(xf * xf, axis=-1, keepdims=True) + EPS)
    return (y * g.astype(jnp.float32)).astype(x.dtype)


def rel_bucket(rel):
    nb = N_BUCKETS // 2
    max_exact = nb // 2
    ret = jnp.where(rel > 0, nb, 0)
    n = jnp.abs(rel)
    nf = jnp.maximum(n, 1).astype(jnp.float32)
    large = max_exact + (jnp.log(nf / max_exact) / math.log(MAX_DISTANCE / max_exact)
                         * (nb - max_exact)).astype(jnp.int32)
    large = jnp.minimum(large, nb - 1)
    return ret + jnp.where(n < max_exact, n, large)


def dilated_group(q, k, v, bias, window, dilation):
    b, s, h, dh = q.shape
    n_side = window // (2 * dilation)
    pad = n_side * dilation
    offsets = jnp.arange(-n_side, n_side + 1, dtype=jnp.int32) * dilation
    kp = jnp.pad(k, ((0, 0), (pad, pad), (0, 0), (0, 0)))
    vp = jnp.pad(v, ((0, 0), (pad, pad), (0, 0), (0, 0)))
    scale = HEAD_DIM ** -0.5
    bias_f = bias.astype(jnp.float32)

    def block(i0):
        qi = lax.dynamic_slice_in_dim(q, i0, BLOCK, axis=1)
        pos = i0 + jnp.arange(BLOCK, dtype=jnp.int32)
        idx = pos[:, None] + offsets[None, :]
        valid = (idx >= 0) & (idx < s)
        kg = kp[:, idx + pad]
        vg = vp[:, idx + pad]
        logits = jnp.einsum('bthd,btkhd->bhtk', qi, kg).astype(jnp.float32) * scale
        logits = jnp.where(valid[None, None], logits + bias_f[None, :, None, :], NEG)
        lse = jax.nn.logsumexp(logits, axis=-1)
        p = jnp.exp(logits - lse[..., None])
        o = jnp.einsum('bhtk,btkhd->bthd', p.astype(v.dtype), vg)
        return o, lse

    starts = jnp.arange(s // BLOCK, dtype=jnp.int32) * BLOCK
    o, lse = lax.map(block, starts)
    o = o.transpose(1, 0, 2, 3, 4).reshape(b, s, h, dh)
    lse = lse.transpose(1, 2, 0, 3).reshape(b, h, s)
    return o, lse


def dilated_mixture(q, k, v, rel_bias):
    outs, lses = [], []
    for gi, (window, dilation) in enumerate(DIL_PATTERNS):
        hs = slice(gi * A_HEADS_PER_GROUP, (gi + 1) * A_HEADS_PER_GROUP)
        n_side = window // (2 * dilation)
        offs = jnp.arange(-n_side, n_side + 1, dtype=jnp.int32) * dilation
        bias = rel_bias[rel_bucket(offs)][:, hs].T
        o, lse = dilated_group(q[:, :, hs], k[:, :, hs], v[:, :, hs], bias, window, dilation)
        outs.append(o)
        lses.append(lse)
    w = jax.nn.softmax(jnp.stack(lses, 0), axis=0)
    out = jnp.einsum('gbhs,gbshd->bshd', w, jnp.stack(outs, 0).astype(jnp.float32))
    return out.astype(q.dtype)


def window_gqa(q, k, v, rel_bias, sink):
    b, s, hq, dh = q.shape
    span = BLOCK + 2 * B_RADIUS
    kp = jnp.pad(k, ((0, 0), (B_RADIUS, B_RADIUS), (0, 0), (0, 0)))
    vp = jnp.pad(v, ((0, 0), (B_RADIUS, B_RADIUS), (0, 0), (0, 0)))
    qg = q.reshape(b, s, B_KV_HEADS, B_GROUP, dh)
    rel = (jnp.arange(span, dtype=jnp.int32)[None, :] - B_RADIUS) - jnp.arange(BLOCK, dtype=jnp.int32)[:, None]
    band = jnp.abs(rel) <= B_RADIUS
    bias = rel_bias[rel_bucket(rel)][..., A_HEADS:]
    bias = bias.transpose(2, 0, 1).reshape(B_KV_HEADS, B_GROUP, BLOCK, span).astype(jnp.float32)
    sink_f = sink.astype(jnp.float32).reshape(B_KV_HEADS, B_GROUP)[None, :, :, None, None]
    scale = HEAD_DIM ** -0.5

    def block(i0):
        qi = lax.dynamic_slice_in_dim(qg, i0, BLOCK, axis=1)
        ki = lax.dynamic_slice_in_dim(kp, i0, span, axis=1)
        vi = lax.dynamic_slice_in_dim(vp, i0, span, axis=1)
        kpos = i0 - B_RADIUS + jnp.arange(span, dtype=jnp.int32)
        valid = band & ((kpos >= 0) & (kpos < s))[None, :]
        logits = jnp.einsum('btcgd,bucd->bcgtu', qi, ki).astype(jnp.float32) * scale
        logits = jnp.where(valid, logits + bias[None], NEG)
        m = jnp.maximum(jnp.max(logits, axis=-1, keepdims=True), sink_f)
        e = jnp.exp(logits - m)
        p = e / (jnp.sum(e, axis=-1, keepdims=True) + jnp.exp(sink_f - m))
        return jnp.einsum('bcgtu,bucd->btcgd', p.astype(v.dtype), vi)

    starts = jnp.arange(s // BLOCK, dtype=jnp.int32) * BLOCK
    o = lax.map(block, starts)
    return o.transpose(1, 0, 2, 3, 4, 5).reshape(b, s, hq * dh)


def dwconv(u, w, bias):
    c = u.shape[-1]
    y = lax.conv_general_dilated(u, w[:, None, :].astype(u.dtype), window_strides=(1,),
                                 padding=((CONV_WIDTH // 2, CONV_WIDTH // 2),),
                                 dimension_numbers=('NWC', 'WIO', 'NWC'),
                                 feature_group_count=c)
    return y + bias


def encoder_layer(x, g_attn, w_in, b_gate, rel_bias, sink, w_a_out, w_b_out, w_o,
                  g_ffn, w_up, conv_w, conv_b, w_down):
    b, s, _ = x.shape
    h = rmsnorm(x, g_attn)
    proj = h @ w_in
    cuts = np.cumsum([A_WIDTH, A_WIDTH, A_WIDTH, B_Q_WIDTH, B_KV_WIDTH, B_KV_WIDTH]).tolist()
    qa, ka, va, qb, kb, vb, gates = jnp.split(proj, cuts, axis=-1)
    a = dilated_mixture(qa.reshape(b, s, A_HEADS, HEAD_DIM), ka.reshape(b, s, A_HEADS, HEAD_DIM),
                        va.reshape(b, s, A_HEADS, HEAD_DIM), rel_bias)
    a = a.reshape(b, s, A_OUT_WIDTH) @ w_a_out
    bo = window_gqa(qb.reshape(b, s, B_Q_HEADS, HEAD_DIM), kb.reshape(b, s, B_KV_HEADS, HEAD_DIM),
                    vb.reshape(b, s, B_KV_HEADS, HEAD_DIM), rel_bias, sink)
    bo = bo @ w_b_out
    g_a, g_b = jnp.split(jax.nn.sigmoid(gates + b_gate), 2, axis=-1)
    x = x + (g_a * a + g_b * bo) @ w_o
    u = dwconv(rmsnorm(x, g_ffn) @ w_up, conv_w, conv_b)
    u_gate, u_val = jnp.split(u, 2, axis=-1)
    return x + (jax.nn.gelu(u_gate) * u_val) @ w_down


def trunk(x, g_attn, w_in, b_gate, rel_bias, sink, w_a_out, w_b_out, w_o,
          g_ffn, w_up, conv_w, conv_b, w_down, g_final):
    for l in range(DEPTH):
        x = encoder_layer(x, g_attn[l], w_in[l], b_gate[l], rel_bias, sink[l], w_a_out[l],
                          w_b_out[l], w_o[l], g_ffn[l], w_up[l], conv_w[l], conv_b[l], w_down[l])
    return rmsnorm(x, g_final)


def setup_inputs(seed: int = 0) -> dict:
    key = jax.random.key(seed)
    ks = jax.random.split(key, 16)
    f = jnp.float32
    L = DEPTH

    def nrm(k, shape, scale):
        return jax.random.normal(k, shape, f) * scale

    return {
        'x_prompt': nrm(ks[0], (BATCH, SEQ, D_MODEL), 1.0),
        'x_sample': nrm(ks[1], (DEC_BATCH, DEC_SEQ, D_MODEL), 1.0),
        'g_attn': 1.0 + nrm(ks[2], (L, D_MODEL), 0.02),
        'w_in': nrm(ks[3], (L, D_MODEL, IN_WIDTH), D_MODEL ** -0.5),
        'b_gate': nrm(ks[4], (L, N_BRANCH * D_MODEL), 0.02),
        'rel_bias': nrm(ks[5], (N_BUCKETS, N_HEADS_TOTAL), 0.5),
        'sink': nrm(ks[6], (L, B_Q_HEADS), 0.5),
        'w_a_out': nrm(ks[7], (L, A_OUT_WIDTH, D_MODEL), A_OUT_WIDTH ** -0.5),
        'w_b_out': nrm(ks[8], (L, B_Q_WIDTH, D_MODEL), B_Q_WIDTH ** -0.5),
        'w_o': nrm(ks[9], (L, D_MODEL, D_MODEL), D_MODEL ** -0.5),
        'g_ffn': 1.0 + nrm(ks[10], (L, D_MODEL), 0.02),
        'w_up': nrm(ks[11], (L, D_MODEL, 2 * D_FF), D_MODEL ** -0.5),
        'conv_w': nrm(ks[12], (L, CONV_WIDTH, 2 * D_FF), CONV_WIDTH ** -0.5),
        'conv_b': nrm(ks[13], (L, 2 * D_FF), 0.02),
        'w_down': nrm(ks[14], (L, D_FF, D_MODEL), D_FF ** -0.5),
        'g_final': 1.0 + nrm(ks[15], (D_MODEL,), 0.02),
    }


def reference(x_prompt, x_sample, g_attn, w_in, b_gate, rel_bias, sink, w_a_out, w_b_out, w_o,
              g_ffn, w_up, conv_w, conv_b, w_down, g_final):
    y_prompt = trunk(x_prompt, g_attn, w_in, b_gate, rel_bias, sink, w_a_out, w_b_out, w_o,
                     g_ffn, w_up, conv_w, conv_b, w_down, g_final)
    y_sample = trunk(x_sample, g_attn, w_in, b_gate, rel_bias, sink, w_a_out, w_b_out, w_o,
                     g_ffn, w_up, conv_w, conv_b, w_down, g_final)
    return (y_prompt, y_sample)
```

```python
import math
from contextlib import ExitStack

import numpy as np

import concourse.bass as bass
import concourse.mybir as mybir
from concourse.bass_utils import run_bass_kernel_spmd

F32 = mybir.dt.float32
BF16 = mybir.dt.bfloat16
ALU = mybir.AluOpType
AF = mybir.ActivationFunctionType

NCORES = 8
TOK = 6144
D = 1024
UNIT = 2048
NUNIT = 3
DFF = 2816
NCP = 22
EPS = 1e-6
PAD = 256
GELU_K = 0.7978845608028654
DILS = (1, 4, 16)

ENGS = ("sp", "act", "dve", "pool", "pe")


class Res:
    __slots__ = ("name", "w", "r")

    def __init__(self, name=""):
        self.name = name
        self.w = None
        self.r = {}


class Sched:
    def __init__(self, nc, stack):
        self.nc = nc
        self.stack = stack
        self.q = {e: [] for e in ENGS}
        self.sem = {}
        self.val = {}
        for e in ENGS:
            self.sem[e] = stack.enter_context(nc.semaphore("sem_" + e))
            self.val[e] = 0
        self.waited = {e: {} for e in ENGS}
        self.nchan = 0
        self.stopped = False

    def chan(self):
        sid = "dma%d" % self.nchan
        self.nchan += 1
        self.sem[sid] = self.stack.enter_context(self.nc.semaphore(sid))
        self.val[sid] = 0
        return sid

    def op(self, eng, fn, reads=(), writes=(), extra=(), chan=None, signal=True, self_wait=False):
        if self.stopped:
            return None
        deps = {}

        def need(ev):
            if ev is None:
                return
            if deps.get(ev[0], 0) < ev[1]:
                deps[ev[0]] = ev[1]

        for r in reads:
            need(r.w)
        for w in writes:
            need(w.w)
            for sid, v in w.r.items():
                need((sid, v))
        for e in extra:
            need(e)
        waits = []
        wd = self.waited[eng]
        for sid, v in deps.items():
            if sid == eng and eng == "pe" and not self_wait:
                continue
            if wd.get(sid, 0) < v:
                wd[sid] = v
                waits.append((sid, v))
        if chan is not None:
            self.val[chan] += 16
            ev = (chan, self.val[chan])
            inc = (chan, 16)
        elif signal:
            self.val[eng] += 1
            ev = (eng, self.val[eng])
            inc = (eng, 1)
        else:
            ev = None
            inc = None
        self.q[eng].append((waits, fn, inc))
        if ev is not None:
            for r in reads:
                if r.r.get(ev[0], 0) < ev[1]:
                    r.r[ev[0]] = ev[1]
            for w in writes:
                w.w = ev
                w.r = {}
        return ev

    def barrier(self):
        if self.stopped:
            return
        snap = dict(self.val)
        for e in ENGS:
            waits = []
            for sid, v in snap.items():
                if v == 0 or sid == e:
                    continue
                if self.waited[e].get(sid, 0) < v:
                    self.waited[e][sid] = v
                    waits.append((sid, v))
            if waits:
                self.q[e].append((waits, None, None))

    def emit(self, eng_name, e):
        for waits, fn, inc in self.q[eng_name]:
            for sid, v in waits:
                e.wait_ge(self.sem[sid], v)
            if fn is None:
                continue
            inst = fn(e)
            if inc is not None:
                inst.then_inc(self.sem[inc[0]], inc[1])


def _rel_bucket(rel):
    rel = np.asarray(rel, dtype=np.int64)
    nb = 16
    max_exact = 8
    ret = np.where(rel > 0, nb, 0)
    n = np.abs(rel)
    nf = np.maximum(n, 1).astype(np.float32)
    large = max_exact + (np.log(nf / np.float32(max_exact)) / np.float32(math.log(1024 / max_exact))
                         * np.float32(nb - max_exact)).astype(np.int32)
    large = np.minimum(large, nb - 1)
    return ret + np.where(n < max_exact, n, large)


def _onehot_tables():
    oh = np.zeros((32, 4 * 512), np.float32)
    for t in range(4):
        d = DILS[t] if t < 3 else 1
        R = 64 if t < 3 else 128
        for delta in range(-R, R + 1):
            b = int(_rel_bucket(delta * d))
            oh[b, t * 512 + delta + PAD] = 1.0
    return oh


def geom(d, left, right):
    n = UNIT // d
    klo = -64 if left else 0
    khi = n + (64 if right else 0)
    nk = khi - klo
    kblocks = []
    s = klo
    while s < khi:
        kblocks.append((s, min(128, khi - s)))
        s += 128
    qblocks = []
    i = -1
    while True:
        qs = klo + 64 + 128 * i
        if qs >= n:
            break
        v0, v1 = max(qs, 0), min(qs + 128, n)
        if v1 > v0:
            kbs = []
            for side, bi in ((0, i), (1, i + 1)):
                if 0 <= bi < len(kblocks):
                    kbs.append((side, bi))
            qblocks.append((qs, v0, v1, kbs))
        i += 1
    return n, klo, khi, nk, kblocks, qblocks


def geom_b(left, right):
    n = UNIT
    klo = -128 if left else 0
    khi = n + (128 if right else 0)
    nk = khi - klo
    kblocks = [(s, 128) for s in range(klo, khi, 128)]
    qblocks = []
    for i in range(n // 128):
        qs = 128 * i
        kbs = []
        for m in range(3):
            st = qs - 128 + 128 * m
            if klo <= st < khi:
                kbs.append((m, (st - klo) // 128))
        qblocks.append((qs, qs, qs + 128, kbs))
    return n, klo, khi, nk, kblocks, qblocks


class Ctx:
    pass


def mm_chain(S, cx, steps, reads, wres, transpose=False, step_reads=None):
    n = len(steps)
    ev = None
    for i, (o, a, b) in enumerate(steps):
        first, last = (i == 0), (i == n - 1)
        if transpose:
            fn = (lambda e, o=o, a=a, b=b: e.transpose(o, a, b))
        else:
            fn = (lambda e, o=o, a=a, b=b, first=first, last=last: e.matmul(o, a, b, start=first, stop=last))
        rd = list(reads) + (list(step_reads[i]) if step_reads is not None else [])
        ev = S.op("pe", fn, reads=rd, writes=wres if (first or last) else (), signal=last)
    return ev


STOP = [None]
ATT_CUT = [5]


class _Stop(Exception):
    pass


def build_program():
    nc = bass.Bass("TRN2", target_bir_lowering=False)
    stack = ExitStack()

    def dram_in(name, shape, dt=F32):
        return nc.dram_tensor(name, list(shape), dt, kind="ExternalInput").ap()

    xin = dram_in("xin", [TOK, D])
    yout = nc.dram_tensor("yout", [TOK, D], F32, kind="ExternalOutput").ap()
    flagv_d = dram_in("flagv", [128, 4])
    wqkv_d = dram_in("wqkv", [128, 6 * 3072])
    wbkv_d = dram_in("wbkv", [128, 2048])
    wbq_d = dram_in("wbq", [128, 4 * 1024])
    wm_d = dram_in("wm", [128, 8 * 2816])
    wo_d = dram_in("wo", [128, 8 * 1024])
    wup_d = dram_in("wup", [128, NCP * 2048])
    wdn_d = dram_in("wdn", [128, 8 * 2816])
    small_d = dram_in("small", [128, 224])
    gfinb_d = dram_in("gfinb", [128, 1024])
    relb_d = dram_in("relb", [32, 20])
    oh_d = dram_in("oh", [32, 2048])
    ident_d = dram_in("ident", [128, 128])
    er_d = nc.dram_tensor("er_scr", [4 * 20, 512], F32, kind="Internal").ap()
    esave_d = nc.dram_tensor("esave", [128, 6144], F32, kind="Internal").ap()
    dscr_d = nc.dram_tensor("dscr", [2, 2048], F32, kind="Internal").ap()
    rscr_d = nc.dram_tensor("rscr", [2, 2048], F32, kind="Internal").ap()

    S = Sched(nc, stack)
    cx = Ctx()

    XN_OFF = 0
    OAB_OFF = 49152
    CONST_OFF = 73728
    R_OFF = 81920
    R_SIZE = 124 * 1024
    TOTAL = R_OFF + R_SIZE
    arena = stack.enter_context(nc.sbuf_tensor("arena", [128, TOTAL // 2], BF16))
    A16 = arena[:]
    A32 = arena[:].bitcast(F32)
    PS16 = A16.ap[0][0]
    PS32 = A32.ap[0][0]
    psum = stack.enter_context(nc.psum_tensor("psum", [128, 4096], F32))
    P32 = psum[:]
    P16 = psum[:].bitcast(BF16)
    PP32 = P32.ap[0][0]
    PP16 = P16.ap[0][0]

    def v16(boff, dims, p0=0, np_=128):
        assert boff % 2 == 0
        return bass.AP(A16.tensor, p0 * PS16 + boff // 2, [[PS16, np_]] + [list(x) for x in dims])

    def v32(boff, dims, p0=0, np_=128):
        assert boff % 4 == 0
        return bass.AP(A32.tensor, p0 * PS32 + boff // 4, [[PS32, np_]] + [list(x) for x in dims])

    def pv32(col, dims, p0=0, np_=128):
        return bass.AP(P32.tensor, p0 * PP32 + col, [[PP32, np_]] + [list(x) for x in dims])

    def pv16(col, dims, p0=0, np_=128):
        return bass.AP(P16.tensor, p0 * PP16 + col, [[PP16, np_]] + [list(x) for x in dims])

    def dv(ap, off, dims):
        return bass.AP(ap.tensor, ap.offset + off, [list(x) for x in dims])

    banks = [Res("bank%d" % i) for i in range(8)]
    bank_rr = [0]

    def next_bank():
        b = bank_rr[0]
        bank_rr[0] = (b + 1) % 8
        return b

    def next_bank_pair():
        if bank_rr[0] % 2:
            bank_rr[0] = (bank_rr[0] + 1) % 8
        b = bank_rr[0]
        bank_rr[0] = (b + 2) % 8
        return b

    c_cur = [CONST_OFF]

    def calloc(nbytes):
        o = c_cur[0]
        c_cur[0] += (nbytes + 31) // 32 * 32
        assert c_cur[0] <= R_OFF
        return o

    IDB = calloc(256)
    IDF = calloc(512)
    ONESB = calloc(256)
    SMALL = calloc(896)
    BGH = calloc(64)
    ESINK = calloc(32)
    FLAG = calloc(16)
    SSQ = calloc(128)
    RSTD = calloc(128)
    SAVE = calloc(44 * 2 * 4)
    W2NM = calloc(44 * 4)
    W0NM = calloc(44 * 4)
    W2N1 = calloc(44 * 4)
    W0N1 = calloc(44 * 4)
    FL_CV = calloc(44 * 4 * 2)
    FL_H = calloc(64)
    X1C = calloc(32)
    RDS = calloc(128)
    EPSC = calloc(32)
    ZEROC = calloc(32)
    GFINB = calloc(4096)
    RT = SSQ + 64
    SM_GA, SM_GF, SM_GFIN, SM_BG, SM_CW, SM_CB, SM_SINK = 0, 8, 16, 24, 40, 172, 216
    R_const = Res("const")

    def sm(off, n=1, p0=0, np_=128):
        return v32(SMALL + 4 * off, [[1, n]], p0, np_)

    B_ACC = R_OFF
    B_QT = B_ACC + 16384
    B_KT = B_QT + 4096
    B_VA = B_KT + 6144
    B_EA = B_VA + 12288
    B_EB = B_EA + 12288
    B_ES = B_EB + 12288
    B_PT = B_ES + 4096
    B_RD = B_PT + 2048
    B_RS = B_RD + 4096
    B_WQ = B_RS + 4096
    B_ACC2 = B_WQ + 12288
    B_END = B_ACC2 + 16384
    assert B_END <= TOTAL
    A_XB = R_OFF
    A_XS = A_XB + 49152
    A_JUNK = A_XS + 4096
    assert A_JUNK + 2048 <= TOTAL
    C_X1T = R_OFF
    C_XN1 = C_X1T + 16512
    C_HT = C_XN1 + 8192
    C_UCG = C_HT + 22528
    C_UCV = C_UCG + 2064
    C_UCG2 = C_UCV + 2064
    C_UCV2 = C_UCG2 + 2064
    C_TMP = C_UCV2 + 2064
    C_SQ = C_TMP + 12288
    C_RB = C_SQ + 2048
    C_XR = C_RB + 4096
    C_OUTS = C_XR + 8192
    C_MG = C_OUTS + 4096
    C_WM = C_MG + 8192
    C_WUP = C_WM + 11264
    C_WDN = C_WUP + 8192
    C_END = C_WDN + 11264
    assert C_END <= TOTAL, C_END - TOTAL

    ch_c = S.chan()
    S.op("sp", lambda e: e.dma_start(out=v32(SMALL, [[1, 224]]), in_=small_d), writes=[R_const], chan=ch_c)
    S.op("sp", lambda e: e.dma_start(out=v32(GFINB, [[1, 1024]]), in_=gfinb_d), writes=[R_const], chan=ch_c)
    S.op("sp", lambda e: e.dma_start(out=v32(IDF, [[1, 128]]), in_=ident_d), writes=[R_const], chan=ch_c)
    S.op("sp", lambda e: e.dma_start(out=v32(FLAG, [[1, 4]]), in_=flagv_d), writes=[R_const], chan=ch_c)
    S.op("dve", lambda e: e.tensor_copy(out=v16(IDB, [[1, 128]]), in_=v32(IDF, [[1, 128]])),
         reads=[R_const], writes=[R_const])
    S.op("pool", lambda e: e.memset(v16(ONESB, [[1, 128]]), 1.0), writes=[R_const])
    S.op("pool", lambda e: e.memset(v32(SAVE, [[1, 88]]), 0.0), writes=[R_const])
    S.op("pool", lambda e: e.memset(v32(X1C, [[1, 8]]), 0.0), writes=[R_const])
    S.op("pool", lambda e: e.memset(v32(EPSC, [[1, 8]]), EPS), writes=[R_const])
    S.op("pool", lambda e: e.memset(v32(ZEROC, [[1, 8]]), 0.0), writes=[R_const])
    S.op("dve", lambda e: e.tensor_scalar(out=v32(BGH, [[1, 16]]), in0=sm(SM_BG, 16), scalar1=0.5, scalar2=None,
                                          op0=ALU.mult), reads=[R_const], writes=[R_const])
    S.op("act", lambda e: e.activation(out=v32(ESINK, [[1, 8]]), in_=sm(SM_SINK, 8), func=AF.Exp),
         reads=[R_const], writes=[R_const])
    for dst, src, scal in ((W2NM, SM_CW + 88, None), (W0NM, SM_CW, None), (W2N1, SM_CW + 88, -1.0),
                           (W0N1, SM_CW, -1.0)):
        if scal is None:
            S.op("dve", lambda e, dst=dst, src=src: e.tensor_scalar(
                out=v32(dst, [[1, 44]]), in0=sm(src, 44), scalar1=v32(FLAG + 8, [[1, 1]]), scalar2=None,
                op0=ALU.mult), reads=[R_const], writes=[R_const])
        else:
            S.op("dve", lambda e, dst=dst, src=src, scal=scal: e.tensor_scalar(
                out=v32(dst, [[1, 44]]), in0=sm(src, 44), scalar1=scal, scalar2=None, op0=ALU.mult),
                reads=[R_const], writes=[R_const])

    R_scr = Res("scratch_R")
    ch_e = S.chan()
    OHS = R_OFF + 90112
    RELS = OHS + 8192
    ONES32 = OHS + 8192 + 128
    ERT = OHS + 16384
    S.op("sp", lambda e: e.dma_start(out=v32(OHS, [[1, 2048]], 0, 32), in_=oh_d), writes=[R_scr], chan=ch_e)
    S.op("sp", lambda e: e.dma_start(out=v32(RELS, [[1, 20]], 0, 32), in_=relb_d), writes=[R_scr], chan=ch_e)
    S.op("pool", lambda e: e.memset(v32(ONES32, [[1, 20]], 0, 32), 1.0), writes=[R_scr])
    R_er = Res("er")
    ch_er = S.chan()
    R_ert = [Res("ert0"), Res("ert1")]
    for t in range(4):
        bv = next_bank()
        bm = next_bank()
        mm_chain(S, cx, [(pv32(bv * 512, [[1, 512]], 0, 20), v32(RELS, [[1, 20]], 0, 32),
                          v32(OHS + t * 2048, [[1, 512]], 0, 32))], [R_scr], [banks[bv]])
        mm_chain(S, cx, [(pv32(bm * 512, [[1, 512]], 0, 20), v32(ONES32, [[1, 20]], 0, 32),
                          v32(OHS + t * 2048, [[1, 512]], 0, 32))], [R_scr], [banks[bm]])
        tmp = ERT + (t % 2) * 2048
        R_t = R_ert[t % 2]
        S.op("act", lambda e, bv=bv, tmp=tmp: e.activation(out=v32(tmp, [[1, 512]], 0, 20),
                                                            in_=pv32(bv * 512, [[1, 512]], 0, 20), func=AF.Exp),
             reads=[banks[bv]], writes=[R_t])
        S.op("dve", lambda e, bm=bm, tmp=tmp: e.tensor_tensor(out=v32(tmp, [[1, 512]], 0, 20),
                                                               in0=v32(tmp, [[1, 512]], 0, 20),
                                                               in1=pv32(bm * 512, [[1, 512]], 0, 20), op=ALU.mult),
             reads=[banks[bm], R_t], writes=[R_t])
        S.op("sp", lambda e, t=t, tmp=tmp: e.dma_start(out=er_d[t * 20:(t + 1) * 20, :],
                                                       in_=v32(tmp, [[1, 512]], 0, 20)),
             reads=[R_t], writes=[R_er], chan=ch_er)

    CV_OFF = R_OFF + 110592
    cv_names = [("wm", wm_d, 8 * 2816), ("wo", wo_d, 8192), ("wup", wup_d, NCP * 2048), ("wdn", wdn_d, 8 * 2816),
                ("wqkv", wqkv_d, 6 * 3072), ("wbkv", wbkv_d, 2048), ("wbq", wbq_d, 4096)]
    WB = {}
    R_cv = {}
    R_cvs = [Res("cvs0"), Res("cvs1")]
    ch_cvi = [S.chan(), S.chan()]
    ch_cvo = [S.chan(), S.chan()]
    cv_chunks = []
    for (nm, src_ap, ncols) in cv_names:
        WB[nm] = nc.dram_tensor(nm + "_b", [128, ncols], BF16, kind="Internal").ap()
        R_cv[nm] = Res("cv_" + nm)
        c = 0
        while c < ncols:
            w_ = min(4096, ncols - c)
            cv_chunks.append((nm, src_ap, ncols, c, w_))
            c += w_

    def conv_gen():
        def emit_in(k):
            nm, src_ap, ncols, c, w_ = cv_chunks[k]
            sl = k % 2
            S.op("pool", lambda e: e.dma_start(out=v16(CV_OFF + sl * 8192, [[1, w_]]),
                                               in_=dv(src_ap, c, [[ncols, 128], [1, w_]])),
                 writes=[R_cvs[sl]], chan=ch_cvi[sl])

        def emit_out(k):
            nm, src_ap, ncols, c, w_ = cv_chunks[k]
            sl = k % 2
            S.op("pool", lambda e: e.dma_start(out=dv(WB[nm], c, [[ncols, 128], [1, w_]]),
                                               in_=v16(CV_OFF + sl * 8192, [[1, w_]])),
                 reads=[R_cvs[sl]], writes=[R_cv[nm]], chan=ch_cvo[sl])

        for k in range(len(cv_chunks)):
            emit_in(k)
            if k >= 1:
                emit_out(k - 1)
            yield
        emit_out(len(cv_chunks) - 1)
        yield

    cvg = conv_gen()

    def conv_step(n):
        for _ in range(n):
            next(cvg, None)


    def chk(tag):
        if STOP[0] == tag:
            S.stopped = True

    units = [(0, False, True), (2048, True, False), (4096, False, False)]
    R_xn = Res("xn")
    R_oab = Res("oab")
    XNW = 3072

    def xn_ap(k, col, n, step=1, p0=0, np_=128):
        return v16(XN_OFF + 2 * (k * XNW + col), [[step, n]], p0, np_)

    R_save = Res("save")
    R_x1t = [Res("x1t%d" % i) for i in range(8)]
    R_x1c = Res("x1c")
    R_esave = Res("esave")
    R_dscr = Res("dscr")
    R_rscr = Res("rscr")
    R_rds = Res("rds")
    ch_out = S.chan()
    R_outsA = Res("outsA")
    R_outsB = Res("outsB")

    chk('prologue')
    for ui, (own0, left, right) in enumerate(units):
        ext0 = own0 - (1024 if left else 0)
        next_ = 2048 + (1024 if (left or right) else 0)
        ownc = own0 - ext0
        nblk = next_ // 128

        if ui > 0:
            S.barrier()
        R_wq = [Res("wq0"), Res("wq1")]
        ch_wq = [S.chan(), S.chan()]
        wq_i = [0]
        def load_wslab(src_ap, ncols, R_src):
            s = wq_i[0] % 2
            wq_i[0] += 1
            S.op("pool", lambda e, s=s: e.dma_start(out=v16(B_WQ + s * 6144, [[1, ncols]]), in_=src_ap),
                 reads=[R_src], writes=[R_wq[s]], chan=ch_wq[s])
            return s

        slab_specs = []
        R_nodep = Res("nodep")
        if ui == 0:
            for hp_ in range(2):
                for g_ in range(3):
                    slab_specs.append((dv(wqkv_d, (g_ * 2 + hp_) * 3072, [[6 * 3072, 128], [1, 3072]]), 3072, R_nodep))
            slab_specs.append((dv(wbkv_d, 0, [[2048, 128], [1, 2048]]), 2048, R_nodep))
            for ci_ in range(4):
                slab_specs.append((dv(wbq_d, ci_ * 1024, [[4096, 128], [1, 1024]]), 1024, R_nodep))
        else:
            for hp_ in range(2):
                for g_ in range(3):
                    slab_specs.append((dv(WB["wqkv"], (g_ * 2 + hp_) * 3072, [[6 * 3072, 128], [1, 3072]]), 3072,
                                       R_cv["wqkv"]))
            slab_specs.append((dv(WB["wbkv"], 0, [[2048, 128], [1, 2048]]), 2048, R_cv["wbkv"]))
            for ci_ in range(4):
                slab_specs.append((dv(WB["wbq"], ci_ * 1024, [[4096, 128], [1, 1024]]), 1024, R_cv["wbq"]))
        slab_state = {"i": 0, "slot": load_wslab(*slab_specs[0])}

        R_xb = [Res("xb%d" % i) for i in range(12)]
        ch_xb = [S.chan() for _ in range(12)]
        R_ssqs = [Res("ssq0"), Res("ssq1")]
        R_xs = [Res("xs%d" % i) for i in range(2)]
        R_junk = Res("junk")
        for bi_, b0 in enumerate(range(0, nblk, 6)):
            nb = min(6, nblk - b0)
            hf = bi_ % 2
            R_ssq = R_ssqs[hf]
            SSQ_ = SSQ + 32 * hf
            RSTD_ = RSTD + 32 * hf
            S.op("act", lambda e, SSQ_=SSQ_: e.activation(out=v32(SSQ_, [[1, 8]]), in_=v32(ZEROC, [[1, 8]]), func=AF.Copy),
                 reads=[R_const], writes=[R_ssq])
            for j in range(nb):
                b = b0 + j
                sl = hf * 6 + j
                S.op(("sp", "act")[j % 2], lambda e, sl=sl, r0=ext0 + b * 128: e.dma_start(out=v32(A_XB + sl * 4096, [[1, 1024]]),
                                                         in_=xin[r0: r0 + 128, :]),
                     writes=[R_xb[sl]], chan=ch_xb[sl])
                S.op("act", lambda e, sl=sl, j=j, SSQ_=SSQ_: e.activation(out=v16(A_JUNK, [[1, 1024]]),
                                                        in_=v32(A_XB + sl * 4096, [[1, 1024]]), func=AF.Square,
                                                        accum_out=v32(SSQ_ + 4 * j, [[1, 1]])),
                     reads=[R_xb[sl]], writes=[R_junk, R_ssq])
            S.op("dve", lambda e, nb=nb, SSQ_=SSQ_, RSTD_=RSTD_: e.tensor_scalar(
                out=v32(RSTD_, [[1, nb]]), in0=v32(SSQ_, [[1, nb]]),
                scalar1=1.0 / D, scalar2=EPS, op0=ALU.mult, op1=ALU.add),
                 reads=[R_ssq], writes=[R_ssq])
            S.op("act", lambda e, nb=nb, RSTD_=RSTD_: e.activation(out=v32(RSTD_, [[1, nb]]), in_=v32(RSTD_, [[1, nb]]),
                                                      func=AF.Sqrt), reads=[R_ssq], writes=[R_ssq])
            S.op("dve", lambda e, nb=nb, RSTD_=RSTD_: e.reciprocal(out=v32(RSTD_, [[1, nb]]), in_=v32(RSTD_, [[1, nb]])),
                 reads=[R_ssq], writes=[R_ssq])
            pend_ev = None
            for j in range(nb):
                b = b0 + j
                s = b % 2
                sl = hf * 6 + j
                S.op("dve", lambda e, j=j, s=s, sl=sl, RSTD_=RSTD_: e.tensor_scalar(out=v16(A_XS + s * 2048, [[1, 1024]]),
                                                                in0=v32(A_XB + sl * 4096, [[1, 1024]]),
                                                                scalar1=v32(RSTD_ + 4 * j, [[1, 1]]), scalar2=None,
                                                                op0=ALU.mult),
                     reads=[R_xb[sl], R_ssq], writes=[R_xs[s]])
                bk = next_bank()
                mm_chain(S, cx, [(pv16(bk * 1024 + c * 128, [[1, 128]]), v16(A_XS + s * 2048 + c * 256, [[1, 128]]),
                                  v16(IDB, [[1, 128]])) for c in range(8)], [R_xs[s], R_const], [banks[bk]],
                         transpose=True)
                if pend_ev is not None:
                    pend_ev()
                pend_ev = (lambda b=b, bk=bk: S.op("dve", lambda e: e.tensor_tensor(
                    out=v16(XN_OFF + 2 * (b * 128), [[XNW, 8], [1, 128]]),
                    in0=pv16(bk * 1024, [[128, 8], [1, 128]]),
                    in1=v32(SMALL + 4 * SM_GA, [[1, 8], [0, 128]]), op=ALU.mult),
                    reads=[banks[bk], R_const], writes=[R_xn]))
            if pend_ev is not None:
                pend_ev()

        chk('ph1_%d' % ui)
        S.barrier()
        R_E = Res("E")
        ch_E = S.chan()
        R_es = [Res("es0"), Res("es1")]
        R_pt = [Res("pt0"), Res("pt1")]
        if ui == 0:
            R_stg = Res("stage")
            STG = B_ACC
            for g in range(3):
                dst = STG + g * 1024 * 4
                src = bass.AP(er_d.tensor, (g * 20 + 4 * g) * 512 - 64 + PAD - 127,
                              [[1, 128], [512, 4], [128, 2], [1, 128]])
                S.op(("sp", "act")[g % 2], lambda e, dst=dst, src=src: e.dma_start(
                    out=v32(dst, [[256, 4], [128, 2], [1, 128]]), in_=src),
                     reads=[R_er], writes=[R_stg], chan=ch_E)
            src = bass.AP(er_d.tensor, (3 * 20 + 12) * 512 - 128 + PAD - 127, [[1, 128], [512, 8], [128, 3], [1, 128]])
            S.op("act", lambda e, src=src: e.dma_start(out=v32(STG + 12288, [[384, 8], [128, 3], [1, 128]]), in_=src),
                 reads=[R_er], writes=[R_stg], chan=ch_E)
            S.op("dve", lambda e: e.tensor_copy(out=v32(B_EA, [[128, 48], [1, 128]]),
                                                in_=v32(STG + 127 * 4, [[128, 48], [-1, 128]])),
                 reads=[R_stg], writes=[R_E])
            ch_Es = S.chan()
            S.op("sp", lambda e: e.dma_start(out=esave_d, in_=v32(B_EA, [[1, 6144]])), reads=[R_E],
                 writes=[R_esave], chan=ch_Es)
            S.barrier()
        else:
            S.op("sp", lambda e: e.dma_start(out=v32(B_EA, [[1, 6144]]), in_=esave_d), reads=[R_esave],
                 writes=[R_E], chan=ch_E)
        chk('etab_%d' % ui)
        R_accs = [Res("acc0"), Res("acc1")]
        ACCB = [B_ACC, B_ACC2]
        acc_cur = {"i": 0}
        R_qt = Res("qt")
        R_kt = Res("kt")
        R_va = Res("va")
        R_rd = Res("rd")
        R_rs = Res("rs")
        ch_rs = S.chan()

        def proj_fm(s, wcol, wk, evac, col0, n):
            bk = next_bank()
            mm_chain(S, cx, [(pv32(bk * 512, [[1, n]]), v16(B_WQ + s * 6144 + 2 * (k * wk + wcol), [[1, 128]]),
                              xn_ap(k, col0, n)) for k in range(8)], [R_wq[s], R_xn], [banks[bk]])
            evac(bk)

        def attn(rq_list, kblocks_all, klo, nk, n, d, nside, e_offs, hh_list, hooks=None):
            SW = nside * 128
            groups = [hh_list] if nside == 2 else [[hh] for hh in hh_list]
            items = []
            for (r, qblocks) in rq_list:
                for (qs, v0, v1, kbs) in qblocks:
                    for gidx, grp in enumerate(groups):
                        items.append((r, qs, v0, v1, kbs, gidx, grp))

            def stage_s(idx):
                r, qs, v0, v1, kbs, gidx, grp = items[idx]
                w = v1 - v0
                c0 = v0 - qs
                bs = next_bank()
                es = idx % 2
                steps = []
                for gi, hh in enumerate(grp):
                    for (side, gb) in kbs:
                        st, ln = kblocks_all[gb]
                        steps.append((pv32(bs * 512 + gi * SW + side * 128 + c0, [[1, w]], 0, ln),
                                      v16(B_KT + 2 * (r * nk + (st - klo)), [[1, ln]], 64 * hh, 64),
                                      v16(B_QT + 2 * (r * n + v0), [[1, w]], 64 * hh, 64)))
                nst = len(steps)
                nper = len(kbs)
                prev_ev = None
                for i, (o, a, b_) in enumerate(steps):
                    boundary_next = (nside == 2 and (i + 1) % nper == 0 and i != nst - 1)
                    boundary_here = (nside == 2 and i % nper == 0 and i != 0)
                    ev_ = S.op("pe", lambda e, o=o, a=a, b_=b_: e.matmul(o, a, b_, start=True, stop=True),
                               reads=[R_kt, R_qt], writes=[banks[bs]] if i in (0, nst - 1) else (),
                               extra=[prev_ev] if (boundary_here and prev_ev) else (),
                               signal=(i == nst - 1) or boundary_next, self_wait=boundary_here)
                    if boundary_next:
                        prev_ev = ev_
                wd = len(grp) * SW
                S.op("act", lambda e, bs=bs, es=es, wd=wd: e.activation(
                    out=v32(B_ES + es * 2048, [[1, wd]]), in_=pv32(bs * 512, [[1, wd]]), func=AF.Exp,
                    scale=0.125), reads=[banks[bs]], writes=[R_es[es]])
                eoff = e_offs[gidx]
                S.op("dve", lambda e, es=es, wd=wd, eoff=eoff: e.tensor_tensor(
                    out=v16(B_PT + es * 1024, [[1, wd]]), in0=v32(B_ES + es * 2048, [[1, wd]]),
                    in1=v32(eoff, [[1, wd]]), op=ALU.mult), reads=[R_es[es], R_E], writes=[R_pt[es]])

            def stage_pv(idx):
                r, qs, v0, v1, kbs, gidx, grp = items[idx]
                w = v1 - v0
                c0 = v0 - qs
                es = idx % 2
                bo = next_bank()
                nh = len(grp)
                h0 = grp[0]
                cnt = 0
                tot = nh * len(kbs)
                for gi, hh in enumerate(grp):
                    for j, (side, gb) in enumerate(kbs):
                        st, ln = kblocks_all[gb]
                        cnt += 1
                        o = pv32(bo * 512 + hh * 128 + c0, [[1, w]])
                        a = v16(B_VA + 2 * (gb * 192 + hh * 64), [[1, 128]], 0, ln)
                        b_ = v16(B_PT + es * 1024 + 2 * (gi * SW + side * 128 + c0), [[1, w]], 0, ln)
                        S.op("pe", lambda e, o=o, a=a, b_=b_, j=j, nk_=len(kbs): e.matmul(
                            o, a, b_, start=(j == 0), stop=(j == nk_ - 1)),
                            reads=[R_va, R_pt[es]], writes=[banks[bo]] if cnt in (1, tot) else (),
                            signal=(cnt == tot))
                ab = ACCB[acc_cur["i"]]
                R_acc = R_accs[acc_cur["i"]]
                S.op("dve", lambda e, bo=bo, nh=nh, h0=h0, v0=v0, w=w, c0=c0, r=r, ab=ab: e.tensor_tensor(
                    out=v32(ab + 4 * (h0 * 2048 + v0 * d + r), [[2048, nh], [d, w]]),
                    in0=pv32(bo * 512 + h0 * 128 + c0, [[128, nh], [1, w]]),
                    in1=v32(ab + 4 * (h0 * 2048 + v0 * d + r), [[2048, nh], [d, w]]), op=ALU.add),
                    reads=[banks[bo], R_acc], writes=[R_acc])

            for idx in range(len(items) + 1):
                if idx < len(items):
                    stage_s(idx)
                if hooks and idx in hooks:
                    hooks[idx]()
                if idx >= 1:
                    stage_pv(idx - 1)

        def set_va_ones(nb_, halo_list):
            S.op("pool", lambda e: e.memset(v16(B_VA + 2 * 64, [[192, nb_], [1, 64]]), 1.0), writes=[R_va])
            for gb, hm in halo_list:
                S.op("pool", lambda e, gb=gb, hm=hm: e.tensor_copy(
                    out=v16(B_VA + 2 * (gb * 192 + 64), [[1, 64]]),
                    in_=v32(FLAG + 4 * hm, [[0, 64]])), reads=[R_const, R_va], writes=[R_va])

        def finalize_a(ai, sink_cols=None):
            ab = ACCB[ai]
            R_acc = R_accs[ai]
            S.op("sp", lambda e: e.dma_start(out=dscr_d[0:1, :], in_=v32(ab, [[1, 2048]], 64, 1)),
                 reads=[R_acc], writes=[R_dscr], chan=ch_rs)
            S.op("sp", lambda e: e.dma_start(out=dscr_d[1:2, :], in_=v32(ab + 8192, [[1, 2048]], 0, 1)),
                 reads=[R_acc], writes=[R_dscr], chan=ch_rs)
            S.op("sp", lambda e: e.dma_start(out=v32(RDS, [[1, 32]], 0, 64),
                                             in_=bass.AP(dscr_d.tensor, 0, [[32, 64], [1, 32]])),
                 reads=[R_dscr], writes=[R_rds], chan=ch_rs)
            S.op("sp", lambda e: e.dma_start(out=v32(RDS, [[1, 32]], 64, 64),
                                             in_=bass.AP(dscr_d.tensor, 2048, [[32, 64], [1, 32]])),
                 reads=[R_dscr], writes=[R_rds], chan=ch_rs)

        def finalize_b(ai, dst_chunk, sink_cols=None):
            ab = ACCB[ai]
            R_acc = R_accs[ai]
            if sink_cols is not None:
                for half, col in enumerate(sink_cols):
                    S.op("dve", lambda e, half=half, col=col: e.tensor_scalar(
                        out=v32(RDS, [[1, 32]], 64 * half, 64), in0=v32(RDS, [[1, 32]], 64 * half, 64),
                        scalar1=v32(ESINK + 4 * col, [[1, 1]], 64 * half, 64), scalar2=None, op0=ALU.add),
                        reads=[R_rds, R_const], writes=[R_rds])
            S.op("dve", lambda e: e.reciprocal(out=v32(RDS, [[1, 32]]), in_=v32(RDS, [[1, 32]])),
                 reads=[R_rds], writes=[R_rds])
            S.op("sp", lambda e: e.dma_start(out=bass.AP(rscr_d.tensor, 0, [[32, 128], [1, 32]]),
                                             in_=v32(RDS, [[1, 32]])),
                 reads=[R_rds], writes=[R_rscr], chan=ch_rs)
            S.op("sp", lambda e: e.dma_start(out=v32(B_RD, [[1, 2048]], 0, 64),
                                             in_=bass.AP(rscr_d.tensor, 0, [[0, 64], [1, 2048]])),
                 reads=[R_rscr], writes=[R_rs], chan=ch_rs)
            S.op("sp", lambda e: e.dma_start(out=v32(B_RD, [[1, 2048]], 64, 64),
                                             in_=bass.AP(rscr_d.tensor, 2048, [[0, 64], [1, 2048]])),
                 reads=[R_rscr], writes=[R_rs], chan=ch_rs)

        def finalize_c(ai, dst_chunk):
            ab = ACCB[ai]
            R_acc = R_accs[ai]
            S.op("dve", lambda e: e.tensor_tensor(
                out=v16(OAB_OFF + 2 * (dst_chunk * 2048), [[1, 2048]], 0, 64),
                in0=v32(ab, [[1, 2048]], 0, 64), in1=v32(B_RD, [[1, 2048]], 0, 64), op=ALU.mult),
                reads=[R_acc, R_rs], writes=[R_oab])
            S.op("dve", lambda e: e.tensor_tensor(
                out=v16(OAB_OFF + 2 * (dst_chunk * 2048), [[1, 2048]], 64, 64),
                in0=v32(ab + 8192, [[1, 2048]], 64, 64), in1=v32(B_RD, [[1, 2048]], 64, 64), op=ALU.mult),
                reads=[R_acc, R_rs], writes=[R_oab])

        pend = {"f": None}

        def fin_start(ai, dst_chunk, sink_cols=None):
            finalize_a(ai, sink_cols)
            pend["f"] = (ai, dst_chunk, sink_cols, 0)

        def fin_step():
            f = pend["f"]
            if f is None:
                return
            ai, dst_chunk, sink_cols, stage = f
            if stage == 0:
                finalize_b(ai, dst_chunk, sink_cols)
                pend["f"] = (ai, dst_chunk, sink_cols, 1)
            else:
                finalize_c(ai, dst_chunk)
                pend["f"] = None

        def fin_flush():
            while pend["f"] is not None:
                fin_step()

        def project_kv(s, wk, kcol, vcol, d, n, klo, khi, nk, kblocks):
            nbr = len(kblocks)
            ranges = []
            if left:
                ranges.append((klo * d, 0))
            ranges += [(o, o + 512) for o in range(0, 2048, 512)]
            if right:
                ranges.append((2048, 2048 + (khi - n) * d))
            tiles = []
            for (a, b_) in ranges:
                o = a
                while o < b_:
                    nn = min(512, b_ - o)
                    tiles.append((o, nn))
                    o += nn
            for (o, nn) in tiles:
                def evac(bk, o=o, nn=nn):
                    S.op("act", lambda e: e.activation(
                        out=v16(B_KT + 2 * (o // d - klo), [[1, nn // d], [nk, d]]),
                        in_=pv32(bk * 512, [[d, nn // d], [1, d]]), func=AF.Copy),
                        reads=[banks[bk]], writes=[R_kt])
                proj_fm(s, kcol, wk, evac, ownc + o, nn)
            for r in range(d):
                for bi, (st, ln) in enumerate(kblocks):
                    gb = r * nbr + bi
                    bk = next_bank()
                    col = ownc + st * d + r
                    mm_chain(S, cx, [(pv32(bk * 512, [[1, 128]], 0, ln), xn_ap(k, col, ln, d),
                                      v16(B_WQ + s * 6144 + 2 * (k * wk + vcol), [[1, 128]])) for k in range(8)],
                             [R_wq[s], R_xn], [banks[bk]])
                    hm = None
                    if st < 0:
                        hm = 1 if ln == 128 and st == -64 else 0
                    elif st >= n:
                        hm = 0
                    if hm is None:
                        S.op("act", lambda e, gb=gb, bk=bk, ln=ln: e.activation(
                            out=v16(B_VA + 2 * (gb * 192), [[128, 2], [1, 64]], 0, ln),
                            in_=pv32(bk * 512, [[64, 2], [1, 64]], 0, ln), func=AF.Copy),
                            reads=[banks[bk], R_va], writes=[R_va])
                    else:
                        S.op("dve", lambda e, gb=gb, bk=bk, ln=ln, hm=hm: e.tensor_scalar(
                            out=v16(B_VA + 2 * (gb * 192), [[128, 2], [1, 64]], 0, ln),
                            in0=pv32(bk * 512, [[64, 2], [1, 64]], 0, ln),
                            scalar1=v32(FLAG + 4 * hm, [[1, 1]], 0, ln), scalar2=None, op0=ALU.mult),
                            reads=[banks[bk], R_const, R_va], writes=[R_va])

        def take_slab():
            conv_step(3)
            cur = slab_state["slot"]
            slab_state["i"] += 1
            if slab_state["i"] < len(slab_specs):
                slab_state["slot"] = load_wslab(*slab_specs[slab_state["i"]])
            return cur

        for hp in range(2):
            acc_cur["i"] = hp % 2
            S.op("pool", lambda e, ab=ACCB[hp % 2]: e.memset(v32(ab, [[1, 4096]]), 0.0), writes=[R_accs[hp % 2]])
            for g in range(3):
                d = DILS[g]
                n, klo, khi, nk, kblocks, qblocks = geom(d, left, right)
                nbr = len(kblocks)
                ph = g * 2 + hp
                s = take_slab()
                halo_list = []
                for r in range(d):
                    for bi, (st, ln) in enumerate(kblocks):
                        if st < 0:
                            halo_list.append((r * nbr + bi, 1))
                        elif st >= n:
                            halo_list.append((r * nbr + bi, 0))
                allblocks = [(st, ln) for r in range(d) for (st, ln) in kblocks]
                set_va_ones(len(allblocks), halo_list)
                chk('va1_%d_%d_%d' % (ui, hp, g))
                for o in range(0, 2048, 512):
                    def evq(bk, o=o, d=d, n=n):
                        S.op("act", lambda e: e.activation(
                            out=v16(B_QT + 2 * (o // d), [[1, 512 // d], [n, d]]),
                            in_=pv32(bk * 512, [[d, 512 // d], [1, d]]), func=AF.Copy),
                            reads=[banks[bk]], writes=[R_qt])
                    proj_fm(s, 0, 384, evq, ownc + o, 512)
                chk('qproj_%d_%d_%d' % (ui, hp, g))
                fin_step()
                project_kv(s, 384, 128, 256, d, n, klo, khi, nk, kblocks)
                fin_step()
                chk('kvproj_%d_%d_%d' % (ui, hp, g))
                rq = []
                for r in range(d):
                    rq.append((r, [(qs, v0, v1, [(side, r * nbr + bi) for (side, bi) in kbs])
                                   for (qs, v0, v1, kbs) in qblocks]))
                attn(rq, allblocks, klo, nk, n, d, 2, [B_EA + ph * 2048], [0, 1])
                chk('attng_%d_%d_%d' % (ui, hp, g))
            fin_flush()
            fin_start(hp % 2, hp)
            chk('attnA%d_%d' % (hp, ui))

        n, klo, khi, nk, kblocks, qblocks = geom_b(left, right)
        s = take_slab()
        set_va_ones(len(kblocks), [(bi, 0) for bi, (st, ln) in enumerate(kblocks) if st < 0 or st >= n])
        fin_step()
        project_kv(s, 256, 0, 128, 1, n, klo, khi, nk, kblocks)
        fin_step()
        for ci in range(4):
            acc_cur["i"] = ci % 2
            S.op("pool", lambda e, ab=ACCB[ci % 2]: e.memset(v32(ab, [[1, 4096]]), 0.0), writes=[R_accs[ci % 2]])
            s = take_slab()
            for o in range(0, 2048, 512):
                def evq(bk, o=o):
                    S.op("act", lambda e: e.activation(out=v16(B_QT + 2 * o, [[1, 512]]),
                                                       in_=pv32(bk * 512, [[1, 512]]), func=AF.Copy),
                         reads=[banks[bk]], writes=[R_qt])
                proj_fm(s, 0, 128, evq, ownc + o, 512)
            fin_step()
            attn([(0, qblocks)], kblocks, klo, nk, n, 1, 3,
                 [B_EB + ci * 1536, B_EB + (4 + ci) * 1536], [0, 1], hooks={8: fin_step})
            fin_flush()
            fin_start(ci % 2, 2 + ci, (ci, 4 + ci))
        fin_flush()
        conv_step(40)

        chk('attn_%d' % ui)
        S.barrier()
        R_xr = [Res("xr0"), Res("xr1")]
        ch_xr = [S.chan(), S.chan()]
        R_wm = [Res("wm0"), Res("wm1")]
        ch_wm = [S.chan(), S.chan()]
        R_wup = [Res("wup0"), Res("wup1")]
        ch_wup = [S.chan(), S.chan()]
        R_wdn = [Res("wdn0"), Res("wdn1")]
        ch_wdn = [S.chan(), S.chan()]
        R_mg = Res("mg")
        R_xn1 = [Res("xn1_%d" % i) for i in range(8)]
        R_ht = [Res("ht%d" % i) for i in range(NCP)]
        R_ucg = Res("ucg")
        R_ucv = Res("ucv")
        R_ucg2 = Res("ucg2")
        R_ucv2 = Res("ucv2")
        R_tmp = [Res("tmp%d" % i) for i in range(6)]
        R_sq = [Res("sq0"), Res("sq1")]
        R_rb = [Res("rb0"), Res("rb1")]
        wmi = [0]
        outs_i = [0]
        wupi = [0]
        wdni = [0]
        xri = [0]

        class WStream:
            def __init__(self, items, slots):
                self.items = items
                self.slots = slots
                self.i = 0
                self._load(0)

            def _load(self, i):
                if i >= len(self.items):
                    return
                src, ncols, R_src = self.items[i]
                boff, R_w, ch_w = self.slots[i % len(self.slots)]
                S.op("pool", lambda e: e.dma_start(out=v16(boff, [[1, ncols]]), in_=src), reads=[R_src],
                     writes=[R_w], chan=ch_w)

            def take(self):
                i = self.i
                self.i += 1
                self._load(i + 1)
                boff, R_w, ch_w = self.slots[i % len(self.slots)]
                return boff, R_w

        wm_items = []
        for tt_ in range(4):
            for m_ in range(8):
                wm_items.append((dv(WB["wm"], m_ * 2816, [[8 * 2816, 128], [1, 2816]]), 2816, R_cv["wm"]))
            for m_ in range(0, 8, 2):
                wm_items.append((dv(WB["wo"], m_ * 1024, [[8192, 128], [1, 2048]]), 2048, R_cv["wo"]))
        wdn_items = [(dv(WB["wdn"], m_ * 2816, [[8 * 2816, 128], [1, 2816]]), 2816, R_cv["wdn"])
                     for _ in range(5 if ui == NUNIT - 1 else 4) for m_ in range(8)]
        wm_stream = WStream(wm_items, [(C_WM, R_wm[0], ch_wm[0]), (C_WM + 5632, R_wm[1], ch_wm[1])])
        wdn_stream = WStream(wdn_items, [(C_WDN, R_wdn[0], ch_wdn[0]), (C_WDN + 5632, R_wdn[1], ch_wdn[1])])

        def T(i):
            return C_TMP + i * 2048

        MT = [(C_OUTS, [R_outsA]), (C_OUTS + 2048, [R_outsB]), (C_SQ, [R_sq[0], R_sq[1]]), (C_SQ + 2048, [R_rb[0]])]

        def x1t(m, j0, nn, p0=0, np_=128):
            return v32(C_X1T + 4 * (m * 516 + j0), [[1, nn]], p0, np_)

        def rms_bc(j0, nn, rbi):
            bk = next_bank()
            for m in range(8):
                sq = m % 2
                if m % 2 == 0:
                    S.op("act", lambda e, m=m, sq=sq: e.activation(out=v16(C_SQ + sq * 1024, [[1, nn]]),
                                                                   in_=x1t(m, j0, nn), func=AF.Square),
                         reads=[R_x1t[m]], writes=[R_sq[sq]])
                else:
                    S.op("dve", lambda e, m=m, sq=sq: e.tensor_tensor(out=v16(C_SQ + sq * 1024, [[1, nn]]),
                                                                      in0=x1t(m, j0, nn), in1=x1t(m, j0, nn),
                                                                      op=ALU.mult),
                         reads=[R_x1t[m]], writes=[R_sq[sq]])
                S.op("pe", lambda e, m=m, sq=sq, bk=bk: e.matmul(pv32(bk * 512, [[1, nn]]), v16(ONESB, [[1, 128]]),
                                                              v16(C_SQ + sq * 1024, [[1, nn]]), start=(m == 0),
                                                              stop=(m == 7)),
                     reads=[R_sq[sq], R_const], writes=[banks[bk]], signal=True)
            rb = C_RB + rbi * 2048
            S.op("act", lambda e, bk=bk: e.activation(out=v32(rb, [[1, nn]]), in_=pv32(bk * 512, [[1, nn]]),
                                                      func=AF.Ln, bias=v32(EPSC, [[1, 1]]), scale=1.0 / D),
                 reads=[banks[bk], R_const], writes=[R_rb[rbi]])
            S.op("act", lambda e: e.activation(out=v32(rb, [[1, nn]]), in_=v32(rb, [[1, nn]]), func=AF.Exp,
                                               scale=-0.5),
                 reads=[R_rb[rbi]], writes=[R_rb[rbi]])

        def down_part(j0, nn, rhs_fn):
            for m in range(8):
                wb, R_w = wdn_stream.take()
                bk = next_bank()
                mm_chain(S, cx, [(pv32(bk * 512, [[1, nn]]), v16(wb + 2 * (k * 128), [[1, 128]]),
                                  rhs_fn(k)) for k in range(NCP)], [R_w], [banks[bk]],
                         step_reads=[[R_ht[k]] for k in range(NCP)])
                S.op("dve", lambda e, m=m, bk=bk: e.scalar_tensor_tensor(
                    out=x1t(m, j0, nn), in0=pv32(bk * 512, [[1, nn]]), scalar=0.5, in1=x1t(m, j0, nn),
                    op0=ALU.mult, op1=ALU.add), reads=[banks[bk], R_x1t[m]], writes=[R_x1t[m]])

        def final_part(j0, nn, tok0):
            R_rt = Res("rt")
            blocks = []
            st = 0
            while st < nn:
                ln = min(128, nn - st)
                blocks.append((st, ln))
                st += ln

            def do_t(bi):
                st, ln = blocks[bi]
                bp = next_bank_pair()
                mm_chain(S, cx, [(pv32(bp * 512 + m * 128, [[1, 128]], 0, ln), x1t(m, j0 + st, ln),
                                  v32(IDF, [[1, 128]])) for m in range(8)], R_x1t + [R_const],
                         [banks[bp], banks[bp + 1]], transpose=True)
                return bp

            def do_e(bi, bp):
                st, ln = blocks[bi]
                osl = outs_i[0] % 2
                outs_i[0] += 1
                ob = C_OUTS if osl == 0 else C_SQ
                ores = [R_outsA, R_outsB] if osl == 0 else [R_sq[0], R_sq[1], R_rb[0]]
                S.op("dve", lambda e: e.scalar_tensor_tensor(
                    out=v32(ob, [[1, 1024]], 0, ln), in0=pv32(bp * 512, [[1, 1024]], 0, ln),
                    scalar=v32(RT + 4 * bi, [[1, 1]], 0, ln), in1=v32(GFINB, [[1, 1024]], 0, ln),
                    op0=ALU.mult, op1=ALU.mult),
                    reads=[banks[bp], banks[bp + 1], R_rt, R_const], writes=ores)
                S.op("sp", lambda e, t0=tok0 + st: e.dma_start(out=yout[t0:t0 + ln, :],
                                                               in_=v32(ob, [[1, 1024]], 0, ln)),
                     reads=ores, writes=[], chan=ch_out)

            nb = len(blocks)
            pre = [do_t(bi) for bi in range(min(2, nb))]
            rms_bc(j0, nn, 1)
            bk = next_bank()
            for bi, (st, ln) in enumerate(blocks):
                mm_chain(S, cx, [(pv32(bk * 512 + bi * 128, [[1, 128]], 0, ln), v32(C_RB + 2048 + 4 * st, [[1, ln]]),
                                  v32(IDF, [[1, 128]]))], [R_rb[1], R_const], [banks[bk]], transpose=True)
                S.op("act", lambda e, bi=bi, ln=ln, bk=bk: e.activation(
                    out=v32(RT + 4 * bi, [[1, 1]], 0, ln), in_=pv32(bk * 512 + bi * 128, [[1, 1]], 0, ln),
                    func=AF.Copy), reads=[banks[bk], R_rt], writes=[R_rt])
            for bi in range(len(pre)):
                do_e(bi, pre[bi])
            for bi in range(len(pre), nb):
                do_e(bi, do_t(bi))

        def merge_gen(tt):
            ocol = ownc + tt * 512
            pend_ew = None
            for m in range(8):
                if pend_ew is not None:
                    pend_ew()
                wb, R_w = wm_stream.take()
                ba, bb, bga, bgb = next_bank(), next_bank(), next_bank(), next_bank()
                mm_chain(S, cx, [(pv32(ba * 512, [[1, 512]]), v16(wb + 2 * (k * 128), [[1, 128]]),
                                  v16(OAB_OFF + 2 * (k * 2048 + tt * 512), [[1, 512]])) for k in range(2)],
                         [R_w, R_oab], [banks[ba]])
                mm_chain(S, cx, [(pv32(bb * 512, [[1, 512]]), v16(wb + 2 * ((2 + k) * 128), [[1, 128]]),
                                  v16(OAB_OFF + 2 * ((2 + k) * 2048 + tt * 512), [[1, 512]])) for k in range(4)],
                         [R_w, R_oab], [banks[bb]])
                mm_chain(S, cx, [(pv32(bga * 512, [[1, 512]]), v16(wb + 2 * ((6 + k) * 128), [[1, 128]]),
                                  xn_ap(k, ocol, 512)) for k in range(8)], [R_w, R_xn], [banks[bga]])
                mm_chain(S, cx, [(pv32(bgb * 512, [[1, 512]]), v16(wb + 2 * ((14 + k) * 128), [[1, 128]]),
                                  xn_ap(k, ocol, 512)) for k in range(8)], [R_w, R_xn], [banks[bgb]])

                def ew(m=m, ba=ba, bb=bb, bga=bga, bgb=bgb):
                    (ta_b, ta_r), (tb_b, tb_r) = MT[2 * (m % 2)], MT[2 * (m % 2) + 1]
                    S.op("act", lambda e: e.activation(
                        out=v32(ta_b, [[1, 512]]), in_=pv32(bga * 512, [[1, 512]]), func=AF.Tanh,
                        bias=v32(BGH + 4 * m, [[1, 1]]), scale=0.5), reads=[banks[bga], R_const], writes=ta_r)
                    S.op("act", lambda e: e.activation(
                        out=v32(tb_b, [[1, 512]]), in_=pv32(bgb * 512, [[1, 512]]), func=AF.Tanh,
                        bias=v32(BGH + 4 * (8 + m), [[1, 1]]), scale=0.5), reads=[banks[bgb], R_const],
                        writes=tb_r)
                    S.op("dve", lambda e: e.scalar_tensor_tensor(
                        out=v32(ta_b, [[1, 512]]), in0=v32(ta_b, [[1, 512]]), scalar=1.0,
                        in1=pv32(ba * 512, [[1, 512]]), op0=ALU.add, op1=ALU.mult),
                        reads=[banks[ba]] + ta_r, writes=ta_r)
                    S.op("dve", lambda e: e.scalar_tensor_tensor(
                        out=v32(tb_b, [[1, 512]]), in0=v32(tb_b, [[1, 512]]), scalar=1.0,
                        in1=pv32(bb * 512, [[1, 512]]), op0=ALU.add, op1=ALU.mult),
                        reads=[banks[bb]] + tb_r, writes=tb_r)
                    S.op("dve", lambda e: e.tensor_tensor(
                        out=v16(C_MG + 2 * (m * 512), [[1, 512]]), in0=v32(ta_b, [[1, 512]]),
                        in1=v32(tb_b, [[1, 512]]), op=ALU.add), reads=ta_r + tb_r, writes=[R_mg])

                pend_ew = ew
                yield
            pend_ew()
            yield
            chk('merge_%d_%d' % (ui, tt))

        xslots = [(C_XR, R_xr[0], ch_xr[0]), (C_XR + 4096, R_xr[1], ch_xr[1]),
                  (C_WUP, R_wup[0], ch_wup[0]), (C_WUP + 4096, R_wup[1], ch_wup[1])]

        def load_x_tile(tt):
            c0_ = own0 + tt * 512
            for tb in range(4):
                xb_, R_x, ch_x = xslots[tb]
                S.op("sp", lambda e, xb_=xb_, t0=c0_ + tb * 128: e.dma_start(
                    out=v32(xb_, [[1, 1024]]), in_=xin[t0:t0 + 128, :]),
                    writes=[R_x], chan=ch_x)

        def mid_phase(tt):
            gt = ui * 4 + tt
            c0 = own0 + tt * 512
            S.op("dve", lambda e: e.tensor_copy(out=v32(C_X1T, [[516, 8]]), in_=v32(X1C, [[1, 8]])),
                 reads=[R_x1c] + R_x1t, writes=R_x1t)
            for tb in range(4):
                xb_, R_x, ch_x = xslots[tb]
                bp = next_bank_pair()
                mm_chain(S, cx, [(pv32(bp * 512 + m * 128, [[1, 128]]), v32(xb_ + 4 * (m * 128), [[1, 128]]),
                                  v32(IDF, [[1, 128]])) for m in range(8)], [R_x, R_const],
                         [banks[bp], banks[bp + 1]], transpose=True)
                S.op("act", lambda e, bp=bp, tb=tb: e.activation(
                    out=v32(C_X1T + 4 * (1 + tb * 128), [[516, 8], [1, 128]]),
                    in_=pv32(bp * 512, [[128, 8], [1, 128]]), func=AF.Copy),
                    reads=[banks[bp], banks[bp + 1]] + R_x1t, writes=R_x1t)
            for m in range(8):
                if m % 2 == 0:
                    wb, R_w = wm_stream.take()
                else:
                    wb = wb + 2048
                bk = next_bank()
                mm_chain(S, cx, [(pv32(bk * 512, [[1, 512]]), v16(wb + 2 * (k * 128), [[1, 128]]),
                                  v16(C_MG + 2 * (k * 512), [[1, 512]])) for k in range(8)], [R_w, R_mg],
                         [banks[bk]])
                S.op("dve", lambda e, m=m, bk=bk: e.scalar_tensor_tensor(
                    out=x1t(m, 1, 512), in0=pv32(bk * 512, [[1, 512]]), scalar=0.5, in1=x1t(m, 1, 512),
                    op0=ALU.mult, op1=ALU.add), reads=[banks[bk], R_x1t[m]], writes=[R_x1t[m]])
            chk('y_%d_%d' % (ui, tt))
            rms_bc(1, 512, 0)
            for m in range(8):
                S.op("dve", lambda e, m=m: e.scalar_tensor_tensor(
                    out=v16(C_XN1 + 2 * (m * 512), [[1, 512]]), in0=x1t(m, 1, 512), scalar=sm(SM_GF + m),
                    in1=v32(C_RB, [[1, 512]]), op0=ALU.mult, op1=ALU.mult),
                    reads=[R_x1t[m], R_rb[0], R_const], writes=[R_xn1[m]])
            chk('xn1_%d_%d' % (ui, tt))
            wup_slots = [(C_WUP, R_wup[0], ch_wup[0]), (C_WUP + 4096, R_wup[1], ch_wup[1]),
                         (C_XR, R_xr[0], ch_xr[0]), (C_XR + 4096, R_xr[1], ch_xr[1])]

            def load_wup(cp):
                wb, R_w, ch_w = wup_slots[cp % 4]
                S.op("pool", lambda e, cp=cp, wb=wb: e.dma_start(
                    out=v16(wb, [[1, 2048]]), in_=dv(WB["wup"], cp * 2048, [[NCP * 2048, 128], [1, 2048]])),
                    reads=[R_cv["wup"]], writes=[R_w], chan=ch_w)

            def chain_cp(cp, st_):
                wb, R_w, ch_w = wup_slots[cp % 4]
                UCG_, UCV_ = (C_UCG, C_UCV) if st_ == 0 else (C_UCG2, C_UCV2)
                R_g, R_v = (R_ucg, R_ucv) if st_ == 0 else (R_ucg2, R_ucv2)
                iG, iV, iX = 3 * st_, 3 * st_ + 1, 3 * st_ + 2
                if cp + 3 < NCP:
                    load_wup(cp + 3)
                bg, bv = next_bank(), next_bank()
                mm_chain(S, cx, [(pv32(bg * 512, [[1, 512]]), v16(wb + 2 * (k * 256), [[1, 128]]),
                                  v16(C_XN1 + 2 * (k * 512), [[1, 512]])) for k in range(8)], [R_w],
                         [banks[bg]], step_reads=[[R_xn1[k]] for k in range(8)])
                mm_chain(S, cx, [(pv32(bv * 512, [[1, 512]]), v16(wb + 2 * (k * 256 + 128), [[1, 128]]),
                                  v16(C_XN1 + 2 * (k * 512), [[1, 512]])) for k in range(8)], [R_w],
                         [banks[bv]], step_reads=[[R_xn1[k]] for k in range(8)])
                yield
                for (UC, R_uc, bk, ch_i, ti, ceng) in ((UCG_, R_g, bg, cp, iG, "dve"), (UCV_, R_v, bv, NCP + cp, iV, "dve")):
                    S.op("act", lambda e, UC=UC, ch_i=ch_i: e.activation(
                        out=v32(UC, [[1, 2]]), in_=v32(SAVE + 8 * ch_i, [[1, 2]]), func=AF.Copy),
                        reads=[R_save, R_uc], writes=[R_uc])
                    S.op("act", lambda e, UC=UC, bk=bk: e.activation(
                        out=v32(UC + 8, [[1, 512]]), in_=pv32(bk * 512, [[1, 512]]), func=AF.Copy),
                        reads=[banks[bk], R_uc], writes=[R_uc])
                    yield
                    S.op("act", lambda e, UC=UC, ch_i=ch_i: e.activation(
                        out=v32(SAVE + 8 * ch_i, [[1, 2]]), in_=v32(UC + 4 * 512, [[1, 2]]), func=AF.Copy),
                        reads=[R_uc, R_save], writes=[R_save])
                    S.op("act", lambda e, UC=UC, ch_i=ch_i, ti=ti: e.activation(
                        out=v32(T(ti), [[1, 512]]), in_=v32(UC + 4, [[1, 512]]), func=AF.Identity,
                        bias=sm(SM_CB + ch_i), scale=sm(SM_CW + 44 + ch_i)),
                        reads=[R_uc, R_const], writes=[R_tmp[ti]])
                    yield
                    S.op(ceng, lambda e, UC=UC, ch_i=ch_i, ti=ti: e.scalar_tensor_tensor(
                        out=v32(T(ti), [[1, 512]]), in0=v32(UC, [[1, 512]]), scalar=sm(SM_CW + ch_i),
                        in1=v32(T(ti), [[1, 512]]), op0=ALU.mult, op1=ALU.add),
                        reads=[R_uc, R_const, R_tmp[ti]], writes=[R_tmp[ti]])
                    yield
                    S.op(ceng, lambda e, UC=UC, ch_i=ch_i, ti=ti: e.scalar_tensor_tensor(
                        out=v32(T(ti), [[1, 512]]), in0=v32(UC + 8, [[1, 512]]), scalar=sm(SM_CW + 88 + ch_i),
                        in1=v32(T(ti), [[1, 512]]), op0=ALU.mult, op1=ALU.add),
                        reads=[R_uc, R_const, R_tmp[ti]], writes=[R_tmp[ti]])
                    if tt == 0 and ui > 0:
                        w2x = W2NM if ui == 1 else W2N1
                        w0x = W0NM if ui == 1 else W0N1
                        S.op(ceng, lambda e, UC=UC, ch_i=ch_i, ti=ti, w2x=w2x: e.scalar_tensor_tensor(
                            out=v32(T(ti), [[1, 1]]), in0=v32(UC + 8, [[1, 1]]), scalar=v32(w2x + 4 * ch_i, [[1, 1]]),
                            in1=v32(T(ti), [[1, 1]]), op0=ALU.mult, op1=ALU.add),
                            reads=[R_uc, R_const, R_tmp[ti]], writes=[R_tmp[ti]])
                        S.op(ceng, lambda e, UC=UC, ch_i=ch_i, ti=ti, w0x=w0x: e.scalar_tensor_tensor(
                            out=v32(T(ti) + 4, [[1, 1]]), in0=v32(UC + 4, [[1, 1]]),
                            scalar=v32(w0x + 4 * ch_i, [[1, 1]]), in1=v32(T(ti) + 4, [[1, 1]]),
                            op0=ALU.mult, op1=ALU.add),
                            reads=[R_uc, R_const, R_tmp[ti]], writes=[R_tmp[ti]])
                    yield
                S.op("act", lambda e: e.activation(out=v32(T(iX), [[1, 512]]), in_=v32(T(iG), [[1, 512]]),
                                                   func=AF.Square, scale=math.sqrt(0.044715)),
                     reads=[R_tmp[iG]], writes=[R_tmp[iX]])
                yield
                S.op("dve", lambda e: e.scalar_tensor_tensor(
                    out=v32(T(iX), [[1, 512]]), in0=v32(T(iX), [[1, 512]]), scalar=1.0, in1=v32(T(iG), [[1, 512]]),
                    op0=ALU.add, op1=ALU.mult), reads=[R_tmp[iX], R_tmp[iG]], writes=[R_tmp[iX]])
                yield
                S.op("act", lambda e: e.activation(out=v32(T(iX), [[1, 512]]), in_=v32(T(iX), [[1, 512]]),
                                                   func=AF.Tanh, scale=GELU_K), reads=[R_tmp[iX]], writes=[R_tmp[iX]])
                yield
                S.op("dve", lambda e: e.scalar_tensor_tensor(
                    out=v32(T(iX), [[1, 512]]), in0=v32(T(iX), [[1, 512]]), scalar=1.0, in1=v32(T(iG), [[1, 512]]),
                    op0=ALU.add, op1=ALU.mult), reads=[R_tmp[iX], R_tmp[iG]], writes=[R_tmp[iX]])
                yield
                S.op("dve", lambda e, cp=cp: e.tensor_tensor(
                    out=v16(C_HT + 2 * (cp * 512), [[1, 512]]), in0=v32(T(iX), [[1, 512]]), in1=v32(T(iV), [[1, 512]]),
                    op=ALU.mult), reads=[R_tmp[iX], R_tmp[iV]], writes=[R_ht[cp]])

            load_wup(0)
            load_wup(1)
            load_wup(2)
            mg = merge_gen(tt + 1) if tt + 1 < 4 else None
            for cp0 in range(0, NCP, 2):
                if mg is not None and cp0 >= 8:
                    next(mg, None)
                gens = [chain_cp(cp0, 0), chain_cp(cp0 + 1, 1)]
                alive = [True, True]
                step = 0
                while any(alive):
                    for gi_ in range(2):
                        if not alive[gi_]:
                            continue
                        if gi_ == 1 and step < 0:
                            continue
                        try:
                            next(gens[gi_])
                        except StopIteration:
                            alive[gi_] = False
                    step += 1
            if mg is not None:
                for _ in mg:
                    pass
            chk('up_%d_%d' % (ui, tt))
            j0 = 1 if gt == 0 else 0
            down_part(j0, 512 - j0, lambda k, j0=j0: v16(C_HT + 2 * (k * 512 + j0), [[1, 512 - j0]]))
            S.op("dve", lambda e: e.tensor_copy(out=v32(X1C, [[1, 8]]),
                                                in_=v32(C_X1T + 4 * 512, [[516, 8]])),
                 reads=R_x1t + [R_x1c], writes=[R_x1c])
            return j0, c0

        load_x_tile(0)
        for _ in merge_gen(0):
            pass
        for tt in range(4):
            j0, c0 = mid_phase(tt)
            if tt + 1 < 4:
                load_x_tile(tt + 1)
            final_part(j0, 512 - j0, c0 - 1 + j0)
            chk('down_%d_%d' % (ui, tt))

        if ui == NUNIT - 1:
            cv, t1 = FL_CV, FL_CV + 176
            R_fl = Res("flush")
            S.op("dve", lambda e: e.tensor_copy(out=v32(C_X1T, [[516, 8]]), in_=v32(X1C, [[1, 8]])),
                 reads=[R_x1c] + R_x1t, writes=R_x1t)
            S.op("dve", lambda e: e.tensor_tensor(out=v32(cv, [[1, 44]]), in0=v32(SAVE, [[2, 44]]),
                                                  in1=sm(SM_CW, 44), op=ALU.mult),
                 reads=[R_save, R_const], writes=[R_fl])
            S.op("dve", lambda e: e.tensor_tensor(out=v32(t1, [[1, 44]]), in0=v32(SAVE + 4, [[2, 44]]),
                                                  in1=sm(SM_CW + 44, 44), op=ALU.mult),
                 reads=[R_save, R_const, R_fl], writes=[R_fl])
            S.op("dve", lambda e: e.tensor_tensor(out=v32(cv, [[1, 44]]), in0=v32(cv, [[1, 44]]),
                                                  in1=v32(t1, [[1, 44]]), op=ALU.add), reads=[R_fl], writes=[R_fl])
            S.op("dve", lambda e: e.tensor_tensor(out=v32(cv, [[1, 44]]), in0=v32(cv, [[1, 44]]),
                                                  in1=sm(SM_CB, 44), op=ALU.add), reads=[R_fl, R_const],
                 writes=[R_fl])
            S.op("dve", lambda e: e.tensor_tensor(out=v32(t1, [[1, 22]]), in0=v32(cv, [[1, 22]]),
                                                  in1=v32(cv, [[1, 22]]), op=ALU.mult), reads=[R_fl], writes=[R_fl])
            S.op("dve", lambda e: e.tensor_scalar(out=v32(t1, [[1, 22]]), in0=v32(t1, [[1, 22]]), scalar1=0.044715,
                                                  scalar2=1.0, op0=ALU.mult, op1=ALU.add), reads=[R_fl],
                 writes=[R_fl])
            S.op("dve", lambda e: e.tensor_tensor(out=v32(t1, [[1, 22]]), in0=v32(t1, [[1, 22]]),
                                                  in1=v32(cv, [[1, 22]]), op=ALU.mult), reads=[R_fl], writes=[R_fl])
            S.op("act", lambda e: e.activation(out=v32(t1, [[1, 22]]), in_=v32(t1, [[1, 22]]), func=AF.Tanh,
                                               scale=GELU_K), reads=[R_fl], writes=[R_fl])
            S.op("dve", lambda e: e.scalar_tensor_tensor(out=v32(t1, [[1, 22]]), in0=v32(t1, [[1, 22]]), scalar=1.0,
                                                         in1=v32(cv, [[1, 22]]), op0=ALU.add, op1=ALU.mult),
                 reads=[R_fl], writes=[R_fl])
            S.op("dve", lambda e: e.tensor_tensor(out=v16(FL_H, [[1, 22]]), in0=v32(t1, [[1, 22]]),
                                                  in1=v32(cv + 88, [[1, 22]]), op=ALU.mult), reads=[R_fl],
                 writes=R_ht)
            down_part(0, 1, lambda k: v16(FL_H + 2 * k, [[1, 1]]))
            final_part(0, 1, TOK - 1)

    S.stopped = False
    S.barrier()

    with nc.Block() as block:
        @block.sync
        def _(e):
            S.emit("sp", e)

        @block.scalar
        def _(e):
            S.emit("act", e)

        @block.vector
        def _(e):
            S.emit("dve", e)

        @block.gpsimd
        def _(e):
            S.emit("pool", e)

        @block.tensor
        def _(e):
            S.emit("pe", e)
    stack.close()
    return nc


def _kp(w):
    K = w.shape[0] // 128
    return np.ascontiguousarray(w.reshape(K, 128, w.shape[1]).transpose(1, 0, 2))


_PROG = None


def kernel(x_prompt, x_sample, g_attn, w_in, b_gate, rel_bias, sink, w_a_out, w_b_out, w_o,
           g_ffn, w_up, conv_w, conv_b, w_down, g_final):
    global _PROG
    f = np.float32
    x_prompt = np.asarray(x_prompt, f)
    x_sample = np.asarray(x_sample, f)
    w_in = np.asarray(w_in, f)[0]
    w_a_out = np.asarray(w_a_out, f)[0]
    w_b_out = np.asarray(w_b_out, f)[0]
    w_o = np.asarray(w_o, f)[0]
    w_up = np.asarray(w_up, f)[0]
    w_down = np.asarray(w_down, f)[0]
    conv_w = np.asarray(conv_w, f)[0]
    conv_b = np.asarray(conv_b, f)[0]
    g_attn = np.asarray(g_attn, f)[0]
    g_ffn = np.asarray(g_ffn, f)[0]
    b_gate = np.asarray(b_gate, f)[0]
    sink = np.asarray(sink, f)[0]
    g_final = np.asarray(g_final, f)
    rel_bias = np.asarray(rel_bias, f)

    winp = _kp(w_in)
    wqkv = np.zeros((128, 6, 8, 384), f)
    for g in range(3):
        for hp in range(2):
            c = (4 * g + 2 * hp) * 64
            ph = g * 2 + hp
            wqkv[:, ph, :, 0:128] = winp[:, :, c:c + 128]
            wqkv[:, ph, :, 128:256] = winp[:, :, 768 + c:768 + c + 128]
            wqkv[:, ph, :, 256:384] = winp[:, :, 1536 + c:1536 + c + 128]
    QB0 = 2304
    KB0 = QB0 + 512
    VB0 = KB0 + 128
    G0 = VB0 + 128
    wbkv = np.zeros((128, 8, 256), f)
    wbkv[:, :, 0:128] = winp[:, :, KB0:KB0 + 128]
    wbkv[:, :, 128:256] = winp[:, :, VB0:VB0 + 128]
    wbq = np.zeros((128, 4, 8, 128), f)
    for ci in range(4):
        wbq[:, ci, :, 0:64] = winp[:, :, QB0 + 64 * ci:QB0 + 64 * ci + 64]
        wbq[:, ci, :, 64:128] = winp[:, :, QB0 + 64 * (4 + ci):QB0 + 64 * (4 + ci) + 64]
    wap = _kp(w_a_out)
    wbo_perm = np.zeros((4, 128, 1024), f)
    for ci in range(4):
        wbo_perm[ci, 0:64] = w_b_out[64 * ci:64 * ci + 64]
        wbo_perm[ci, 64:128] = w_b_out[64 * (4 + ci):64 * (4 + ci) + 64]
    wbop = np.ascontiguousarray(wbo_perm.transpose(1, 0, 2))
    wm = np.zeros((128, 8, 22, 128), f)
    for m in range(8):
        ms = slice(m * 128, (m + 1) * 128)
        wm[:, m, 0:2] = wap[:, :, ms]
        wm[:, m, 2:6] = wbop[:, :, ms]
        wm[:, m, 6:14] = winp[:, :, G0 + m * 128:G0 + (m + 1) * 128]
        wm[:, m, 14:22] = winp[:, :, G0 + 1024 + m * 128:G0 + 1024 + (m + 1) * 128]
    wop = _kp(w_o)
    wo = np.ascontiguousarray(wop.reshape(128, 8, 8, 128).transpose(0, 2, 1, 3))
    wupp = _kp(w_up)
    wup = np.zeros((128, NCP, 8, 256), f)
    for cp in range(NCP):
        wup[:, cp, :, 0:128] = wupp[:, :, cp * 128:(cp + 1) * 128]
        wup[:, cp, :, 128:256] = wupp[:, :, DFF + cp * 128:DFF + (cp + 1) * 128]
    wdnp = _kp(w_down)
    wdn = np.ascontiguousarray(wdnp.reshape(128, NCP, 8, 128).transpose(0, 2, 1, 3))

    small = np.zeros((128, 224), f)
    small[:, 0:8] = g_attn.reshape(8, 128).T
    small[:, 8:16] = g_ffn.reshape(8, 128).T
    small[:, 16:24] = g_final.reshape(8, 128).T
    small[:, 24:40] = b_gate.reshape(16, 128).T
    for k in range(3):
        small[:, 40 + 44 * k:40 + 44 * (k + 1)] = conv_w[k].reshape(44, 128).T
    small[:, 172:216] = conv_b.reshape(44, 128).T
    small[:, 216:224] = sink[None, :]
    oh = _onehot_tables()
    ident = np.eye(128, dtype=f)

    common = dict(wqkv=wqkv.reshape(128, -1), wbkv=wbkv.reshape(128, -1), wbq=wbq.reshape(128, -1),
                  wm=wm.reshape(128, -1), wo=wo.reshape(128, -1), wup=wup.reshape(128, -1),
                  wdn=wdn.reshape(128, -1), small=small, relb=rel_bias, oh=oh, ident=ident,
                  gfinb=np.ascontiguousarray(np.broadcast_to(g_final[None, :], (128, 1024))))
    in_maps = []
    for c in range(NCORES):
        if c < 4:
            xs = np.concatenate([x_prompt[c], x_sample[c]], axis=0)
            fl = 1.0
        else:
            b = 4 + 3 * (c - 4)
            xs = np.concatenate([x_sample[b], x_sample[b + 1], x_sample[b + 2]], axis=0)
            fl = 0.0
        flagv = np.zeros((128, 4), f)
        flagv[:, 0] = fl
        flagv[:, 1] = 1.0
        flagv[:64, 1] = fl
        flagv[:, 2] = fl - 1.0
        m = dict(common)
        m["xin"] = np.ascontiguousarray(xs)
        m["flagv"] = flagv
        in_maps.append(m)

    if _PROG is None:
        _PROG = build_program()
    res = run_bass_kernel_spmd(_PROG, in_maps, core_ids=list(range(NCORES)))
    y_prompt = np.zeros_like(x_prompt)
    y_sample = np.zeros_like(x_sample)
    for c in range(NCORES):
        y = res.results[c]["yout"]
        if c < 4:
            y_prompt[c] = y[:4096]
            y_sample[c] = y[4096:]
        else:
            b = 4 + 3 * (c - 4)
            for i in range(3):
                y_sample[b + i] = y[2048 * i:2048 * (i + 1)]
    return y_prompt, y_sample
```

```python
import math
from contextlib import ExitStack

import numpy as np

import concourse.bass as bass
import concourse.mybir as mybir
from concourse.bass_utils import run_bass_kernel_spmd

F32 = mybir.dt.float32
BF16 = mybir.dt.bfloat16
ALU = mybir.AluOpType
AF = mybir.ActivationFunctionType

NCORES = 8
TOK = 6144
D = 1024
UNIT = 2048
NUNIT = 3
DFF = 2816
NCP = 22
EPS = 1e-6
PAD = 256
GELU_K = 0.7978845608028654
DILS = (1, 4, 16)

ENGS = ("sp", "act", "dve", "pool", "pe")


class Res:
    __slots__ = ("name", "w", "r")

    def __init__(self, name=""):
        self.name = name
        self.w = None
        self.r = {}


class Sched:
    def __init__(self, nc, stack):
        self.nc = nc
        self.stack = stack
        self.q = {e: [] for e in ENGS}
        self.sem = {}
        self.val = {}
        for e in ENGS:
            self.sem[e] = stack.enter_context(nc.semaphore("sem_" + e))
            self.val[e] = 0
        self.waited = {e: {} for e in ENGS}
        self.nchan = 0
        self.stopped = False

    def chan(self):
        sid = "dma%d" % self.nchan
        self.nchan += 1
        self.sem[sid] = self.stack.enter_context(self.nc.semaphore(sid))
        self.val[sid] = 0
        return sid

    def op(self, eng, fn, reads=(), writes=(), extra=(), chan=None, signal=True, self_wait=False):
        if self.stopped:
            return None
        deps = {}

        def need(ev):
            if ev is None:
                return
            if deps.get(ev[0], 0) < ev[1]:
                deps[ev[0]] = ev[1]

        for r in reads:
            need(r.w)
        for w in writes:
            need(w.w)
            for sid, v in w.r.items():
                need((sid, v))
        for e in extra:
            need(e)
        waits = []
        wd = self.waited[eng]
        for sid, v in deps.items():
            if sid == eng and eng == "pe" and not self_wait:
                continue
            if wd.get(sid, 0) < v:
                wd[sid] = v
                waits.append((sid, v))
        if chan is not None:
            self.val[chan] += 16
            ev = (chan, self.val[chan])
            inc = (chan, 16)
        elif signal:
            self.val[eng] += 1
            ev = (eng, self.val[eng])
            inc = (eng, 1)
        else:
            ev = None
            inc = None
        self.q[eng].append((waits, fn, inc))
        if ev is not None:
            for r in reads:
                if r.r.get(ev[0], 0) < ev[1]:
                    r.r[ev[0]] = ev[1]
            for w in writes:
                w.w = ev
                w.r = {}
        return ev

    def barrier(self):
        if self.stopped:
            return
        snap = dict(self.val)
        for e in ENGS:
            waits = []
            for sid, v in snap.items():
                if v == 0 or sid == e:
                    continue
                if self.waited[e].get(sid, 0) < v:
                    self.waited[e][sid] = v
                    waits.append((sid, v))
            if waits:
                self.q[e].append((waits, None, None))

    def emit(self, eng_name, e):
        for waits, fn, inc in self.q[eng_name]:
            for sid, v in waits:
                e.wait_ge(self.sem[sid], v)
            if fn is None:
                continue
            inst = fn(e)
            if inc is not None:
                inst.then_inc(self.sem[inc[0]], inc[1])


def _rel_bucket(rel):
    rel = np.asarray(rel, dtype=np.int64)
    nb = 16
    max_exact = 8
    ret = np.where(rel > 0, nb, 0)
    n = np.abs(rel)
    nf = np.maximum(n, 1).astype(np.float32)
    large = max_exact + (np.log(nf / np.float32(max_exact)) / np.float32(math.log(1024 / max_exact))
                         * np.float32(nb - max_exact)).astype(np.int32)
    large = np.minimum(large, nb - 1)
    return ret + np.where(n < max_exact, n, large)


def _onehot_tables():
    oh = np.zeros((32, 4 * 512), np.float32)
    for t in range(4):
        d = DILS[t] if t < 3 else 1
        R = 64 if t < 3 else 128
        for delta in range(-R, R + 1):
            b = int(_rel_bucket(delta * d))
            oh[b, t * 512 + delta + PAD] = 1.0
    return oh


def geom(d, left, right):
    n = UNIT // d
    klo = -64 if left else 0
    khi = n + (64 if right else 0)
    nk = khi - klo
    kblocks = []
    s = klo
    while s < khi:
        kblocks.append((s, min(128, khi - s)))
        s += 128
    qblocks = []
    i = -1
    while True:
        qs = klo + 64 + 128 * i
        if qs >= n:
            break
        v0, v1 = max(qs, 0), min(qs + 128, n)
        if v1 > v0:
            kbs = []
            for side, bi in ((0, i), (1, i + 1)):
                if 0 <= bi < len(kblocks):
                    kbs.append((side, bi))
            qblocks.append((qs, v0, v1, kbs))
        i += 1
    return n, klo, khi, nk, kblocks, qblocks


def geom_b(left, right):
    n = UNIT
    klo = -128 if left else 0
    khi = n + (128 if right else 0)
    nk = khi - klo
    kblocks = [(s, 128) for s in range(klo, khi, 128)]
    qblocks = []
    for i in range(n // 128):
        qs = 128 * i
        kbs = []
        for m in range(3):
            st = qs - 128 + 128 * m
            if klo <= st < khi:
                kbs.append((m, (st - klo) // 128))
        qblocks.append((qs, qs, qs + 128, kbs))
    return n, klo, khi, nk, kblocks, qblocks


class Ctx:
    pass


def mm_chain(S, cx, steps, reads, wres, transpose=False, step_reads=None):
    n = len(steps)
    ev = None
    for i, (o, a, b) in enumerate(steps):
        first, last = (i == 0), (i == n - 1)
        if transpose:
            fn = (lambda e, o=o, a=a, b=b: e.transpose(o, a, b))
        else:
            fn = (lambda e, o=o, a=a, b=b, first=first, last=last: e.matmul(o, a, b, start=first, stop=last))
        rd = list(reads) + (list(step_reads[i]) if step_reads is not None else [])
        ev = S.op("pe", fn, reads=rd, writes=wres if (first or last) else (), signal=last)
    return ev


STOP = [None]
ATT_CUT = [5]


class _Stop(Exception):
    pass


def build_program():
    nc = bass.Bass("TRN2", target_bir_lowering=False)
    stack = ExitStack()

    def dram_in(name, shape, dt=F32):
        return nc.dram_tensor(name, list(shape), dt, kind="ExternalInput").ap()

    xin = dram_in("xin", [TOK, D])
    yout = nc.dram_tensor("yout", [TOK, D], F32, kind="ExternalOutput").ap()
    flagv_d = dram_in("flagv", [128, 4])
    wqkv_d = dram_in("wqkv", [128, 6 * 3072])
    wbkv_d = dram_in("wbkv", [128, 2048])
    wbq_d = dram_in("wbq", [128, 4 * 1024])
    wm_d = dram_in("wm", [128, 8 * 2816])
    wo_d = dram_in("wo", [128, 8 * 1024])
    wup_d = dram_in("wup", [128, NCP * 2048])
    wdn_d = dram_in("wdn", [128, 8 * 2816])
    small_d = dram_in("small", [128, 224])
    gfinb_d = dram_in("gfinb", [128, 1024])
    relb_d = dram_in("relb", [32, 20])
    oh_d = dram_in("oh", [32, 2048])
    ident_d = dram_in("ident", [128, 128])
    er_d = nc.dram_tensor("er_scr", [4 * 20, 512], F32, kind="Internal").ap()
    esave_d = nc.dram_tensor("esave", [128, 6144], F32, kind="Internal").ap()
    dscr_d = nc.dram_tensor("dscr", [2, 2048], F32, kind="Internal").ap()
    rscr_d = nc.dram_tensor("rscr", [2, 2048], F32, kind="Internal").ap()

    S = Sched(nc, stack)
    cx = Ctx()

    XN_OFF = 0
    OAB_OFF = 49152
    CONST_OFF = 73728
    R_OFF = 81920
    R_SIZE = 124 * 1024
    TOTAL = R_OFF + R_SIZE
    arena = stack.enter_context(nc.sbuf_tensor("arena", [128, TOTAL // 2], BF16))
    A16 = arena[:]
    A32 = arena[:].bitcast(F32)
    PS16 = A16.ap[0][0]
    PS32 = A32.ap[0][0]
    psum = stack.enter_context(nc.psum_tensor("psum", [128, 4096], F32))
    P32 = psum[:]
    P16 = psum[:].bitcast(BF16)
    PP32 = P32.ap[0][0]
    PP16 = P16.ap[0][0]

    def v16(boff, dims, p0=0, np_=128):
        assert boff % 2 == 0
        return bass.AP(A16.tensor, p0 * PS16 + boff // 2, [[PS16, np_]] + [list(x) for x in dims])

    def v32(boff, dims, p0=0, np_=128):
        assert boff % 4 == 0
        return bass.AP(A32.tensor, p0 * PS32 + boff // 4, [[PS32, np_]] + [list(x) for x in dims])

    def pv32(col, dims, p0=0, np_=128):
        return bass.AP(P32.tensor, p0 * PP32 + col, [[PP32, np_]] + [list(x) for x in dims])

    def pv16(col, dims, p0=0, np_=128):
        return bass.AP(P16.tensor, p0 * PP16 + col, [[PP16, np_]] + [list(x) for x in dims])

    def dv(ap, off, dims):
        return bass.AP(ap.tensor, ap.offset + off, [list(x) for x in dims])

    banks = [Res("bank%d" % i) for i in range(8)]
    bank_rr = [0]

    def next_bank():
        b = bank_rr[0]
        bank_rr[0] = (b + 1) % 8
        return b

    def next_bank_pair():
        if bank_rr[0] % 2:
            bank_rr[0] = (bank_rr[0] + 1) % 8
        b = bank_rr[0]
        bank_rr[0] = (b + 2) % 8
        return b

    c_cur = [CONST_OFF]

    def calloc(nbytes):
        o = c_cur[0]
        c_cur[0] += (nbytes + 31) // 32 * 32
        assert c_cur[0] <= R_OFF
        return o

    IDB = calloc(256)
    IDF = calloc(512)
    ONESB = calloc(256)
    SMALL = calloc(896)
    BGH = calloc(64)
    ESINK = calloc(32)
    FLAG = calloc(16)
    SSQ = calloc(128)
    RSTD = calloc(128)
    SAVE = calloc(44 * 2 * 4)
    W2NM = calloc(44 * 4)
    W0NM = calloc(44 * 4)
    W2N1 = calloc(44 * 4)
    W0N1 = calloc(44 * 4)
    FL_CV = calloc(44 * 4 * 2)
    FL_H = calloc(64)
    X1C = calloc(32)
    RDS = calloc(128)
    EPSC = calloc(32)
    ZEROC = calloc(32)
    GFINB = calloc(4096)
    RT = SSQ + 64
    SM_GA, SM_GF, SM_GFIN, SM_BG, SM_CW, SM_CB, SM_SINK = 0, 8, 16, 24, 40, 172, 216
    R_const = Res("const")

    def sm(off, n=1, p0=0, np_=128):
        return v32(SMALL + 4 * off, [[1, n]], p0, np_)

    B_ACC = R_OFF
    B_QT = B_ACC + 16384
    B_KT = B_QT + 4096
    B_VA = B_KT + 6144
    B_EA = B_VA + 12288
    B_EB = B_EA + 12288
    B_ES = B_EB + 12288
    B_PT = B_ES + 4096
    B_RD = B_PT + 2048
    B_RS = B_RD + 4096
    B_WQ = B_RS + 4096
    B_ACC2 = B_WQ + 12288
    B_END = B_ACC2 + 16384
    assert B_END <= TOTAL
    A_XB = R_OFF
    A_XS = A_XB + 49152
    A_JUNK = A_XS + 4096
    assert A_JUNK + 2048 <= TOTAL
    C_X1T = R_OFF
    C_XN1 = C_X1T + 16512
    C_HT = C_XN1 + 8192
    C_UCG = C_HT + 22528
    C_UCV = C_UCG + 2064
    C_UCG2 = C_UCV + 2064
    C_UCV2 = C_UCG2 + 2064
    C_TMP = C_UCV2 + 2064
    C_SQ = C_TMP + 12288
    C_RB = C_SQ + 2048
    C_XR = C_RB + 4096
    C_OUTS = C_XR + 8192
    C_MG = C_OUTS + 4096
    C_WM = C_MG + 8192
    C_WUP = C_WM + 11264
    C_WDN = C_WUP + 8192
    C_END = C_WDN + 11264
    assert C_END <= TOTAL, C_END - TOTAL

    ch_c = S.chan()
    S.op("sp", lambda e: e.dma_start(out=v32(SMALL, [[1, 224]]), in_=small_d), writes=[R_const], chan=ch_c)
    S.op("sp", lambda e: e.dma_start(out=v32(GFINB, [[1, 1024]]), in_=gfinb_d), writes=[R_const], chan=ch_c)
    S.op("sp", lambda e: e.dma_start(out=v32(IDF, [[1, 128]]), in_=ident_d), writes=[R_const], chan=ch_c)
    S.op("sp", lambda e: e.dma_start(out=v32(FLAG, [[1, 4]]), in_=flagv_d), writes=[R_const], chan=ch_c)
    S.op("dve", lambda e: e.tensor_copy(out=v16(IDB, [[1, 128]]), in_=v32(IDF, [[1, 128]])),
         reads=[R_const], writes=[R_const])
    S.op("pool", lambda e: e.memset(v16(ONESB, [[1, 128]]), 1.0), writes=[R_const])
    S.op("pool", lambda e: e.memset(v32(SAVE, [[1, 88]]), 0.0), writes=[R_const])
    S.op("pool", lambda e: e.memset(v32(X1C, [[1, 8]]), 0.0), writes=[R_const])
    S.op("pool", lambda e: e.memset(v32(EPSC, [[1, 8]]), EPS), writes=[R_const])
    S.op("pool", lambda e: e.memset(v32(ZEROC, [[1, 8]]), 0.0), writes=[R_const])
    S.op("dve", lambda e: e.tensor_scalar(out=v32(BGH, [[1, 16]]), in0=sm(SM_BG, 16), scalar1=0.5, scalar2=None,
                                          op0=ALU.mult), reads=[R_const], writes=[R_const])
    S.op("act", lambda e: e.activation(out=v32(ESINK, [[1, 8]]), in_=sm(SM_SINK, 8), func=AF.Exp),
         reads=[R_const], writes=[R_const])
    for dst, src, scal in ((W2NM, SM_CW + 88, None), (W0NM, SM_CW, None), (W2N1, SM_CW + 88, -1.0),
                           (W0N1, SM_CW, -1.0)):
        if scal is None:
            S.op("dve", lambda e, dst=dst, src=src: e.tensor_scalar(
                out=v32(dst, [[1, 44]]), in0=sm(src, 44), scalar1=v32(FLAG + 8, [[1, 1]]), scalar2=None,
                op0=ALU.mult), reads=[R_const], writes=[R_const])
        else:
            S.op("dve", lambda e, dst=dst, src=src, scal=scal: e.tensor_scalar(
                out=v32(dst, [[1, 44]]), in0=sm(src, 44), scalar1=scal, scalar2=None, op0=ALU.mult),
                reads=[R_const], writes=[R_const])

    R_scr = Res("scratch_R")
    ch_e = S.chan()
    OHS = R_OFF + 90112
    RELS = OHS + 8192
    ONES32 = OHS + 8192 + 128
    ERT = OHS + 16384
    S.op("sp", lambda e: e.dma_start(out=v32(OHS, [[1, 2048]], 0, 32), in_=oh_d), writes=[R_scr], chan=ch_e)
    S.op("sp", lambda e: e.dma_start(out=v32(RELS, [[1, 20]], 0, 32), in_=relb_d), writes=[R_scr], chan=ch_e)
    S.op("pool", lambda e: e.memset(v32(ONES32, [[1, 20]], 0, 32), 1.0), writes=[R_scr])
    R_er = Res("er")
    ch_er = S.chan()
    R_ert = [Res("ert0"), Res("ert1")]
    for t in range(4):
        bv = next_bank()
        bm = next_bank()
        mm_chain(S, cx, [(pv32(bv * 512, [[1, 512]], 0, 20), v32(RELS, [[1, 20]], 0, 32),
                          v32(OHS + t * 2048, [[1, 512]], 0, 32))], [R_scr], [banks[bv]])
        mm_chain(S, cx, [(pv32(bm * 512, [[1, 512]], 0, 20), v32(ONES32, [[1, 20]], 0, 32),
                          v32(OHS + t * 2048, [[1, 512]], 0, 32))], [R_scr], [banks[bm]])
        tmp = ERT + (t % 2) * 2048
        R_t = R_ert[t % 2]
        S.op("act", lambda e, bv=bv, tmp=tmp: e.activation(out=v32(tmp, [[1, 512]], 0, 20),
                                                            in_=pv32(bv * 512, [[1, 512]], 0, 20), func=AF.Exp),
             reads=[banks[bv]], writes=[R_t])
        S.op("dve", lambda e, bm=bm, tmp=tmp: e.tensor_tensor(out=v32(tmp, [[1, 512]], 0, 20),
                                                               in0=v32(tmp, [[1, 512]], 0, 20),
                                                               in1=pv32(bm * 512, [[1, 512]], 0, 20), op=ALU.mult),
             reads=[banks[bm], R_t], writes=[R_t])
        S.op("sp", lambda e, t=t, tmp=tmp: e.dma_start(out=er_d[t * 20:(t + 1) * 20, :],
                                                       in_=v32(tmp, [[1, 512]], 0, 20)),
             reads=[R_t], writes=[R_er], chan=ch_er)

    CV_OFF = R_OFF + 110592
    cv_names = [("wm", wm_d, 8 * 2816), ("wo", wo_d, 8192), ("wup", wup_d, NCP * 2048), ("wdn", wdn_d, 8 * 2816),
                ("wqkv", wqkv_d, 6 * 3072), ("wbkv", wbkv_d, 2048), ("wbq", wbq_d, 4096)]
    WB = {}
    R_cv = {}
    R_cvs = [Res("cvs0"), Res("cvs1")]
    ch_cvi = [S.chan(), S.chan()]
    ch_cvo = [S.chan(), S.chan()]
    cv_chunks = []
    for (nm, src_ap, ncols) in cv_names:
        WB[nm] = nc.dram_tensor(nm + "_b", [128, ncols], BF16, kind="Internal").ap()
        R_cv[nm] = Res("cv_" + nm)
        c = 0
        while c < ncols:
            w_ = min(4096, ncols - c)
            cv_chunks.append((nm, src_ap, ncols, c, w_))
            c += w_

    def conv_gen():
        def emit_in(k):
            nm, src_ap, ncols, c, w_ = cv_chunks[k]
            sl = k % 2
            S.op("pool", lambda e: e.dma_start(out=v16(CV_OFF + sl * 8192, [[1, w_]]),
                                               in_=dv(src_ap, c, [[ncols, 128], [1, w_]])),
                 writes=[R_cvs[sl]], chan=ch_cvi[sl])

        def emit_out(k):
            nm, src_ap, ncols, c, w_ = cv_chunks[k]
            sl = k % 2
            S.op("pool", lambda e: e.dma_start(out=dv(WB[nm], c, [[ncols, 128], [1, w_]]),
                                               in_=v16(CV_OFF + sl * 8192, [[1, w_]])),
                 reads=[R_cvs[sl]], writes=[R_cv[nm]], chan=ch_cvo[sl])

        for k in range(len(cv_chunks)):
            emit_in(k)
            if k >= 1:
                emit_out(k - 1)
            yield
        emit_out(len(cv_chunks) - 1)
        yield

    cvg = conv_gen()

    def conv_step(n):
        for _ in range(n):
            next(cvg, None)


    def chk(tag):
        if STOP[0] == tag:
            S.stopped = True

    units = [(0, False, True), (2048, True, False), (4096, False, False)]
    R_xn = Res("xn")
    R_oab = Res("oab")
    XNW = 3072

    def xn_ap(k, col, n, step=1, p0=0, np_=128):
        return v16(XN_OFF + 2 * (k * XNW + col), [[step, n]], p0, np_)

    R_save = Res("save")
    R_x1t = [Res("x1t%d" % i) for i in range(8)]
    R_x1c = Res("x1c")
    R_esave = Res("esave")
    R_dscr = Res("dscr")
    R_rscr = Res("rscr")
    R_rds = Res("rds")
    ch_out = S.chan()
    R_outsA = Res("outsA")
    R_outsB = Res("outsB")

    chk('prologue')
    for ui, (own0, left, right) in enumerate(units):
        ext0 = own0 - (1024 if left else 0)
        next_ = 2048 + (1024 if (left or right) else 0)
        ownc = own0 - ext0
        nblk = next_ // 128

        if ui > 0:
            S.barrier()
        R_wq = [Res("wq0"), Res("wq1")]
        ch_wq = [S.chan(), S.chan()]
        wq_i = [0]
        def load_wslab(src_ap, ncols, R_src):
            s = wq_i[0] % 2
            wq_i[0] += 1
            S.op("pool", lambda e, s=s: e.dma_start(out=v16(B_WQ + s * 6144, [[1, ncols]]), in_=src_ap),
                 reads=[R_src], writes=[R_wq[s]], chan=ch_wq[s])
            return s

        slab_specs = []
        R_nodep = Res("nodep")
        if ui == 0:
            for hp_ in range(2):
                for g_ in range(3):
                    slab_specs.append((dv(wqkv_d, (g_ * 2 + hp_) * 3072, [[6 * 3072, 128], [1, 3072]]), 3072, R_nodep))
            slab_specs.append((dv(wbkv_d, 0, [[2048, 128], [1, 2048]]), 2048, R_nodep))
            for ci_ in range(4):
                slab_specs.append((dv(wbq_d, ci_ * 1024, [[4096, 128], [1, 1024]]), 1024, R_nodep))
        else:
            for hp_ in range(2):
                for g_ in range(3):
                    slab_specs.append((dv(WB["wqkv"], (g_ * 2 + hp_) * 3072, [[6 * 3072, 128], [1, 3072]]), 3072,
                                       R_cv["wqkv"]))
            slab_specs.append((dv(WB["wbkv"], 0, [[2048, 128], [1, 2048]]), 2048, R_cv["wbkv"]))
            for ci_ in range(4):
                slab_specs.append((dv(WB["wbq"], ci_ * 1024, [[4096, 128], [1, 1024]]), 1024, R_cv["wbq"]))
        slab_state = {"i": 0, "slot": load_wslab(*slab_specs[0])}

        R_xb = [Res("xb%d" % i) for i in range(12)]
        ch_xb = [S.chan() for _ in range(12)]
        R_ssqs = [Res("ssq0"), Res("ssq1")]
        R_xs = [Res("xs%d" % i) for i in range(2)]
        R_junk = Res("junk")
        for bi_, b0 in enumerate(range(0, nblk, 6)):
            nb = min(6, nblk - b0)
            hf = bi_ % 2
            R_ssq = R_ssqs[hf]
            SSQ_ = SSQ + 32 * hf
            RSTD_ = RSTD + 32 * hf
            S.op("act", lambda e, SSQ_=SSQ_: e.activation(out=v32(SSQ_, [[1, 8]]), in_=v32(ZEROC, [[1, 8]]), func=AF.Copy),
                 reads=[R_const], writes=[R_ssq])
            for j in range(nb):
                b = b0 + j
                sl = hf * 6 + j
                S.op(("sp", "act")[j % 2], lambda e, sl=sl, r0=ext0 + b * 128: e.dma_start(out=v32(A_XB + sl * 4096, [[1, 1024]]),
                                                         in_=xin[r0: r0 + 128, :]),
                     writes=[R_xb[sl]], chan=ch_xb[sl])
                S.op("act", lambda e, sl=sl, j=j, SSQ_=SSQ_: e.activation(out=v16(A_JUNK, [[1, 1024]]),
                                                        in_=v32(A_XB + sl * 4096, [[1, 1024]]), func=AF.Square,
                                                        accum_out=v32(SSQ_ + 4 * j, [[1, 1]])),
                     reads=[R_xb[sl]], writes=[R_junk, R_ssq])
            S.op("dve", lambda e, nb=nb, SSQ_=SSQ_, RSTD_=RSTD_: e.tensor_scalar(
                out=v32(RSTD_, [[1, nb]]), in0=v32(SSQ_, [[1, nb]]),
                scalar1=1.0 / D, scalar2=EPS, op0=ALU.mult, op1=ALU.add),
                 reads=[R_ssq], writes=[R_ssq])
            S.op("act", lambda e, nb=nb, RSTD_=RSTD_: e.activation(out=v32(RSTD_, [[1, nb]]), in_=v32(RSTD_, [[1, nb]]),
                                                      func=AF.Sqrt), reads=[R_ssq], writes=[R_ssq])
            S.op("dve", lambda e, nb=nb, RSTD_=RSTD_: e.reciprocal(out=v32(RSTD_, [[1, nb]]), in_=v32(RSTD_, [[1, nb]])),
                 reads=[R_ssq], writes=[R_ssq])
            pend_ev = None
            for j in range(nb):
                b = b0 + j
                s = b % 2
                sl = hf * 6 + j
                S.op("dve", lambda e, j=j, s=s, sl=sl, RSTD_=RSTD_: e.tensor_scalar(out=v16(A_XS + s * 2048, [[1, 1024]]),
                                                                in0=v32(A_XB + sl * 4096, [[1, 1024]]),
                                                                scalar1=v32(RSTD_ + 4 * j, [[1, 1]]), scalar2=None,
                                                                op0=ALU.mult),
                     reads=[R_xb[sl], R_ssq], writes=[R_xs[s]])
                bk = next_bank()
                mm_chain(S, cx, [(pv16(bk * 1024 + c * 128, [[1, 128]]), v16(A_XS + s * 2048 + c * 256, [[1, 128]]),
                                  v16(IDB, [[1, 128]])) for c in range(8)], [R_xs[s], R_const], [banks[bk]],
                         transpose=True)
                if pend_ev is not None:
                    pend_ev()
                pend_ev = (lambda b=b, bk=bk: S.op("dve", lambda e: e.tensor_tensor(
                    out=v16(XN_OFF + 2 * (b * 128), [[XNW, 8], [1, 128]]),
                    in0=pv16(bk * 1024, [[128, 8], [1, 128]]),
                    in1=v32(SMALL + 4 * SM_GA, [[1, 8], [0, 128]]), op=ALU.mult),
                    reads=[banks[bk], R_const], writes=[R_xn]))
            if pend_ev is not None:
                pend_ev()

        chk('ph1_%d' % ui)
        S.barrier()
        R_E = Res("E")
        ch_E = S.chan()
        R_es = [Res("es0"), Res("es1")]
        R_pt = [Res("pt0"), Res("pt1")]
        if ui == 0:
            R_stg = Res("stage")
            STG = B_ACC
            for g in range(3):
                dst = STG + g * 1024 * 4
                src = bass.AP(er_d.tensor, (g * 20 + 4 * g) * 512 - 64 + PAD - 127,
                              [[1, 128], [512, 4], [128, 2], [1, 128]])
                S.op(("sp", "act")[g % 2], lambda e, dst=dst, src=src: e.dma_start(
                    out=v32(dst, [[256, 4], [128, 2], [1, 128]]), in_=src),
                     reads=[R_er], writes=[R_stg], chan=ch_E)
            src = bass.AP(er_d.tensor, (3 * 20 + 12) * 512 - 128 + PAD - 127, [[1, 128], [512, 8], [128, 3], [1, 128]])
            S.op("act", lambda e, src=src: e.dma_start(out=v32(STG + 12288, [[384, 8], [128, 3], [1, 128]]), in_=src),
                 reads=[R_er], writes=[R_stg], chan=ch_E)
            S.op("dve", lambda e: e.tensor_copy(out=v32(B_EA, [[128, 48], [1, 128]]),
                                                in_=v32(STG + 127 * 4, [[128, 48], [-1, 128]])),
                 reads=[R_stg], writes=[R_E])
            ch_Es = S.chan()
            S.op("sp", lambda e: e.dma_start(out=esave_d, in_=v32(B_EA, [[1, 6144]])), reads=[R_E],
                 writes=[R_esave], chan=ch_Es)
            S.barrier()
        else:
            S.op("sp", lambda e: e.dma_start(out=v32(B_EA, [[1, 6144]]), in_=esave_d), reads=[R_esave],
                 writes=[R_E], chan=ch_E)
        chk('etab_%d' % ui)
        R_accs = [Res("acc0"), Res("acc1")]
        ACCB = [B_ACC, B_ACC2]
        acc_cur = {"i": 0}
        R_qt = Res("qt")
        R_kt = Res("kt")
        R_va = Res("va")
        R_rd = Res("rd")
        R_rs = Res("rs")
        ch_rs = S.chan()

        def proj_fm(s, wcol, wk, evac, col0, n):
            bk = next_bank()
            mm_chain(S, cx, [(pv32(bk * 512, [[1, n]]), v16(B_WQ + s * 6144 + 2 * (k * wk + wcol), [[1, 128]]),
                              xn_ap(k, col0, n)) for k in range(8)], [R_wq[s], R_xn], [banks[bk]])
            evac(bk)

        def attn(rq_list, kblocks_all, klo, nk, n, d, nside, e_offs, hh_list, hooks=None):
            SW = nside * 128
            groups = [hh_list] if nside == 2 else [[hh] for hh in hh_list]
            items = []
            for (r, qblocks) in rq_list:
                for (qs, v0, v1, kbs) in qblocks:
                    for gidx, grp in enumerate(groups):
                        items.append((r, qs, v0, v1, kbs, gidx, grp))

            def stage_s(idx):
                r, qs, v0, v1, kbs, gidx, grp = items[idx]
                w = v1 - v0
                c0 = v0 - qs
                bs = next_bank()
                es = idx % 2
                steps = []
                for gi, hh in enumerate(grp):
                    for (side, gb) in kbs:
                        st, ln = kblocks_all[gb]
                        steps.append((pv32(bs * 512 + gi * SW + side * 128 + c0, [[1, w]], 0, ln),
                                      v16(B_KT + 2 * (r * nk + (st - klo)), [[1, ln]], 64 * hh, 64),
                                      v16(B_QT + 2 * (r * n + v0), [[1, w]], 64 * hh, 64)))
                nst = len(steps)
                nper = len(kbs)
                prev_ev = None
                for i, (o, a, b_) in enumerate(steps):
                    boundary_next = (nside == 2 and (i + 1) % nper == 0 and i != nst - 1)
                    boundary_here = (nside == 2 and i % nper == 0 and i != 0)
                    ev_ = S.op("pe", lambda e, o=o, a=a, b_=b_: e.matmul(o, a, b_, start=True, stop=True),
                               reads=[R_kt, R_qt], writes=[banks[bs]] if i in (0, nst - 1) else (),
                               extra=[prev_ev] if (boundary_here and prev_ev) else (),
                               signal=(i == nst - 1) or boundary_next, self_wait=boundary_here)
                    if boundary_next:
                        prev_ev = ev_
                wd = len(grp) * SW
                S.op("act", lambda e, bs=bs, es=es, wd=wd: e.activation(
                    out=v32(B_ES + es * 2048, [[1, wd]]), in_=pv32(bs * 512, [[1, wd]]), func=AF.Exp,
                    scale=0.125), reads=[banks[bs]], writes=[R_es[es]])
                eoff = e_offs[gidx]
                S.op("dve", lambda e, es=es, wd=wd, eoff=eoff: e.tensor_tensor(
                    out=v16(B_PT + es * 1024, [[1, wd]]), in0=v32(B_ES + es * 2048, [[1, wd]]),
                    in1=v32(eoff, [[1, wd]]), op=ALU.mult), reads=[R_es[es], R_E], writes=[R_pt[es]])

            def stage_pv(idx):
                r, qs, v0, v1, kbs, gidx, grp = items[idx]
                w = v1 - v0
                c0 = v0 - qs
                es = idx % 2
                bo = next_bank()
                nh = len(grp)
                h0 = grp[0]
                cnt = 0
                tot = nh * len(kbs)
                for gi, hh in enumerate(grp):
                    for j, (side, gb) in enumerate(kbs):
                        st, ln = kblocks_all[gb]
                        cnt += 1
                        o = pv32(bo * 512 + hh * 128 + c0, [[1, w]])
                        a = v16(B_VA + 2 * (gb * 192 + hh * 64), [[1, 128]], 0, ln)
                        b_ = v16(B_PT + es * 1024 + 2 * (gi * SW + side * 128 + c0), [[1, w]], 0, ln)
                        S.op("pe", lambda e, o=o, a=a, b_=b_, j=j, nk_=len(kbs): e.matmul(
                            o, a, b_, start=(j == 0), stop=(j == nk_ - 1)),
                            reads=[R_va, R_pt[es]], writes=[banks[bo]] if cnt in (1, tot) else (),
                            signal=(cnt == tot))
                ab = ACCB[acc_cur["i"]]
                R_acc = R_accs[acc_cur["i"]]
                S.op("dve", lambda e, bo=bo, nh=nh, h0=h0, v0=v0, w=w, c0=c0, r=r, ab=ab: e.tensor_tensor(
                    out=v32(ab + 4 * (h0 * 2048 + v0 * d + r), [[2048, nh], [d, w]]),
                    in0=pv32(bo * 512 + h0 * 128 + c0, [[128, nh], [1, w]]),
                    in1=v32(ab + 4 * (h0 * 2048 + v0 * d + r), [[2048, nh], [d, w]]), op=ALU.add),
                    reads=[banks[bo], R_acc], writes=[R_acc])

            for idx in range(len(items) + 1):
                if idx < len(items):
                    stage_s(idx)
                if hooks and idx in hooks:
                    hooks[idx]()
                if idx >= 1:
                    stage_pv(idx - 1)

        def set_va_ones(nb_, halo_list):
            S.op("pool", lambda e: e.memset(v16(B_VA + 2 * 64, [[192, nb_], [1, 64]]), 1.0), writes=[R_va])
            for gb, hm in halo_list:
                S.op("pool", lambda e, gb=gb, hm=hm: e.tensor_copy(
                    out=v16(B_VA + 2 * (gb * 192 + 64), [[1, 64]]),
                    in_=v32(FLAG + 4 * hm, [[0, 64]])), reads=[R_const, R_va], writes=[R_va])

        def finalize_a(ai, sink_cols=None):
            ab = ACCB[ai]
            R_acc = R_accs[ai]
            S.op("sp", lambda e: e.dma_start(out=dscr_d[0:1, :], in_=v32(ab, [[1, 2048]], 64, 1)),
                 reads=[R_acc], writes=[R_dscr], chan=ch_rs)
            S.op("sp", lambda e: e.dma_start(out=dscr_d[1:2, :], in_=v32(ab + 8192, [[1, 2048]], 0, 1)),
                 reads=[R_acc], writes=[R_dscr], chan=ch_rs)
            S.op("sp", lambda e: e.dma_start(out=v32(RDS, [[1, 32]], 0, 64),
                                             in_=bass.AP(dscr_d.tensor, 0, [[32, 64], [1, 32]])),
                 reads=[R_dscr], writes=[R_rds], chan=ch_rs)
            S.op("sp", lambda e: e.dma_start(out=v32(RDS, [[1, 32]], 64, 64),
                                             in_=bass.AP(dscr_d.tensor, 2048, [[32, 64], [1, 32]])),
                 reads=[R_dscr], writes=[R_rds], chan=ch_rs)

        def finalize_b(ai, dst_chunk, sink_cols=None):
            ab = ACCB[ai]
            R_acc = R_accs[ai]
            if sink_cols is not None:
                for half, col in enumerate(sink_cols):
                    S.op("dve", lambda e, half=half, col=col: e.tensor_scalar(
                        out=v32(RDS, [[1, 32]], 64 * half, 64), in0=v32(RDS, [[1, 32]], 64 * half, 64),
                        scalar1=v32(ESINK + 4 * col, [[1, 1]], 64 * half, 64), scalar2=None, op0=ALU.add),
                        reads=[R_rds, R_const], writes=[R_rds])
            S.op("dve", lambda e: e.reciprocal(out=v32(RDS, [[1, 32]]), in_=v32(RDS, [[1, 32]])),
                 reads=[R_rds], writes=[R_rds])
            S.op("sp", lambda e: e.dma_start(out=bass.AP(rscr_d.tensor, 0, [[32, 128], [1, 32]]),
                                             in_=v32(RDS, [[1, 32]])),
                 reads=[R_rds], writes=[R_rscr], chan=ch_rs)
            S.op("sp", lambda e: e.dma_start(out=v32(B_RD, [[1, 2048]], 0, 64),
                                             in_=bass.AP(rscr_d.tensor, 0, [[0, 64], [1, 2048]])),
                 reads=[R_rscr], writes=[R_rs], chan=ch_rs)
            S.op("sp", lambda e: e.dma_start(out=v32(B_RD, [[1, 2048]], 64, 64),
                                             in_=bass.AP(rscr_d.tensor, 2048, [[0, 64], [1, 2048]])),
                 reads=[R_rscr], writes=[R_rs], chan=ch_rs)

        def finalize_c(ai, dst_chunk):
            ab = ACCB[ai]
            R_acc = R_accs[ai]
            S.op("dve", lambda e: e.tensor_tensor(
                out=v16(OAB_OFF + 2 * (dst_chunk * 2048), [[1, 2048]], 0, 64),
                in0=v32(ab, [[1, 2048]], 0, 64), in1=v32(B_RD, [[1, 2048]], 0, 64), op=ALU.mult),
                reads=[R_acc, R_rs], writes=[R_oab])
            S.op("dve", lambda e: e.tensor_tensor(
                out=v16(OAB_OFF + 2 * (dst_chunk * 2048), [[1, 2048]], 64, 64),
                in0=v32(ab + 8192, [[1, 2048]], 64, 64), in1=v32(B_RD, [[1, 2048]], 64, 64), op=ALU.mult),
                reads=[R_acc, R_rs], writes=[R_oab])

        pend = {"f": None}

        def fin_start(ai, dst_chunk, sink_cols=None):
            finalize_a(ai, sink_cols)
            pend["f"] = (ai, dst_chunk, sink_cols, 0)

        def fin_step():
            f = pend["f"]
            if f is None:
                return
            ai, dst_chunk, sink_cols, stage = f
            if stage == 0:
                finalize_b(ai, dst_chunk, sink_cols)
                pend["f"] = (ai, dst_chunk, sink_cols, 1)
            else:
                finalize_c(ai, dst_chunk)
                pend["f"] = None

        def fin_flush():
            while pend["f"] is not None:
                fin_step()

        def project_kv(s, wk, kcol, vcol, d, n, klo, khi, nk, kblocks):
            nbr = len(kblocks)
            ranges = []
            if left:
                ranges.append((klo * d, 0))
            ranges += [(o, o + 512) for o in range(0, 2048, 512)]
            if right:
                ranges.append((2048, 2048 + (khi - n) * d))
            tiles = []
            for (a, b_) in ranges:
                o = a
                while o < b_:
                    nn = min(512, b_ - o)
                    tiles.append((o, nn))
                    o += nn
            for (o, nn) in tiles:
                def evac(bk, o=o, nn=nn):
                    S.op("act", lambda e: e.activation(
                        out=v16(B_KT + 2 * (o // d - klo), [[1, nn // d], [nk, d]]),
                        in_=pv32(bk * 512, [[d, nn // d], [1, d]]), func=AF.Copy),
                        reads=[banks[bk]], writes=[R_kt])
                proj_fm(s, kcol, wk, evac, ownc + o, nn)
            for r in range(d):
                for bi, (st, ln) in enumerate(kblocks):
                    gb = r * nbr + bi
                    bk = next_bank()
                    col = ownc + st * d + r
                    mm_chain(S, cx, [(pv32(bk * 512, [[1, 128]], 0, ln), xn_ap(k, col, ln, d),
                                      v16(B_WQ + s * 6144 + 2 * (k * wk + vcol), [[1, 128]])) for k in range(8)],
                             [R_wq[s], R_xn], [banks[bk]])
                    hm = None
                    if st < 0:
                        hm = 1 if ln == 128 and st == -64 else 0
                    elif st >= n:
                        hm = 0
                    if hm is None:
                        S.op("act", lambda e, gb=gb, bk=bk, ln=ln: e.activation(
                            out=v16(B_VA + 2 * (gb * 192), [[128, 2], [1, 64]], 0, ln),
                            in_=pv32(bk * 512, [[64, 2], [1, 64]], 0, ln), func=AF.Copy),
                            reads=[banks[bk], R_va], writes=[R_va])
                    else:
                        S.op("dve", lambda e, gb=gb, bk=bk, ln=ln, hm=hm: e.tensor_scalar(
                            out=v16(B_VA + 2 * (gb * 192), [[128, 2], [1, 64]], 0, ln),
                            in0=pv32(bk * 512, [[64, 2], [1, 64]], 0, ln),
                            scalar1=v32(FLAG + 4 * hm, [[1, 1]], 0, ln), scalar2=None, op0=ALU.mult),
                            reads=[banks[bk], R_const, R_va], writes=[R_va])

        def take_slab():
            conv_step(3)
            cur = slab_state["slot"]
            slab_state["i"] += 1
            if slab_state["i"] < len(slab_specs):
                slab_state["slot"] = load_wslab(*slab_specs[slab_state["i"]])
            return cur

        for hp in range(2):
            acc_cur["i"] = hp % 2
            S.op("pool", lambda e, ab=ACCB[hp % 2]: e.memset(v32(ab, [[1, 4096]]), 0.0), writes=[R_accs[hp % 2]])
            for g in range(3):
                d = DILS[g]
                n, klo, khi, nk, kblocks, qblocks = geom(d, left, right)
                nbr = len(kblocks)
                ph = g * 2 + hp
                s = take_slab()
                halo_list = []
                for r in range(d):
                    for bi, (st, ln) in enumerate(kblocks):
                        if st < 0:
                            halo_list.append((r * nbr + bi, 1))
                        elif st >= n:
                            halo_list.append((r * nbr + bi, 0))
                allblocks = [(st, ln) for r in range(d) for (st, ln) in kblocks]
                set_va_ones(len(allblocks), halo_list)
                chk('va1_%d_%d_%d' % (ui, hp, g))
                for o in range(0, 2048, 512):
                    def evq(bk, o=o, d=d, n=n):
                        S.op("act", lambda e: e.activation(
                            out=v16(B_QT + 2 * (o // d), [[1, 512 // d], [n, d]]),
                            in_=pv32(bk * 512, [[d, 512 // d], [1, d]]), func=AF.Copy),
                            reads=[banks[bk]], writes=[R_qt])
                    proj_fm(s, 0, 384, evq, ownc + o, 512)
                chk('qproj_%d_%d_%d' % (ui, hp, g))
                fin_step()
                project_kv(s, 384, 128, 256, d, n, klo, khi, nk, kblocks)
                fin_step()
                chk('kvproj_%d_%d_%d' % (ui, hp, g))
                rq = []
                for r in range(d):
                    rq.append((r, [(qs, v0, v1, [(side, r * nbr + bi) for (side, bi) in kbs])
                                   for (qs, v0, v1, kbs) in qblocks]))
                attn(rq, allblocks, klo, nk, n, d, 2, [B_EA + ph * 2048], [0, 1])
                chk('attng_%d_%d_%d' % (ui, hp, g))
            fin_flush()
            fin_start(hp % 2, hp)
            chk('attnA%d_%d' % (hp, ui))

        n, klo, khi, nk, kblocks, qblocks = geom_b(left, right)
        s = take_slab()
        set_va_ones(len(kblocks), [(bi, 0) for bi, (st, ln) in enumerate(kblocks) if st < 0 or st >= n])
        fin_step()
        project_kv(s, 256, 0, 128, 1, n, klo, khi, nk, kblocks)
        fin_step()
        for ci in range(4):
            acc_cur["i"] = ci % 2
            S.op("pool", lambda e, ab=ACCB[ci % 2]: e.memset(v32(ab, [[1, 4096]]), 0.0), writes=[R_accs[ci % 2]])
            s = take_slab()
            for o in range(0, 2048, 512):
                def evq(bk, o=o):
                    S.op("act", lambda e: e.activation(out=v16(B_QT + 2 * o, [[1, 512]]),
                                                       in_=pv32(bk * 512, [[1, 512]]), func=AF.Copy),
                         reads=[banks[bk]], writes=[R_qt])
                proj_fm(s, 0, 128, evq, ownc + o, 512)
            fin_step()
            attn([(0, qblocks)], kblocks, klo, nk, n, 1, 3,
                 [B_EB + ci * 1536, B_EB + (4 + ci) * 1536], [0, 1], hooks={8: fin_step})
            fin_flush()
            fin_start(ci % 2, 2 + ci, (ci, 4 + ci))
        fin_flush()
        conv_step(40)

        chk('attn_%d' % ui)
        S.barrier()
        R_xr = [Res("xr0"), Res("xr1")]
        ch_xr = [S.chan(), S.chan()]
        R_wm = [Res("wm0"), Res("wm1")]
        ch_wm = [S.chan(), S.chan()]
        R_wup = [Res("wup0"), Res("wup1")]
        ch_wup = [S.chan(), S.chan()]
        R_wdn = [Res("wdn0"), Res("wdn1")]
        ch_wdn = [S.chan(), S.chan()]
        R_mg = Res("mg")
        R_xn1 = [Res("xn1_%d" % i) for i in range(8)]
        R_ht = [Res("ht%d" % i) for i in range(NCP)]
        R_ucg = Res("ucg")
        R_ucv = Res("ucv")
        R_ucg2 = Res("ucg2")
        R_ucv2 = Res("ucv2")
        R_tmp = [Res("tmp%d" % i) for i in range(6)]
        R_sq = [Res("sq0"), Res("sq1")]
        R_rb = [Res("rb0"), Res("rb1")]
        wmi = [0]
        outs_i = [0]
        wupi = [0]
        wdni = [0]
        xri = [0]

        class WStream:
            def __init__(self, items, slots):
                self.items = items
                self.slots = slots
                self.i = 0
                self._load(0)

            def _load(self, i):
                if i >= len(self.items):
                    return
                src, ncols, R_src = self.items[i]
                boff, R_w, ch_w = self.slots[i % len(self.slots)]
                S.op("pool", lambda e: e.dma_start(out=v16(boff, [[1, ncols]]), in_=src), reads=[R_src],
                     writes=[R_w], chan=ch_w)

            def take(self):
                i = self.i
                self.i += 1
                self._load(i + 1)
                boff, R_w, ch_w = self.slots[i % len(self.slots)]
                return boff, R_w

        wm_items = []
        for tt_ in range(4):
            for m_ in range(8):
                wm_items.append((dv(WB["wm"], m_ * 2816, [[8 * 2816, 128], [1, 2816]]), 2816, R_cv["wm"]))
            for m_ in range(0, 8, 2):
                wm_items.append((dv(WB["wo"], m_ * 1024, [[8192, 128], [1, 2048]]), 2048, R_cv["wo"]))
        wdn_items = [(dv(WB["wdn"], m_ * 2816, [[8 * 2816, 128], [1, 2816]]), 2816, R_cv["wdn"])
                     for _ in range(5 if ui == NUNIT - 1 else 4) for m_ in range(8)]
        wm_stream = WStream(wm_items, [(C_WM, R_wm[0], ch_wm[0]), (C_WM + 5632, R_wm[1], ch_wm[1])])
        wdn_stream = WStream(wdn_items, [(C_WDN, R_wdn[0], ch_wdn[0]), (C_WDN + 5632, R_wdn[1], ch_wdn[1])])

        def T(i):
            return C_TMP + i * 2048

        MT = [(C_OUTS, [R_outsA]), (C_OUTS + 2048, [R_outsB]), (C_SQ, [R_sq[0], R_sq[1]]), (C_SQ + 2048, [R_rb[0]])]

        def x1t(m, j0, nn, p0=0, np_=128):
            return v32(C_X1T + 4 * (m * 516 + j0), [[1, nn]], p0, np_)

        def rms_bc(j0, nn, rbi):
            bk = next_bank()
            for m in range(8):
                sq = m % 2
                if m % 2 == 0:
                    S.op("act", lambda e, m=m, sq=sq: e.activation(out=v16(C_SQ + sq * 1024, [[1, nn]]),
                                                                   in_=x1t(m, j0, nn), func=AF.Square),
                         reads=[R_x1t[m]], writes=[R_sq[sq]])
                else:
                    S.op("dve", lambda e, m=m, sq=sq: e.tensor_tensor(out=v16(C_SQ + sq * 1024, [[1, nn]]),
                                                                      in0=x1t(m, j0, nn), in1=x1t(m, j0, nn),
                                                                      op=ALU.mult),
                         reads=[R_x1t[m]], writes=[R_sq[sq]])
                S.op("pe", lambda e, m=m, sq=sq, bk=bk: e.matmul(pv32(bk * 512, [[1, nn]]), v16(ONESB, [[1, 128]]),
                                                              v16(C_SQ + sq * 1024, [[1, nn]]), start=(m == 0),
                                                              stop=(m == 7)),
                     reads=[R_sq[sq], R_const], writes=[banks[bk]], signal=True)
            rb = C_RB + rbi * 2048
            S.op("act", lambda e, bk=bk: e.activation(out=v32(rb, [[1, nn]]), in_=pv32(bk * 512, [[1, nn]]),
                                                      func=AF.Ln, bias=v32(EPSC, [[1, 1]]), scale=1.0 / D),
                 reads=[banks[bk], R_const], writes=[R_rb[rbi]])
            S.op("act", lambda e: e.activation(out=v32(rb, [[1, nn]]), in_=v32(rb, [[1, nn]]), func=AF.Exp,
                                               scale=-0.5),
                 reads=[R_rb[rbi]], writes=[R_rb[rbi]])

        def down_part(j0, nn, rhs_fn):
            for m in range(8):
                wb, R_w = wdn_stream.take()
                bk = next_bank()
                mm_chain(S, cx, [(pv32(bk * 512, [[1, nn]]), v16(wb + 2 * (k * 128), [[1, 128]]),
                                  rhs_fn(k)) for k in range(NCP)], [R_w], [banks[bk]],
                         step_reads=[[R_ht[k]] for k in range(NCP)])
                S.op("dve", lambda e, m=m, bk=bk: e.scalar_tensor_tensor(
                    out=x1t(m, j0, nn), in0=pv32(bk * 512, [[1, nn]]), scalar=0.5, in1=x1t(m, j0, nn),
                    op0=ALU.mult, op1=ALU.add), reads=[banks[bk], R_x1t[m]], writes=[R_x1t[m]])

        def final_part(j0, nn, tok0):
            R_rt = Res("rt")
            blocks = []
            st = 0
            while st < nn:
                ln = min(128, nn - st)
                blocks.append((st, ln))
                st += ln

            def do_t(bi):
                st, ln = blocks[bi]
                bp = next_bank_pair()
                mm_chain(S, cx, [(pv32(bp * 512 + m * 128, [[1, 128]], 0, ln), x1t(m, j0 + st, ln),
                                  v32(IDF, [[1, 128]])) for m in range(8)], R_x1t + [R_const],
                         [banks[bp], banks[bp + 1]], transpose=True)
                return bp

            def do_e(bi, bp):
                st, ln = blocks[bi]
                osl = outs_i[0] % 2
                outs_i[0] += 1
                ob = C_OUTS if osl == 0 else C_SQ
                ores = [R_outsA, R_outsB] if osl == 0 else [R_sq[0], R_sq[1], R_rb[0]]
                S.op("dve", lambda e: e.scalar_tensor_tensor(
                    out=v32(ob, [[1, 1024]], 0, ln), in0=pv32(bp * 512, [[1, 1024]], 0, ln),
                    scalar=v32(RT + 4 * bi, [[1, 1]], 0, ln), in1=v32(GFINB, [[1, 1024]], 0, ln),
                    op0=ALU.mult, op1=ALU.mult),
                    reads=[banks[bp], banks[bp + 1], R_rt, R_const], writes=ores)
                S.op("sp", lambda e, t0=tok0 + st: e.dma_start(out=yout[t0:t0 + ln, :],
                                                               in_=v32(ob, [[1, 1024]], 0, ln)),
                     reads=ores, writes=[], chan=ch_out)

            nb = len(blocks)
            pre = [do_t(bi) for bi in range(min(3, nb))]
            rms_bc(j0, nn, 1)
            bk = next_bank()
            for bi, (st, ln) in enumerate(blocks):
                mm_chain(S, cx, [(pv32(bk * 512 + bi * 128, [[1, 128]], 0, ln), v32(C_RB + 2048 + 4 * st, [[1, ln]]),
                                  v32(IDF, [[1, 128]]))], [R_rb[1], R_const], [banks[bk]], transpose=True)
                S.op("act", lambda e, bi=bi, ln=ln, bk=bk: e.activation(
                    out=v32(RT + 4 * bi, [[1, 1]], 0, ln), in_=pv32(bk * 512 + bi * 128, [[1, 1]], 0, ln),
                    func=AF.Copy), reads=[banks[bk], R_rt], writes=[R_rt])
            for bi in range(len(pre)):
                do_e(bi, pre[bi])
            for bi in range(len(pre), nb):
                do_e(bi, do_t(bi))

        def merge_gen(tt):
            ocol = ownc + tt * 512
            for m in range(8):
                wb, R_w = wm_stream.take()
                ba, bb, bga, bgb = next_bank(), next_bank(), next_bank(), next_bank()
                mm_chain(S, cx, [(pv32(ba * 512, [[1, 512]]), v16(wb + 2 * (k * 128), [[1, 128]]),
                                  v16(OAB_OFF + 2 * (k * 2048 + tt * 512), [[1, 512]])) for k in range(2)],
                         [R_w, R_oab], [banks[ba]])
                mm_chain(S, cx, [(pv32(bb * 512, [[1, 512]]), v16(wb + 2 * ((2 + k) * 128), [[1, 128]]),
                                  v16(OAB_OFF + 2 * ((2 + k) * 2048 + tt * 512), [[1, 512]])) for k in range(4)],
                         [R_w, R_oab], [banks[bb]])
                mm_chain(S, cx, [(pv32(bga * 512, [[1, 512]]), v16(wb + 2 * ((6 + k) * 128), [[1, 128]]),
                                  xn_ap(k, ocol, 512)) for k in range(8)], [R_w, R_xn], [banks[bga]])
                mm_chain(S, cx, [(pv32(bgb * 512, [[1, 512]]), v16(wb + 2 * ((14 + k) * 128), [[1, 128]]),
                                  xn_ap(k, ocol, 512)) for k in range(8)], [R_w, R_xn], [banks[bgb]])
                (ta_b, ta_r), (tb_b, tb_r) = MT[2 * (m % 2)], MT[2 * (m % 2) + 1]
                S.op("act", lambda e, m=m, bga=bga, ta_b=ta_b: e.activation(
                    out=v32(ta_b, [[1, 512]]), in_=pv32(bga * 512, [[1, 512]]), func=AF.Tanh,
                    bias=v32(BGH + 4 * m, [[1, 1]]), scale=0.5), reads=[banks[bga], R_const], writes=ta_r)
                S.op("act", lambda e, m=m, bgb=bgb, tb_b=tb_b: e.activation(
                    out=v32(tb_b, [[1, 512]]), in_=pv32(bgb * 512, [[1, 512]]), func=AF.Tanh,
                    bias=v32(BGH + 4 * (8 + m), [[1, 1]]), scale=0.5), reads=[banks[bgb], R_const],
                    writes=tb_r)
                S.op("dve", lambda e, ba=ba, ta_b=ta_b: e.scalar_tensor_tensor(
                    out=v32(ta_b, [[1, 512]]), in0=v32(ta_b, [[1, 512]]), scalar=1.0, in1=pv32(ba * 512, [[1, 512]]),
                    op0=ALU.add, op1=ALU.mult), reads=[banks[ba]] + ta_r, writes=ta_r)
                S.op("dve", lambda e, bb=bb, tb_b=tb_b: e.scalar_tensor_tensor(
                    out=v32(tb_b, [[1, 512]]), in0=v32(tb_b, [[1, 512]]), scalar=1.0, in1=pv32(bb * 512, [[1, 512]]),
                    op0=ALU.add, op1=ALU.mult), reads=[banks[bb]] + tb_r, writes=tb_r)
                S.op("dve", lambda e, m=m, ta_b=ta_b, tb_b=tb_b: e.tensor_tensor(
                    out=v16(C_MG + 2 * (m * 512), [[1, 512]]), in0=v32(ta_b, [[1, 512]]), in1=v32(tb_b, [[1, 512]]),
                    op=ALU.add), reads=ta_r + tb_r, writes=[R_mg])
                yield
            chk('merge_%d_%d' % (ui, tt))

        xslots = [(C_XR, R_xr[0], ch_xr[0]), (C_XR + 4096, R_xr[1], ch_xr[1]),
                  (C_WUP, R_wup[0], ch_wup[0]), (C_WUP + 4096, R_wup[1], ch_wup[1])]

        def load_x_tile(tt):
            c0_ = own0 + tt * 512
            for tb in range(4):
                xb_, R_x, ch_x = xslots[tb]
                S.op("sp", lambda e, xb_=xb_, t0=c0_ + tb * 128: e.dma_start(
                    out=v32(xb_, [[1, 1024]]), in_=xin[t0:t0 + 128, :]),
                    writes=[R_x], chan=ch_x)

        def mid_phase(tt):
            gt = ui * 4 + tt
            c0 = own0 + tt * 512
            S.op("dve", lambda e: e.tensor_copy(out=v32(C_X1T, [[516, 8]]), in_=v32(X1C, [[1, 8]])),
                 reads=[R_x1c] + R_x1t, writes=R_x1t)
            for tb in range(4):
                xb_, R_x, ch_x = xslots[tb]
                bp = next_bank_pair()
                mm_chain(S, cx, [(pv32(bp * 512 + m * 128, [[1, 128]]), v32(xb_ + 4 * (m * 128), [[1, 128]]),
                                  v32(IDF, [[1, 128]])) for m in range(8)], [R_x, R_const],
                         [banks[bp], banks[bp + 1]], transpose=True)
                S.op("act", lambda e, bp=bp, tb=tb: e.activation(
                    out=v32(C_X1T + 4 * (1 + tb * 128), [[516, 8], [1, 128]]),
                    in_=pv32(bp * 512, [[128, 8], [1, 128]]), func=AF.Copy),
                    reads=[banks[bp], banks[bp + 1]] + R_x1t, writes=R_x1t)
            for m in range(8):
                if m % 2 == 0:
                    wb, R_w = wm_stream.take()
                else:
                    wb = wb + 2048
                bk = next_bank()
                mm_chain(S, cx, [(pv32(bk * 512, [[1, 512]]), v16(wb + 2 * (k * 128), [[1, 128]]),
                                  v16(C_MG + 2 * (k * 512), [[1, 512]])) for k in range(8)], [R_w, R_mg],
                         [banks[bk]])
                S.op("dve", lambda e, m=m, bk=bk: e.scalar_tensor_tensor(
                    out=x1t(m, 1, 512), in0=pv32(bk * 512, [[1, 512]]), scalar=0.5, in1=x1t(m, 1, 512),
                    op0=ALU.mult, op1=ALU.add), reads=[banks[bk], R_x1t[m]], writes=[R_x1t[m]])
            chk('y_%d_%d' % (ui, tt))
            rms_bc(1, 512, 0)
            for m in range(8):
                S.op("dve", lambda e, m=m: e.scalar_tensor_tensor(
                    out=v16(C_XN1 + 2 * (m * 512), [[1, 512]]), in0=x1t(m, 1, 512), scalar=sm(SM_GF + m),
                    in1=v32(C_RB, [[1, 512]]), op0=ALU.mult, op1=ALU.mult),
                    reads=[R_x1t[m], R_rb[0], R_const], writes=[R_xn1[m]])
            chk('xn1_%d_%d' % (ui, tt))
            wup_slots = [(C_WUP, R_wup[0], ch_wup[0]), (C_WUP + 4096, R_wup[1], ch_wup[1]),
                         (C_XR, R_xr[0], ch_xr[0]), (C_XR + 4096, R_xr[1], ch_xr[1])]

            def load_wup(cp):
                wb, R_w, ch_w = wup_slots[cp % 4]
                S.op("pool", lambda e, cp=cp, wb=wb: e.dma_start(
                    out=v16(wb, [[1, 2048]]), in_=dv(WB["wup"], cp * 2048, [[NCP * 2048, 128], [1, 2048]])),
                    reads=[R_cv["wup"]], writes=[R_w], chan=ch_w)

            def chain_cp(cp, st_):
                wb, R_w, ch_w = wup_slots[cp % 4]
                UCG_, UCV_ = (C_UCG, C_UCV) if st_ == 0 else (C_UCG2, C_UCV2)
                R_g, R_v = (R_ucg, R_ucv) if st_ == 0 else (R_ucg2, R_ucv2)
                iG, iV, iX = 3 * st_, 3 * st_ + 1, 3 * st_ + 2
                if cp + 3 < NCP:
                    load_wup(cp + 3)
                bg, bv = next_bank(), next_bank()
                mm_chain(S, cx, [(pv32(bg * 512, [[1, 512]]), v16(wb + 2 * (k * 256), [[1, 128]]),
                                  v16(C_XN1 + 2 * (k * 512), [[1, 512]])) for k in range(8)], [R_w],
                         [banks[bg]], step_reads=[[R_xn1[k]] for k in range(8)])
                mm_chain(S, cx, [(pv32(bv * 512, [[1, 512]]), v16(wb + 2 * (k * 256 + 128), [[1, 128]]),
                                  v16(C_XN1 + 2 * (k * 512), [[1, 512]])) for k in range(8)], [R_w],
                         [banks[bv]], step_reads=[[R_xn1[k]] for k in range(8)])
                yield
                for (UC, R_uc, bk, ch_i, ti, ceng) in ((UCG_, R_g, bg, cp, iG, "dve"), (UCV_, R_v, bv, NCP + cp, iV, "dve")):
                    S.op("act", lambda e, UC=UC, ch_i=ch_i: e.activation(
                        out=v32(UC, [[1, 2]]), in_=v32(SAVE + 8 * ch_i, [[1, 2]]), func=AF.Copy),
                        reads=[R_save, R_uc], writes=[R_uc])
                    S.op("act", lambda e, UC=UC, bk=bk: e.activation(
                        out=v32(UC + 8, [[1, 512]]), in_=pv32(bk * 512, [[1, 512]]), func=AF.Copy),
                        reads=[banks[bk], R_uc], writes=[R_uc])
                    yield
                    S.op("act", lambda e, UC=UC, ch_i=ch_i: e.activation(
                        out=v32(SAVE + 8 * ch_i, [[1, 2]]), in_=v32(UC + 4 * 512, [[1, 2]]), func=AF.Copy),
                        reads=[R_uc, R_save], writes=[R_save])
                    S.op("act", lambda e, UC=UC, ch_i=ch_i, ti=ti: e.activation(
                        out=v32(T(ti), [[1, 512]]), in_=v32(UC + 4, [[1, 512]]), func=AF.Identity,
                        bias=sm(SM_CB + ch_i), scale=sm(SM_CW + 44 + ch_i)),
                        reads=[R_uc, R_const], writes=[R_tmp[ti]])
                    yield
                    S.op(ceng, lambda e, UC=UC, ch_i=ch_i, ti=ti: e.scalar_tensor_tensor(
                        out=v32(T(ti), [[1, 512]]), in0=v32(UC, [[1, 512]]), scalar=sm(SM_CW + ch_i),
                        in1=v32(T(ti), [[1, 512]]), op0=ALU.mult, op1=ALU.add),
                        reads=[R_uc, R_const, R_tmp[ti]], writes=[R_tmp[ti]])
                    yield
                    S.op(ceng, lambda e, UC=UC, ch_i=ch_i, ti=ti: e.scalar_tensor_tensor(
                        out=v32(T(ti), [[1, 512]]), in0=v32(UC + 8, [[1, 512]]), scalar=sm(SM_CW + 88 + ch_i),
                        in1=v32(T(ti), [[1, 512]]), op0=ALU.mult, op1=ALU.add),
                        reads=[R_uc, R_const, R_tmp[ti]], writes=[R_tmp[ti]])
                    if tt == 0 and ui > 0:
                        w2x = W2NM if ui == 1 else W2N1
                        w0x = W0NM if ui == 1 else W0N1
                        S.op(ceng, lambda e, UC=UC, ch_i=ch_i, ti=ti, w2x=w2x: e.scalar_tensor_tensor(
                            out=v32(T(ti), [[1, 1]]), in0=v32(UC + 8, [[1, 1]]), scalar=v32(w2x + 4 * ch_i, [[1, 1]]),
                            in1=v32(T(ti), [[1, 1]]), op0=ALU.mult, op1=ALU.add),
                            reads=[R_uc, R_const, R_tmp[ti]], writes=[R_tmp[ti]])
                        S.op(ceng, lambda e, UC=UC, ch_i=ch_i, ti=ti, w0x=w0x: e.scalar_tensor_tensor(
                            out=v32(T(ti) + 4, [[1, 1]]), in0=v32(UC + 4, [[1, 1]]),
                            scalar=v32(w0x + 4 * ch_i, [[1, 1]]), in1=v32(T(ti) + 4, [[1, 1]]),
                            op0=ALU.mult, op1=ALU.add),
                            reads=[R_uc, R_const, R_tmp[ti]], writes=[R_tmp[ti]])
                    yield
                S.op("act", lambda e: e.activation(out=v32(T(iX), [[1, 512]]), in_=v32(T(iG), [[1, 512]]),
                                                   func=AF.Square, scale=math.sqrt(0.044715)),
                     reads=[R_tmp[iG]], writes=[R_tmp[iX]])
                yield
                S.op("dve", lambda e: e.scalar_tensor_tensor(
                    out=v32(T(iX), [[1, 512]]), in0=v32(T(iX), [[1, 512]]), scalar=1.0, in1=v32(T(iG), [[1, 512]]),
                    op0=ALU.add, op1=ALU.mult), reads=[R_tmp[iX], R_tmp[iG]], writes=[R_tmp[iX]])
                yield
                S.op("act", lambda e: e.activation(out=v32(T(iX), [[1, 512]]), in_=v32(T(iX), [[1, 512]]),
                                                   func=AF.Tanh, scale=GELU_K), reads=[R_tmp[iX]], writes=[R_tmp[iX]])
                yield
                S.op("dve", lambda e: e.scalar_tensor_tensor(
                    out=v32(T(iX), [[1, 512]]), in0=v32(T(iX), [[1, 512]]), scalar=1.0, in1=v32(T(iG), [[1, 512]]),
                    op0=ALU.add, op1=ALU.mult), reads=[R_tmp[iX], R_tmp[iG]], writes=[R_tmp[iX]])
                yield
                S.op("dve", lambda e, cp=cp: e.tensor_tensor(
                    out=v16(C_HT + 2 * (cp * 512), [[1, 512]]), in0=v32(T(iX), [[1, 512]]), in1=v32(T(iV), [[1, 512]]),
                    op=ALU.mult), reads=[R_tmp[iX], R_tmp[iV]], writes=[R_ht[cp]])

            load_wup(0)
            load_wup(1)
            load_wup(2)
            mg = merge_gen(tt + 1) if tt + 1 < 4 else None
            for cp0 in range(0, NCP, 2):
                if mg is not None and cp0 >= 10:
                    next(mg, None)
                gens = [chain_cp(cp0, 0), chain_cp(cp0 + 1, 1)]
                alive = [True, True]
                step = 0
                while any(alive):
                    for gi_ in range(2):
                        if not alive[gi_]:
                            continue
                        if gi_ == 1 and step < 0:
                            continue
                        try:
                            next(gens[gi_])
                        except StopIteration:
                            alive[gi_] = False
                    step += 1
            if mg is not None:
                for _ in mg:
                    pass
            chk('up_%d_%d' % (ui, tt))
            j0 = 1 if gt == 0 else 0
            down_part(j0, 512 - j0, lambda k, j0=j0: v16(C_HT + 2 * (k * 512 + j0), [[1, 512 - j0]]))
            S.op("dve", lambda e: e.tensor_copy(out=v32(X1C, [[1, 8]]),
                                                in_=v32(C_X1T + 4 * 512, [[516, 8]])),
                 reads=R_x1t + [R_x1c], writes=[R_x1c])
            return j0, c0

        load_x_tile(0)
        for _ in merge_gen(0):
            pass
        for tt in range(4):
            j0, c0 = mid_phase(tt)
            if tt + 1 < 4:
                load_x_tile(tt + 1)
            final_part(j0, 512 - j0, c0 - 1 + j0)
            chk('down_%d_%d' % (ui, tt))

        if ui == NUNIT - 1:
            cv, t1 = FL_CV, FL_CV + 176
            R_fl = Res("flush")
            S.op("dve", lambda e: e.tensor_copy(out=v32(C_X1T, [[516, 8]]), in_=v32(X1C, [[1, 8]])),
                 reads=[R_x1c] + R_x1t, writes=R_x1t)
            S.op("dve", lambda e: e.tensor_tensor(out=v32(cv, [[1, 44]]), in0=v32(SAVE, [[2, 44]]),
                                                  in1=sm(SM_CW, 44), op=ALU.mult),
                 reads=[R_save, R_const], writes=[R_fl])
            S.op("dve", lambda e: e.tensor_tensor(out=v32(t1, [[1, 44]]), in0=v32(SAVE + 4, [[2, 44]]),
                                                  in1=sm(SM_CW + 44, 44), op=ALU.mult),
                 reads=[R_save, R_const, R_fl], writes=[R_fl])
            S.op("dve", lambda e: e.tensor_tensor(out=v32(cv, [[1, 44]]), in0=v32(cv, [[1, 44]]),
                                                  in1=v32(t1, [[1, 44]]), op=ALU.add), reads=[R_fl], writes=[R_fl])
            S.op("dve", lambda e: e.tensor_tensor(out=v32(cv, [[1, 44]]), in0=v32(cv, [[1, 44]]),
                                                  in1=sm(SM_CB, 44), op=ALU.add), reads=[R_fl, R_const],
                 writes=[R_fl])
            S.op("dve", lambda e: e.tensor_tensor(out=v32(t1, [[1, 22]]), in0=v32(cv, [[1, 22]]),
                                                  in1=v32(cv, [[1, 22]]), op=ALU.mult), reads=[R_fl], writes=[R_fl])
            S.op("dve", lambda e: e.tensor_scalar(out=v32(t1, [[1, 22]]), in0=v32(t1, [[1, 22]]), scalar1=0.044715,
                                                  scalar2=1.0, op0=ALU.mult, op1=ALU.add), reads=[R_fl],
                 writes=[R_fl])
            S.op("dve", lambda e: e.tensor_tensor(out=v32(t1, [[1, 22]]), in0=v32(t1, [[1, 22]]),
                                                  in1=v32(cv, [[1, 22]]), op=ALU.mult), reads=[R_fl], writes=[R_fl])
            S.op("act", lambda e: e.activation(out=v32(t1, [[1, 22]]), in_=v32(t1, [[1, 22]]), func=AF.Tanh,
                                               scale=GELU_K), reads=[R_fl], writes=[R_fl])
            S.op("dve", lambda e: e.scalar_tensor_tensor(out=v32(t1, [[1, 22]]), in0=v32(t1, [[1, 22]]), scalar=1.0,
                                                         in1=v32(cv, [[1, 22]]), op0=ALU.add, op1=ALU.mult),
                 reads=[R_fl], writes=[R_fl])
            S.op("dve", lambda e: e.tensor_tensor(out=v16(FL_H, [[1, 22]]), in0=v32(t1, [[1, 22]]),
                                                  in1=v32(cv + 88, [[1, 22]]), op=ALU.mult), reads=[R_fl],
                 writes=R_ht)
            down_part(0, 1, lambda k: v16(FL_H + 2 * k, [[1, 1]]))
            final_part(0, 1, TOK - 1)

    S.stopped = False
    S.barrier()

    with nc.Block() as block:
        @block.sync
        def _(e):
            S.emit("sp", e)

        @block.scalar
        def _(e):
            S.emit("act", e)

        @block.vector
        def _(e):
            S.emit("dve", e)

        @block.gpsimd
        def _(e):
            S.emit("pool", e)

        @block.tensor
        def _(e):
            S.emit("pe", e)
    stack.close()
    return nc


def _kp(w):
    K = w.shape[0] // 128
    return np.ascontiguousarray(w.reshape(K, 128, w.shape[1]).transpose(1, 0, 2))


_PROG = None


def kernel(x_prompt, x_sample, g_attn, w_in, b_gate, rel_bias, sink, w_a_out, w_b_out, w_o,
           g_ffn, w_up, conv_w, conv_b, w_down, g_final):
    global _PROG
    f = np.float32
    x_prompt = np.asarray(x_prompt, f)
    x_sample = np.asarray(x_sample, f)
    w_in = np.asarray(w_in, f)[0]
    w_a_out = np.asarray(w_a_out, f)[0]
    w_b_out = np.asarray(w_b_out, f)[0]
    w_o = np.asarray(w_o, f)[0]
    w_up = np.asarray(w_up, f)[0]
    w_down = np.asarray(w_down, f)[0]
    conv_w = np.asarray(conv_w, f)[0]
    conv_b = np.asarray(conv_b, f)[0]
    g_attn = np.asarray(g_attn, f)[0]
    g_ffn = np.asarray(g_ffn, f)[0]
    b_gate = np.asarray(b_gate, f)[0]
    sink = np.asarray(sink, f)[0]
    g_final = np.asarray(g_final, f)
    rel_bias = np.asarray(rel_bias, f)

    winp = _kp(w_in)
    wqkv = np.zeros((128, 6, 8, 384), f)
    for g in range(3):
        for hp in range(2):
            c = (4 * g + 2 * hp) * 64
            ph = g * 2 + hp
            wqkv[:, ph, :, 0:128] = winp[:, :, c:c + 128]
            wqkv[:, ph, :, 128:256] = winp[:, :, 768 + c:768 + c + 128]
            wqkv[:, ph, :, 256:384] = winp[:, :, 1536 + c:1536 + c + 128]
    QB0 = 2304
    KB0 = QB0 + 512
    VB0 = KB0 + 128
    G0 = VB0 + 128
    wbkv = np.zeros((128, 8, 256), f)
    wbkv[:, :, 0:128] = winp[:, :, KB0:KB0 + 128]
    wbkv[:, :, 128:256] = winp[:, :, VB0:VB0 + 128]
    wbq = np.zeros((128, 4, 8, 128), f)
    for ci in range(4):
        wbq[:, ci, :, 0:64] = winp[:, :, QB0 + 64 * ci:QB0 + 64 * ci + 64]
        wbq[:, ci, :, 64:128] = winp[:, :, QB0 + 64 * (4 + ci):QB0 + 64 * (4 + ci) + 64]
    wap = _kp(w_a_out)
    wbo_perm = np.zeros((4, 128, 1024), f)
    for ci in range(4):
        wbo_perm[ci, 0:64] = w_b_out[64 * ci:64 * ci + 64]
        wbo_perm[ci, 64:128] = w_b_out[64 * (4 + ci):64 * (4 + ci) + 64]
    wbop = np.ascontiguousarray(wbo_perm.transpose(1, 0, 2))
    wm = np.zeros((128, 8, 22, 128), f)
    for m in range(8):
        ms = slice(m * 128, (m + 1) * 128)
        wm[:, m, 0:2] = wap[:, :, ms]
        wm[:, m, 2:6] = wbop[:, :, ms]
        wm[:, m, 6:14] = winp[:, :, G0 + m * 128:G0 + (m + 1) * 128]
        wm[:, m, 14:22] = winp[:, :, G0 + 1024 + m * 128:G0 + 1024 + (m + 1) * 128]
    wop = _kp(w_o)
    wo = np.ascontiguousarray(wop.reshape(128, 8, 8, 128).transpose(0, 2, 1, 3))
    wupp = _kp(w_up)
    wup = np.zeros((128, NCP, 8, 256), f)
    for cp in range(NCP):
        wup[:, cp, :, 0:128] = wupp[:, :, cp * 128:(cp + 1) * 128]
        wup[:, cp, :, 128:256] = wupp[:, :, DFF + cp * 128:DFF + (cp + 1) * 128]
    wdnp = _kp(w_down)
    wdn = np.ascontiguousarray(wdnp.reshape(128, NCP, 8, 128).transpose(0, 2, 1, 3))

    small = np.zeros((128, 224), f)
    small[:, 0:8] = g_attn.reshape(8, 128).T
    small[:, 8:16] = g_ffn.reshape(8, 128).T
    small[:, 16:24] = g_final.reshape(8, 128).T
    small[:, 24:40] = b_gate.reshape(16, 128).T
    for k in range(3):
        small[:, 40 + 44 * k:40 + 44 * (k + 1)] = conv_w[k].reshape(44, 128).T
    small[:, 172:216] = conv_b.reshape(44, 128).T
    small[:, 216:224] = sink[None, :]
    oh = _onehot_tables()
    ident = np.eye(128, dtype=f)

    common = dict(wqkv=wqkv.reshape(128, -1), wbkv=wbkv.reshape(128, -1), wbq=wbq.reshape(128, -1),
                  wm=wm.reshape(128, -1), wo=wo.reshape(128, -1), wup=wup.reshape(128, -1),
                  wdn=wdn.reshape(128, -1), small=small, relb=rel_bias, oh=oh, ident=ident,
                  gfinb=np.ascontiguousarray(np.broadcast_to(g_final[None, :], (128, 1024))))
    in_maps = []
    for c in range(NCORES):
        if c < 4:
            xs = np.concatenate([x_prompt[c], x_sample[c]], axis=0)
            fl = 1.0
        else:
            b = 4 + 3 * (c - 4)
            xs = np.concatenate([x_sample[b], x_sample[b + 1], x_sample[b + 2]], axis=0)
            fl = 0.0
        flagv = np.zeros((128, 4), f)
        flagv[:, 0] = fl
        flagv[:, 1] = 1.0
        flagv[:64, 1] = fl
        flagv[:, 2] = fl - 1.0
        m = dict(common)
        m["xin"] = np.ascontiguousarray(xs)
        m["flagv"] = flagv
        in_maps.append(m)

    if _PROG is None:
        _PROG = build_program()
    res = run_bass_kernel_spmd(_PROG, in_maps, core_ids=list(range(NCORES)))
    y_prompt = np.zeros_like(x_prompt)
    y_sample = np.zeros_like(x_sample)
    for c in range(NCORES):
        y = res.results[c]["yout"]
        if c < 4:
            y_prompt[c] = y[:4096]
            y_sample[c] = y[4096:]
        else:
            b = 4 + 3 * (c - 4)
            for i in range(3):
                y_sample[b + i] = y[2048 * i:2048 * (i + 1)]
    return y_prompt, y_sample
```

```python
import math
from contextlib import ExitStack

import numpy as np

import concourse.bass as bass
import concourse.mybir as mybir
from concourse.bass_utils import run_bass_kernel_spmd

F32 = mybir.dt.float32
BF16 = mybir.dt.bfloat16
ALU = mybir.AluOpType
AF = mybir.ActivationFunctionType

NCORES = 8
TOK = 6144
D = 1024
UNIT = 2048
NUNIT = 3
DFF = 2816
NCP = 22
EPS = 1e-6
PAD = 256
GELU_K = 0.7978845608028654
DILS = (1, 4, 16)

ENGS = ("sp", "act", "dve", "pool", "pe")


class Res:
    __slots__ = ("name", "w", "r")

    def __init__(self, name=""):
        self.name = name
        self.w = None
        self.r = {}


class Sched:
    def __init__(self, nc, stack):
        self.nc = nc
        self.stack = stack
        self.q = {e: [] for e in ENGS}
        self.sem = {}
        self.val = {}
        for e in ENGS:
            self.sem[e] = stack.enter_context(nc.semaphore("sem_" + e))
            self.val[e] = 0
        self.waited = {e: {} for e in ENGS}
        self.nchan = 0
        self.stopped = False

    def chan(self):
        sid = "dma%d" % self.nchan
        self.nchan += 1
        self.sem[sid] = self.stack.enter_context(self.nc.semaphore(sid))
        self.val[sid] = 0
        return sid

    def op(self, eng, fn, reads=(), writes=(), extra=(), chan=None, signal=True, self_wait=False):
        if self.stopped:
            return None
        deps = {}

        def need(ev):
            if ev is None:
                return
            if deps.get(ev[0], 0) < ev[1]:
                deps[ev[0]] = ev[1]

        for r in reads:
            need(r.w)
        for w in writes:
            need(w.w)
            for sid, v in w.r.items():
                need((sid, v))
        for e in extra:
            need(e)
        waits = []
        wd = self.waited[eng]
        for sid, v in deps.items():
            if sid == eng and eng == "pe" and not self_wait:
                continue
            if wd.get(sid, 0) < v:
                wd[sid] = v
                waits.append((sid, v))
        if chan is not None:
            self.val[chan] += 16
            ev = (chan, self.val[chan])
            inc = (chan, 16)
        elif signal:
            self.val[eng] += 1
            ev = (eng, self.val[eng])
            inc = (eng, 1)
        else:
            ev = None
            inc = None
        self.q[eng].append((waits, fn, inc))
        if ev is not None:
            for r in reads:
                if r.r.get(ev[0], 0) < ev[1]:
                    r.r[ev[0]] = ev[1]
            for w in writes:
                w.w = ev
                w.r = {}
        return ev

    def barrier(self):
        if self.stopped:
            return
        snap = dict(self.val)
        for e in ENGS:
            waits = []
            for sid, v in snap.items():
                if v == 0 or sid == e:
                    continue
                if self.waited[e].get(sid, 0) < v:
                    self.waited[e][sid] = v
                    waits.append((sid, v))
            if waits:
                self.q[e].append((waits, None, None))

    def emit(self, eng_name, e):
        for waits, fn, inc in self.q[eng_name]:
            for sid, v in waits:
                e.wait_ge(self.sem[sid], v)
            if fn is None:
                continue
            inst = fn(e)
            if inc is not None:
                inst.then_inc(self.sem[inc[0]], inc[1])


def _rel_bucket(rel):
    rel = np.asarray(rel, dtype=np.int64)
    nb = 16
    max_exact = 8
    ret = np.where(rel > 0, nb, 0)
    n = np.abs(rel)
    nf = np.maximum(n, 1).astype(np.float32)
    large = max_exact + (np.log(nf / np.float32(max_exact)) / np.float32(math.log(1024 / max_exact))
                         * np.float32(nb - max_exact)).astype(np.int32)
    large = np.minimum(large, nb - 1)
    return ret + np.where(n < max_exact, n, large)


def _onehot_tables():
    oh = np.zeros((32, 4 * 512), np.float32)
    for t in range(4):
        d = DILS[t] if t < 3 else 1
        R = 64 if t < 3 else 128
        for delta in range(-R, R + 1):
            b = int(_rel_bucket(delta * d))
            oh[b, t * 512 + delta + PAD] = 1.0
    return oh


def geom(d, left, right):
    n = UNIT // d
    klo = -64 if left else 0
    khi = n + (64 if right else 0)
    nk = khi - klo
    kblocks = []
    s = klo
    while s < khi:
        kblocks.append((s, min(128, khi - s)))
        s += 128
    qblocks = []
    i = -1
    while True:
        qs = klo + 64 + 128 * i
        if qs >= n:
            break
        v0, v1 = max(qs, 0), min(qs + 128, n)
        if v1 > v0:
            kbs = []
            for side, bi in ((0, i), (1, i + 1)):
                if 0 <= bi < len(kblocks):
                    kbs.append((side, bi))
            qblocks.append((qs, v0, v1, kbs))
        i += 1
    return n, klo, khi, nk, kblocks, qblocks


def geom_b(left, right):
    n = UNIT
    klo = -128 if left else 0
    khi = n + (128 if right else 0)
    nk = khi - klo
    kblocks = [(s, 128) for s in range(klo, khi, 128)]
    qblocks = []
    for i in range(n // 128):
        qs = 128 * i
        kbs = []
        for m in range(3):
            st = qs - 128 + 128 * m
            if klo <= st < khi:
                kbs.append((m, (st - klo) // 128))
        qblocks.append((qs, qs, qs + 128, kbs))
    return n, klo, khi, nk, kblocks, qblocks


class Ctx:
    pass


def mm_chain(S, cx, steps, reads, wres, transpose=False, step_reads=None):
    n = len(steps)
    ev = None
    for i, (o, a, b) in enumerate(steps):
        first, last = (i == 0), (i == n - 1)
        if transpose:
            fn = (lambda e, o=o, a=a, b=b: e.transpose(o, a, b))
        else:
            fn = (lambda e, o=o, a=a, b=b, first=first, last=last: e.matmul(o, a, b, start=first, stop=last))
        rd = list(reads) + (list(step_reads[i]) if step_reads is not None else [])
        ev = S.op("pe", fn, reads=rd, writes=wres if (first or last) else (), signal=last)
    return ev


STOP = [None]
ATT_CUT = [5]


class _Stop(Exception):
    pass


def build_program():
    nc = bass.Bass("TRN2", target_bir_lowering=False)
    stack = ExitStack()

    def dram_in(name, shape, dt=F32):
        return nc.dram_tensor(name, list(shape), dt, kind="ExternalInput").ap()

    xin = dram_in("xin", [TOK, D])
    yout = nc.dram_tensor("yout", [TOK, D], F32, kind="ExternalOutput").ap()
    flagv_d = dram_in("flagv", [128, 4])
    wqkv_d = dram_in("wqkv", [128, 6 * 3072])
    wbkv_d = dram_in("wbkv", [128, 2048])
    wbq_d = dram_in("wbq", [128, 4 * 1024])
    wm_d = dram_in("wm", [128, 8 * 2816])
    wo_d = dram_in("wo", [128, 8 * 1024])
    wup_d = dram_in("wup", [128, NCP * 2048])
    wdn_d = dram_in("wdn", [128, 8 * 2816])
    small_d = dram_in("small", [128, 224])
    gfinb_d = dram_in("gfinb", [128, 1024])
    relb_d = dram_in("relb", [32, 20])
    oh_d = dram_in("oh", [32, 2048])
    ident_d = dram_in("ident", [128, 128])
    er_d = nc.dram_tensor("er_scr", [4 * 20, 512], F32, kind="Internal").ap()
    esave_d = nc.dram_tensor("esave", [128, 6144], F32, kind="Internal").ap()
    dscr_d = nc.dram_tensor("dscr", [2, 2048], F32, kind="Internal").ap()
    rscr_d = nc.dram_tensor("rscr", [2, 2048], F32, kind="Internal").ap()

    S = Sched(nc, stack)
    cx = Ctx()

    XN_OFF = 0
    OAB_OFF = 49152
    CONST_OFF = 73728
    R_OFF = 81920
    R_SIZE = 124 * 1024
    TOTAL = R_OFF + R_SIZE
    arena = stack.enter_context(nc.sbuf_tensor("arena", [128, TOTAL // 2], BF16))
    A16 = arena[:]
    A32 = arena[:].bitcast(F32)
    PS16 = A16.ap[0][0]
    PS32 = A32.ap[0][0]
    psum = stack.enter_context(nc.psum_tensor("psum", [128, 4096], F32))
    P32 = psum[:]
    P16 = psum[:].bitcast(BF16)
    PP32 = P32.ap[0][0]
    PP16 = P16.ap[0][0]

    def v16(boff, dims, p0=0, np_=128):
        assert boff % 2 == 0
        return bass.AP(A16.tensor, p0 * PS16 + boff // 2, [[PS16, np_]] + [list(x) for x in dims])

    def v32(boff, dims, p0=0, np_=128):
        assert boff % 4 == 0
        return bass.AP(A32.tensor, p0 * PS32 + boff // 4, [[PS32, np_]] + [list(x) for x in dims])

    def pv32(col, dims, p0=0, np_=128):
        return bass.AP(P32.tensor, p0 * PP32 + col, [[PP32, np_]] + [list(x) for x in dims])

    def pv16(col, dims, p0=0, np_=128):
        return bass.AP(P16.tensor, p0 * PP16 + col, [[PP16, np_]] + [list(x) for x in dims])

    def dv(ap, off, dims):
        return bass.AP(ap.tensor, ap.offset + off, [list(x) for x in dims])

    banks = [Res("bank%d" % i) for i in range(8)]
    bank_rr = [0]

    def next_bank():
        b = bank_rr[0]
        bank_rr[0] = (b + 1) % 8
        return b

    def next_bank_pair():
        if bank_rr[0] % 2:
            bank_rr[0] = (bank_rr[0] + 1) % 8
        b = bank_rr[0]
        bank_rr[0] = (b + 2) % 8
        return b

    c_cur = [CONST_OFF]

    def calloc(nbytes):
        o = c_cur[0]
        c_cur[0] += (nbytes + 31) // 32 * 32
        assert c_cur[0] <= R_OFF
        return o

    IDB = calloc(256)
    IDF = calloc(512)
    ONESB = calloc(256)
    SMALL = calloc(896)
    BGH = calloc(64)
    ESINK = calloc(32)
    FLAG = calloc(16)
    SSQ = calloc(128)
    RSTD = calloc(128)
    SAVE = calloc(44 * 2 * 4)
    W2NM = calloc(44 * 4)
    W0NM = calloc(44 * 4)
    W2N1 = calloc(44 * 4)
    W0N1 = calloc(44 * 4)
    FL_CV = calloc(44 * 4 * 2)
    FL_H = calloc(64)
    X1C = calloc(32)
    RDS = calloc(128)
    EPSC = calloc(32)
    ZEROC = calloc(32)
    GFINB = calloc(4096)
    RT = SSQ + 64
    SM_GA, SM_GF, SM_GFIN, SM_BG, SM_CW, SM_CB, SM_SINK = 0, 8, 16, 24, 40, 172, 216
    R_const = Res("const")

    def sm(off, n=1, p0=0, np_=128):
        return v32(SMALL + 4 * off, [[1, n]], p0, np_)

    B_ACC = R_OFF
    B_QT = B_ACC + 16384
    B_KT = B_QT + 4096
    B_VA = B_KT + 6144
    B_EA = B_VA + 12288
    B_EB = B_EA + 12288
    B_ES = B_EB + 12288
    B_PT = B_ES + 4096
    B_RD = B_PT + 2048
    B_RS = B_RD + 4096
    B_WQ = B_RS + 4096
    B_ACC2 = B_WQ + 12288
    B_END = B_ACC2 + 16384
    assert B_END <= TOTAL
    A_XB = R_OFF
    A_XS = A_XB + 49152
    A_JUNK = A_XS + 4096
    assert A_JUNK + 2048 <= TOTAL
    C_X1T = R_OFF
    C_XN1 = C_X1T + 16512
    C_HT = C_XN1 + 8192
    C_UCG = C_HT + 22528
    C_UCV = C_UCG + 2064
    C_UCG2 = C_UCV + 2064
    C_UCV2 = C_UCG2 + 2064
    C_TMP = C_UCV2 + 2064
    C_SQ = C_TMP + 12288
    C_RB = C_SQ + 2048
    C_XR = C_RB + 4096
    C_OUTS = C_XR + 8192
    C_MG = C_OUTS + 4096
    C_WM = C_MG + 8192
    C_WUP = C_WM + 11264
    C_WDN = C_WUP + 8192
    C_END = C_WDN + 11264
    assert C_END <= TOTAL, C_END - TOTAL

    ch_c = S.chan()
    S.op("sp", lambda e: e.dma_start(out=v32(SMALL, [[1, 224]]), in_=small_d), writes=[R_const], chan=ch_c)
    S.op("sp", lambda e: e.dma_start(out=v32(GFINB, [[1, 1024]]), in_=gfinb_d), writes=[R_const], chan=ch_c)
    S.op("sp", lambda e: e.dma_start(out=v32(IDF, [[1, 128]]), in_=ident_d), writes=[R_const], chan=ch_c)
    S.op("sp", lambda e: e.dma_start(out=v32(FLAG, [[1, 4]]), in_=flagv_d), writes=[R_const], chan=ch_c)
    S.op("dve", lambda e: e.tensor_copy(out=v16(IDB, [[1, 128]]), in_=v32(IDF, [[1, 128]])),
         reads=[R_const], writes=[R_const])
    S.op("pool", lambda e: e.memset(v16(ONESB, [[1, 128]]), 1.0), writes=[R_const])
    S.op("pool", lambda e: e.memset(v32(SAVE, [[1, 88]]), 0.0), writes=[R_const])
    S.op("pool", lambda e: e.memset(v32(X1C, [[1, 8]]), 0.0), writes=[R_const])
    S.op("pool", lambda e: e.memset(v32(EPSC, [[1, 8]]), EPS), writes=[R_const])
    S.op("pool", lambda e: e.memset(v32(ZEROC, [[1, 8]]), 0.0), writes=[R_const])
    S.op("dve", lambda e: e.tensor_scalar(out=v32(BGH, [[1, 16]]), in0=sm(SM_BG, 16), scalar1=0.5, scalar2=None,
                                          op0=ALU.mult), reads=[R_const], writes=[R_const])
    S.op("act", lambda e: e.activation(out=v32(ESINK, [[1, 8]]), in_=sm(SM_SINK, 8), func=AF.Exp),
         reads=[R_const], writes=[R_const])
    for dst, src, scal in ((W2NM, SM_CW + 88, None), (W0NM, SM_CW, None), (W2N1, SM_CW + 88, -1.0),
                           (W0N1, SM_CW, -1.0)):
        if scal is None:
            S.op("dve", lambda e, dst=dst, src=src: e.tensor_scalar(
                out=v32(dst, [[1, 44]]), in0=sm(src, 44), scalar1=v32(FLAG + 8, [[1, 1]]), scalar2=None,
                op0=ALU.mult), reads=[R_const], writes=[R_const])
        else:
            S.op("dve", lambda e, dst=dst, src=src, scal=scal: e.tensor_scalar(
                out=v32(dst, [[1, 44]]), in0=sm(src, 44), scalar1=scal, scalar2=None, op0=ALU.mult),
                reads=[R_const], writes=[R_const])

    R_scr = Res("scratch_R")
    ch_e = S.chan()
    OHS = R_OFF + 90112
    RELS = OHS + 8192
    ONES32 = OHS + 8192 + 128
    ERT = OHS + 16384
    S.op("sp", lambda e: e.dma_start(out=v32(OHS, [[1, 2048]], 0, 32), in_=oh_d), writes=[R_scr], chan=ch_e)
    S.op("sp", lambda e: e.dma_start(out=v32(RELS, [[1, 20]], 0, 32), in_=relb_d), writes=[R_scr], chan=ch_e)
    S.op("pool", lambda e: e.memset(v32(ONES32, [[1, 20]], 0, 32), 1.0), writes=[R_scr])
    R_er = Res("er")
    ch_er = S.chan()
    R_ert = [Res("ert0"), Res("ert1")]
    for t in range(4):
        bv = next_bank()
        bm = next_bank()
        mm_chain(S, cx, [(pv32(bv * 512, [[1, 512]], 0, 20), v32(RELS, [[1, 20]], 0, 32),
                          v32(OHS + t * 2048, [[1, 512]], 0, 32))], [R_scr], [banks[bv]])
        mm_chain(S, cx, [(pv32(bm * 512, [[1, 512]], 0, 20), v32(ONES32, [[1, 20]], 0, 32),
                          v32(OHS + t * 2048, [[1, 512]], 0, 32))], [R_scr], [banks[bm]])
        tmp = ERT + (t % 2) * 2048
        R_t = R_ert[t % 2]
        S.op("act", lambda e, bv=bv, tmp=tmp: e.activation(out=v32(tmp, [[1, 512]], 0, 20),
                                                            in_=pv32(bv * 512, [[1, 512]], 0, 20), func=AF.Exp),
             reads=[banks[bv]], writes=[R_t])
        S.op("dve", lambda e, bm=bm, tmp=tmp: e.tensor_tensor(out=v32(tmp, [[1, 512]], 0, 20),
                                                               in0=v32(tmp, [[1, 512]], 0, 20),
                                                               in1=pv32(bm * 512, [[1, 512]], 0, 20), op=ALU.mult),
             reads=[banks[bm], R_t], writes=[R_t])
        S.op("sp", lambda e, t=t, tmp=tmp: e.dma_start(out=er_d[t * 20:(t + 1) * 20, :],
                                                       in_=v32(tmp, [[1, 512]], 0, 20)),
             reads=[R_t], writes=[R_er], chan=ch_er)

    CV_OFF = R_OFF + 110592
    cv_names = [("wm", wm_d, 8 * 2816), ("wo", wo_d, 8192), ("wup", wup_d, NCP * 2048), ("wdn", wdn_d, 8 * 2816),
                ("wqkv", wqkv_d, 6 * 3072), ("wbkv", wbkv_d, 2048), ("wbq", wbq_d, 4096)]
    WB = {}
    R_cv = {}
    R_cvs = [Res("cvs0"), Res("cvs1")]
    ch_cvi = [S.chan(), S.chan()]
    ch_cvo = [S.chan(), S.chan()]
    cv_chunks = []
    for (nm, src_ap, ncols) in cv_names:
        WB[nm] = nc.dram_tensor(nm + "_b", [128, ncols], BF16, kind="Internal").ap()
        R_cv[nm] = Res("cv_" + nm)
        c = 0
        while c < ncols:
            w_ = min(4096, ncols - c)
            cv_chunks.append((nm, src_ap, ncols, c, w_))
            c += w_

    def conv_gen():
        def emit_in(k):
            nm, src_ap, ncols, c, w_ = cv_chunks[k]
            sl = k % 2
            S.op("pool", lambda e: e.dma_start(out=v16(CV_OFF + sl * 8192, [[1, w_]]),
                                               in_=dv(src_ap, c, [[ncols, 128], [1, w_]])),
                 writes=[R_cvs[sl]], chan=ch_cvi[sl])

        def emit_out(k):
            nm, src_ap, ncols, c, w_ = cv_chunks[k]
            sl = k % 2
            S.op("pool", lambda e: e.dma_start(out=dv(WB[nm], c, [[ncols, 128], [1, w_]]),
                                               in_=v16(CV_OFF + sl * 8192, [[1, w_]])),
                 reads=[R_cvs[sl]], writes=[R_cv[nm]], chan=ch_cvo[sl])

        for k in range(len(cv_chunks)):
            emit_in(k)
            if k >= 1:
                emit_out(k - 1)
            yield
        emit_out(len(cv_chunks) - 1)
        yield

    cvg = conv_gen()

    def conv_step(n):
        for _ in range(n):
            next(cvg, None)


    def chk(tag):
        if STOP[0] == tag:
            S.stopped = True

    units = [(0, False, True), (2048, True, False), (4096, False, False)]
    R_xn = Res("xn")
    R_oab = Res("oab")
    XNW = 3072

    def xn_ap(k, col, n, step=1, p0=0, np_=128):
        return v16(XN_OFF + 2 * (k * XNW + col), [[step, n]], p0, np_)

    R_save = Res("save")
    R_x1t = [Res("x1t%d" % i) for i in range(8)]
    R_x1c = Res("x1c")
    R_esave = Res("esave")
    R_dscr = Res("dscr")
    R_rscr = Res("rscr")
    R_rds = Res("rds")
    ch_out = S.chan()
    R_outsA = Res("outsA")
    R_outsB = Res("outsB")

    chk('prologue')
    for ui, (own0, left, right) in enumerate(units):
        ext0 = own0 - (1024 if left else 0)
        next_ = 2048 + (1024 if (left or right) else 0)
        ownc = own0 - ext0
        nblk = next_ // 128

        if ui > 0:
            S.barrier()
        R_wq = [Res("wq0"), Res("wq1")]
        ch_wq = [S.chan(), S.chan()]
        wq_i = [0]
        def load_wslab(src_ap, ncols, R_src):
            s = wq_i[0] % 2
            wq_i[0] += 1
            S.op("pool", lambda e, s=s: e.dma_start(out=v16(B_WQ + s * 6144, [[1, ncols]]), in_=src_ap),
                 reads=[R_src], writes=[R_wq[s]], chan=ch_wq[s])
            return s

        slab_specs = []
        R_nodep = Res("nodep")
        if ui == 0:
            for hp_ in range(2):
                for g_ in range(3):
                    slab_specs.append((dv(wqkv_d, (g_ * 2 + hp_) * 3072, [[6 * 3072, 128], [1, 3072]]), 3072, R_nodep))
            slab_specs.append((dv(wbkv_d, 0, [[2048, 128], [1, 2048]]), 2048, R_nodep))
            for ci_ in range(4):
                slab_specs.append((dv(wbq_d, ci_ * 1024, [[4096, 128], [1, 1024]]), 1024, R_nodep))
        else:
            for hp_ in range(2):
                for g_ in range(3):
                    slab_specs.append((dv(WB["wqkv"], (g_ * 2 + hp_) * 3072, [[6 * 3072, 128], [1, 3072]]), 3072,
                                       R_cv["wqkv"]))
            slab_specs.append((dv(WB["wbkv"], 0, [[2048, 128], [1, 2048]]), 2048, R_cv["wbkv"]))
            for ci_ in range(4):
                slab_specs.append((dv(WB["wbq"], ci_ * 1024, [[4096, 128], [1, 1024]]), 1024, R_cv["wbq"]))
        slab_state = {"i": 0, "slot": load_wslab(*slab_specs[0])}

        R_xb = [Res("xb%d" % i) for i in range(12)]
        ch_xb = [S.chan() for _ in range(12)]
        R_ssqs = [Res("ssq0"), Res("ssq1")]
        R_xs = [Res("xs%d" % i) for i in range(2)]
        R_junk = Res("junk")
        for bi_, b0 in enumerate(range(0, nblk, 6)):
            nb = min(6, nblk - b0)
            hf = bi_ % 2
            R_ssq = R_ssqs[hf]
            SSQ_ = SSQ + 32 * hf
            RSTD_ = RSTD + 32 * hf
            S.op("act", lambda e, SSQ_=SSQ_: e.activation(out=v32(SSQ_, [[1, 8]]), in_=v32(ZEROC, [[1, 8]]), func=AF.Copy),
                 reads=[R_const], writes=[R_ssq])
            for gi_, j0_ in enumerate(range(0, nb, 3)):
                ng = min(3, nb - j0_)
                sl0 = hf * 6 + j0_
                r0 = ext0 + (b0 + j0_) * 128
                S.op(("sp", "act")[gi_ % 2], lambda e, sl0=sl0, r0=r0, ng=ng: e.dma_start(
                    out=v32(A_XB + sl0 * 4096, [[1024, ng], [1, 1024]]),
                    in_=bass.AP(xin.tensor, xin.offset + r0 * D, [[D, 128], [128 * D, ng], [1, D]])),
                     writes=[R_xb[sl0 + q] for q in range(ng)], chan=ch_xb[sl0])
            for j in range(nb):
                b = b0 + j
                sl = hf * 6 + j
                S.op("act", lambda e, sl=sl, j=j, SSQ_=SSQ_: e.activation(out=v16(A_JUNK, [[1, 1024]]),
                                                        in_=v32(A_XB + sl * 4096, [[1, 1024]]), func=AF.Square,
                                                        accum_out=v32(SSQ_ + 4 * j, [[1, 1]])),
                     reads=[R_xb[sl]], writes=[R_junk, R_ssq])
            S.op("dve", lambda e, nb=nb, SSQ_=SSQ_, RSTD_=RSTD_: e.tensor_scalar(
                out=v32(RSTD_, [[1, nb]]), in0=v32(SSQ_, [[1, nb]]),
                scalar1=1.0 / D, scalar2=EPS, op0=ALU.mult, op1=ALU.add),
                 reads=[R_ssq], writes=[R_ssq])
            S.op("act", lambda e, nb=nb, RSTD_=RSTD_: e.activation(out=v32(RSTD_, [[1, nb]]), in_=v32(RSTD_, [[1, nb]]),
                                                      func=AF.Sqrt), reads=[R_ssq], writes=[R_ssq])
            S.op("dve", lambda e, nb=nb, RSTD_=RSTD_: e.reciprocal(out=v32(RSTD_, [[1, nb]]), in_=v32(RSTD_, [[1, nb]])),
                 reads=[R_ssq], writes=[R_ssq])
            pend_ev = None
            for j in range(nb):
                b = b0 + j
                s = b % 2
                sl = hf * 6 + j
                S.op("dve", lambda e, j=j, s=s, sl=sl, RSTD_=RSTD_: e.tensor_scalar(out=v16(A_XS + s * 2048, [[1, 1024]]),
                                                                in0=v32(A_XB + sl * 4096, [[1, 1024]]),
                                                                scalar1=v32(RSTD_ + 4 * j, [[1, 1]]), scalar2=None,
                                                                op0=ALU.mult),
                     reads=[R_xb[sl], R_ssq], writes=[R_xs[s]])
                bk = next_bank()
                mm_chain(S, cx, [(pv16(bk * 1024 + c * 128, [[1, 128]]), v16(A_XS + s * 2048 + c * 256, [[1, 128]]),
                                  v16(IDB, [[1, 128]])) for c in range(8)], [R_xs[s], R_const], [banks[bk]],
                         transpose=True)
                if pend_ev is not None:
                    pend_ev()
                pend_ev = (lambda b=b, bk=bk: S.op("dve", lambda e: e.tensor_tensor(
                    out=v16(XN_OFF + 2 * (b * 128), [[XNW, 8], [1, 128]]),
                    in0=pv16(bk * 1024, [[128, 8], [1, 128]]),
                    in1=v32(SMALL + 4 * SM_GA, [[1, 8], [0, 128]]), op=ALU.mult),
                    reads=[banks[bk], R_const], writes=[R_xn]))
            if pend_ev is not None:
                pend_ev()

        chk('ph1_%d' % ui)
        S.barrier()
        R_E = Res("E")
        ch_E = S.chan()
        R_es = [Res("es0"), Res("es1")]
        R_pt = [Res("pt0"), Res("pt1")]
        if ui == 0:
            R_stg = Res("stage")
            STG = B_ACC
            for g in range(3):
                dst = STG + g * 1024 * 4
                src = bass.AP(er_d.tensor, (g * 20 + 4 * g) * 512 - 64 + PAD - 127,
                              [[1, 128], [512, 4], [128, 2], [1, 128]])
                S.op(("sp", "act")[g % 2], lambda e, dst=dst, src=src: e.dma_start(
                    out=v32(dst, [[256, 4], [128, 2], [1, 128]]), in_=src),
                     reads=[R_er], writes=[R_stg], chan=ch_E)
            src = bass.AP(er_d.tensor, (3 * 20 + 12) * 512 - 128 + PAD - 127, [[1, 128], [512, 8], [128, 3], [1, 128]])
            S.op("act", lambda e, src=src: e.dma_start(out=v32(STG + 12288, [[384, 8], [128, 3], [1, 128]]), in_=src),
                 reads=[R_er], writes=[R_stg], chan=ch_E)
            S.op("dve", lambda e: e.tensor_copy(out=v32(B_EA, [[128, 48], [1, 128]]),
                                                in_=v32(STG + 127 * 4, [[128, 48], [-1, 128]])),
                 reads=[R_stg], writes=[R_E])
            ch_Es = S.chan()
            S.op("sp", lambda e: e.dma_start(out=esave_d, in_=v32(B_EA, [[1, 6144]])), reads=[R_E],
                 writes=[R_esave], chan=ch_Es)
            S.barrier()
        else:
            S.op("sp", lambda e: e.dma_start(out=v32(B_EA, [[1, 6144]]), in_=esave_d), reads=[R_esave],
                 writes=[R_E], chan=ch_E)
        chk('etab_%d' % ui)
        R_accs = [Res("acc0"), Res("acc1")]
        ACCB = [B_ACC, B_ACC2]
        acc_cur = {"i": 0}
        R_qt = Res("qt")
        R_kt = Res("kt")
        R_va = Res("va")
        R_rd = Res("rd")
        R_rs = Res("rs")
        ch_rs = S.chan()

        def proj_fm(s, wcol, wk, evac, col0, n):
            bk = next_bank()
            mm_chain(S, cx, [(pv32(bk * 512, [[1, n]]), v16(B_WQ + s * 6144 + 2 * (k * wk + wcol), [[1, 128]]),
                              xn_ap(k, col0, n)) for k in range(8)], [R_wq[s], R_xn], [banks[bk]])
            evac(bk)

        def attn(rq_list, kblocks_all, klo, nk, n, d, nside, e_offs, hh_list, hooks=None):
            SW = nside * 128
            groups = [hh_list] if nside == 2 else [[hh] for hh in hh_list]
            items = []
            for (r, qblocks) in rq_list:
                for (qs, v0, v1, kbs) in qblocks:
                    for gidx, grp in enumerate(groups):
                        items.append((r, qs, v0, v1, kbs, gidx, grp))

            def stage_s(idx):
                r, qs, v0, v1, kbs, gidx, grp = items[idx]
                w = v1 - v0
                c0 = v0 - qs
                bs = next_bank()
                es = idx % 2
                steps = []
                for gi, hh in enumerate(grp):
                    for (side, gb) in kbs:
                        st, ln = kblocks_all[gb]
                        steps.append((pv32(bs * 512 + gi * SW + side * 128 + c0, [[1, w]], 0, ln),
                                      v16(B_KT + 2 * (r * nk + (st - klo)), [[1, ln]], 64 * hh, 64),
                                      v16(B_QT + 2 * (r * n + v0), [[1, w]], 64 * hh, 64)))
                nst = len(steps)
                nper = len(kbs)
                prev_ev = None
                for i, (o, a, b_) in enumerate(steps):
                    boundary_next = (nside == 2 and (i + 1) % nper == 0 and i != nst - 1)
                    boundary_here = (nside == 2 and i % nper == 0 and i != 0)
                    ev_ = S.op("pe", lambda e, o=o, a=a, b_=b_: e.matmul(o, a, b_, start=True, stop=True),
                               reads=[R_kt, R_qt], writes=[banks[bs]] if i in (0, nst - 1) else (),
                               extra=[prev_ev] if (boundary_here and prev_ev) else (),
                               signal=(i == nst - 1) or boundary_next, self_wait=boundary_here)
                    if boundary_next:
                        prev_ev = ev_
                wd = len(grp) * SW
                S.op("act", lambda e, bs=bs, es=es, wd=wd: e.activation(
                    out=v32(B_ES + es * 2048, [[1, wd]]), in_=pv32(bs * 512, [[1, wd]]), func=AF.Exp,
                    scale=0.125), reads=[banks[bs]], writes=[R_es[es]])
                eoff = e_offs[gidx]
                S.op("dve", lambda e, es=es, wd=wd, eoff=eoff: e.tensor_tensor(
                    out=v16(B_PT + es * 1024, [[1, wd]]), in0=v32(B_ES + es * 2048, [[1, wd]]),
                    in1=v32(eoff, [[1, wd]]), op=ALU.mult), reads=[R_es[es], R_E], writes=[R_pt[es]])

            def stage_pv(idx):
                r, qs, v0, v1, kbs, gidx, grp = items[idx]
                w = v1 - v0
                c0 = v0 - qs
                es = idx % 2
                bo = next_bank()
                nh = len(grp)
                h0 = grp[0]
                cnt = 0
                tot = nh * len(kbs)
                for gi, hh in enumerate(grp):
                    for j, (side, gb) in enumerate(kbs):
                        st, ln = kblocks_all[gb]
                        cnt += 1
                        o = pv32(bo * 512 + hh * 128 + c0, [[1, w]])
                        a = v16(B_VA + 2 * (gb * 192 + hh * 64), [[1, 128]], 0, ln)
                        b_ = v16(B_PT + es * 1024 + 2 * (gi * SW + side * 128 + c0), [[1, w]], 0, ln)
                        S.op("pe", lambda e, o=o, a=a, b_=b_, j=j, nk_=len(kbs): e.matmul(
                            o, a, b_, start=(j == 0), stop=(j == nk_ - 1)),
                            reads=[R_va, R_pt[es]], writes=[banks[bo]] if cnt in (1, tot) else (),
                            signal=(cnt == tot))
                ab = ACCB[acc_cur["i"]]
                R_acc = R_accs[acc_cur["i"]]
                S.op("dve", lambda e, bo=bo, nh=nh, h0=h0, v0=v0, w=w, c0=c0, r=r, ab=ab: e.tensor_tensor(
                    out=v32(ab + 4 * (h0 * 2048 + v0 * d + r), [[2048, nh], [d, w]]),
                    in0=pv32(bo * 512 + h0 * 128 + c0, [[128, nh], [1, w]]),
                    in1=v32(ab + 4 * (h0 * 2048 + v0 * d + r), [[2048, nh], [d, w]]), op=ALU.add),
                    reads=[banks[bo], R_acc], writes=[R_acc])

            for idx in range(len(items) + 1):
                if idx < len(items):
                    stage_s(idx)
                if hooks and idx in hooks:
                    hooks[idx]()
                if idx >= 1:
                    stage_pv(idx - 1)

        def set_va_ones(nb_, halo_list):
            S.op("pool", lambda e: e.memset(v16(B_VA + 2 * 64, [[192, nb_], [1, 64]]), 1.0), writes=[R_va])
            for gb, hm in halo_list:
                S.op("pool", lambda e, gb=gb, hm=hm: e.tensor_copy(
                    out=v16(B_VA + 2 * (gb * 192 + 64), [[1, 64]]),
                    in_=v32(FLAG + 4 * hm, [[0, 64]])), reads=[R_const, R_va], writes=[R_va])

        def finalize_a(ai, sink_cols=None):
            ab = ACCB[ai]
            R_acc = R_accs[ai]
            S.op("sp", lambda e: e.dma_start(out=dscr_d[0:1, :], in_=v32(ab, [[1, 2048]], 64, 1)),
                 reads=[R_acc], writes=[R_dscr], chan=ch_rs)
            S.op("sp", lambda e: e.dma_start(out=dscr_d[1:2, :], in_=v32(ab + 8192, [[1, 2048]], 0, 1)),
                 reads=[R_acc], writes=[R_dscr], chan=ch_rs)
            S.op("sp", lambda e: e.dma_start(out=v32(RDS, [[1, 32]], 0, 64),
                                             in_=bass.AP(dscr_d.tensor, 0, [[32, 64], [1, 32]])),
                 reads=[R_dscr], writes=[R_rds], chan=ch_rs)
            S.op("sp", lambda e: e.dma_start(out=v32(RDS, [[1, 32]], 64, 64),
                                             in_=bass.AP(dscr_d.tensor, 2048, [[32, 64], [1, 32]])),
                 reads=[R_dscr], writes=[R_rds], chan=ch_rs)

        def finalize_b(ai, dst_chunk, sink_cols=None):
            ab = ACCB[ai]
            R_acc = R_accs[ai]
            if sink_cols is not None:
                for half, col in enumerate(sink_cols):
                    S.op("dve", lambda e, half=half, col=col: e.tensor_scalar(
                        out=v32(RDS, [[1, 32]], 64 * half, 64), in0=v32(RDS, [[1, 32]], 64 * half, 64),
                        scalar1=v32(ESINK + 4 * col, [[1, 1]], 64 * half, 64), scalar2=None, op0=ALU.add),
                        reads=[R_rds, R_const], writes=[R_rds])
            S.op("dve", lambda e: e.reciprocal(out=v32(RDS, [[1, 32]]), in_=v32(RDS, [[1, 32]])),
                 reads=[R_rds], writes=[R_rds])
            S.op("sp", lambda e: e.dma_start(out=bass.AP(rscr_d.tensor, 0, [[32, 128], [1, 32]]),
                                             in_=v32(RDS, [[1, 32]])),
                 reads=[R_rds], writes=[R_rscr], chan=ch_rs)
            S.op("sp", lambda e: e.dma_start(out=v32(B_RD, [[1, 2048]], 0, 64),
                                             in_=bass.AP(rscr_d.tensor, 0, [[0, 64], [1, 2048]])),
                 reads=[R_rscr], writes=[R_rs], chan=ch_rs)
            S.op("sp", lambda e: e.dma_start(out=v32(B_RD, [[1, 2048]], 64, 64),
                                             in_=bass.AP(rscr_d.tensor, 2048, [[0, 64], [1, 2048]])),
                 reads=[R_rscr], writes=[R_rs], chan=ch_rs)

        def finalize_c(ai, dst_chunk):
            ab = ACCB[ai]
            R_acc = R_accs[ai]
            S.op("dve", lambda e: e.tensor_tensor(
                out=v16(OAB_OFF + 2 * (dst_chunk * 2048), [[1, 2048]], 0, 64),
                in0=v32(ab, [[1, 2048]], 0, 64), in1=v32(B_RD, [[1, 2048]], 0, 64), op=ALU.mult),
                reads=[R_acc, R_rs], writes=[R_oab])
            S.op("dve", lambda e: e.tensor_tensor(
                out=v16(OAB_OFF + 2 * (dst_chunk * 2048), [[1, 2048]], 64, 64),
                in0=v32(ab + 8192, [[1, 2048]], 64, 64), in1=v32(B_RD, [[1, 2048]], 64, 64), op=ALU.mult),
                reads=[R_acc, R_rs], writes=[R_oab])

        pend = {"f": None}

        def fin_start(ai, dst_chunk, sink_cols=None):
            finalize_a(ai, sink_cols)
            pend["f"] = (ai, dst_chunk, sink_cols, 0)

        def fin_step():
            f = pend["f"]
            if f is None:
                return
            ai, dst_chunk, sink_cols, stage = f
            if stage == 0:
                finalize_b(ai, dst_chunk, sink_cols)
                pend["f"] = (ai, dst_chunk, sink_cols, 1)
            else:
                finalize_c(ai, dst_chunk)
                pend["f"] = None

        def fin_flush():
            while pend["f"] is not None:
                fin_step()

        def project_kv(s, wk, kcol, vcol, d, n, klo, khi, nk, kblocks):
            nbr = len(kblocks)
            ranges = []
            if left:
                ranges.append((klo * d, 0))
            ranges += [(o, o + 512) for o in range(0, 2048, 512)]
            if right:
                ranges.append((2048, 2048 + (khi - n) * d))
            tiles = []
            for (a, b_) in ranges:
                o = a
                while o < b_:
                    nn = min(512, b_ - o)
                    tiles.append((o, nn))
                    o += nn
            for (o, nn) in tiles:
                def evac(bk, o=o, nn=nn):
                    S.op("act", lambda e: e.activation(
                        out=v16(B_KT + 2 * (o // d - klo), [[1, nn // d], [nk, d]]),
                        in_=pv32(bk * 512, [[d, nn // d], [1, d]]), func=AF.Copy),
                        reads=[banks[bk]], writes=[R_kt])
                proj_fm(s, kcol, wk, evac, ownc + o, nn)
            for r in range(d):
                for bi, (st, ln) in enumerate(kblocks):
                    gb = r * nbr + bi
                    bk = next_bank()
                    col = ownc + st * d + r
                    mm_chain(S, cx, [(pv32(bk * 512, [[1, 128]], 0, ln), xn_ap(k, col, ln, d),
                                      v16(B_WQ + s * 6144 + 2 * (k * wk + vcol), [[1, 128]])) for k in range(8)],
                             [R_wq[s], R_xn], [banks[bk]])
                    hm = None
                    if st < 0:
                        hm = 1 if ln == 128 and st == -64 else 0
                    elif st >= n:
                        hm = 0
                    if hm is None:
                        S.op("act", lambda e, gb=gb, bk=bk, ln=ln: e.activation(
                            out=v16(B_VA + 2 * (gb * 192), [[128, 2], [1, 64]], 0, ln),
                            in_=pv32(bk * 512, [[64, 2], [1, 64]], 0, ln), func=AF.Copy),
                            reads=[banks[bk], R_va], writes=[R_va])
                    else:
                        S.op("dve", lambda e, gb=gb, bk=bk, ln=ln, hm=hm: e.tensor_scalar(
                            out=v16(B_VA + 2 * (gb * 192), [[128, 2], [1, 64]], 0, ln),
                            in0=pv32(bk * 512, [[64, 2], [1, 64]], 0, ln),
                            scalar1=v32(FLAG + 4 * hm, [[1, 1]], 0, ln), scalar2=None, op0=ALU.mult),
                            reads=[banks[bk], R_const, R_va], writes=[R_va])

        def take_slab():
            conv_step(3)
            cur = slab_state["slot"]
            slab_state["i"] += 1
            if slab_state["i"] < len(slab_specs):
                slab_state["slot"] = load_wslab(*slab_specs[slab_state["i"]])
            return cur

        for hp in range(2):
            acc_cur["i"] = hp % 2
            S.op("pool", lambda e, ab=ACCB[hp % 2]: e.memset(v32(ab, [[1, 4096]]), 0.0), writes=[R_accs[hp % 2]])
            for g in range(3):
                d = DILS[g]
                n, klo, khi, nk, kblocks, qblocks = geom(d, left, right)
                nbr = len(kblocks)
                ph = g * 2 + hp
                s = take_slab()
                halo_list = []
                for r in range(d):
                    for bi, (st, ln) in enumerate(kblocks):
                        if st < 0:
                            halo_list.append((r * nbr + bi, 1))
                        elif st >= n:
                            halo_list.append((r * nbr + bi, 0))
                allblocks = [(st, ln) for r in range(d) for (st, ln) in kblocks]
                set_va_ones(len(allblocks), halo_list)
                chk('va1_%d_%d_%d' % (ui, hp, g))
                for o in range(0, 2048, 512):
                    def evq(bk, o=o, d=d, n=n):
                        S.op("act", lambda e: e.activation(
                            out=v16(B_QT + 2 * (o // d), [[1, 512 // d], [n, d]]),
                            in_=pv32(bk * 512, [[d, 512 // d], [1, d]]), func=AF.Copy),
                            reads=[banks[bk]], writes=[R_qt])
                    proj_fm(s, 0, 384, evq, ownc + o, 512)
                chk('qproj_%d_%d_%d' % (ui, hp, g))
                fin_step()
                project_kv(s, 384, 128, 256, d, n, klo, khi, nk, kblocks)
                fin_step()
                chk('kvproj_%d_%d_%d' % (ui, hp, g))
                rq = []
                for r in range(d):
                    rq.append((r, [(qs, v0, v1, [(side, r * nbr + bi) for (side, bi) in kbs])
                                   for (qs, v0, v1, kbs) in qblocks]))
                attn(rq, allblocks, klo, nk, n, d, 2, [B_EA + ph * 2048], [0, 1])
                chk('attng_%d_%d_%d' % (ui, hp, g))
            fin_flush()
            fin_start(hp % 2, hp)
            chk('attnA%d_%d' % (hp, ui))

        n, klo, khi, nk, kblocks, qblocks = geom_b(left, right)
        s = take_slab()
        set_va_ones(len(kblocks), [(bi, 0) for bi, (st, ln) in enumerate(kblocks) if st < 0 or st >= n])
        fin_step()
        project_kv(s, 256, 0, 128, 1, n, klo, khi, nk, kblocks)
        fin_step()
        for ci in range(4):
            acc_cur["i"] = ci % 2
            S.op("pool", lambda e, ab=ACCB[ci % 2]: e.memset(v32(ab, [[1, 4096]]), 0.0), writes=[R_accs[ci % 2]])
            s = take_slab()
            for o in range(0, 2048, 512):
                def evq(bk, o=o):
                    S.op("act", lambda e: e.activation(out=v16(B_QT + 2 * o, [[1, 512]]),
                                                       in_=pv32(bk * 512, [[1, 512]]), func=AF.Copy),
                         reads=[banks[bk]], writes=[R_qt])
                proj_fm(s, 0, 128, evq, ownc + o, 512)
            fin_step()
            attn([(0, qblocks)], kblocks, klo, nk, n, 1, 3,
                 [B_EB + ci * 1536, B_EB + (4 + ci) * 1536], [0, 1], hooks={8: fin_step})
            fin_flush()
            fin_start(ci % 2, 2 + ci, (ci, 4 + ci))
        fin_flush()
        conv_step(40)

        chk('attn_%d' % ui)
        S.barrier()
        R_xr = [Res("xr0"), Res("xr1")]
        ch_xr = [S.chan(), S.chan()]
        R_wm = [Res("wm0"), Res("wm1")]
        ch_wm = [S.chan(), S.chan()]
        R_wup = [Res("wup0"), Res("wup1")]
        ch_wup = [S.chan(), S.chan()]
        R_wdn = [Res("wdn0"), Res("wdn1")]
        ch_wdn = [S.chan(), S.chan()]
        R_mg = Res("mg")
        R_xn1 = [Res("xn1_%d" % i) for i in range(8)]
        R_ht = [Res("ht%d" % i) for i in range(NCP)]
        R_ucg = Res("ucg")
        R_ucv = Res("ucv")
        R_ucg2 = Res("ucg2")
        R_ucv2 = Res("ucv2")
        R_tmp = [Res("tmp%d" % i) for i in range(6)]
        R_sq = [Res("sq0"), Res("sq1")]
        R_rb = [Res("rb0"), Res("rb1")]
        wmi = [0]
        outs_i = [0]
        wupi = [0]
        wdni = [0]
        xri = [0]

        class WStream:
            def __init__(self, items, slots):
                self.items = items
                self.slots = slots
                self.i = 0
                self._load(0)

            def _load(self, i):
                if i >= len(self.items):
                    return
                src, ncols, R_src = self.items[i]
                boff, R_w, ch_w = self.slots[i % len(self.slots)]
                S.op("pool", lambda e: e.dma_start(out=v16(boff, [[1, ncols]]), in_=src), reads=[R_src],
                     writes=[R_w], chan=ch_w)

            def take(self):
                i = self.i
                self.i += 1
                self._load(i + 1)
                boff, R_w, ch_w = self.slots[i % len(self.slots)]
                return boff, R_w

        wm_items = []
        for tt_ in range(4):
            for m_ in range(8):
                wm_items.append((dv(WB["wm"], m_ * 2816, [[8 * 2816, 128], [1, 2816]]), 2816, R_cv["wm"]))
            for m_ in range(0, 8, 2):
                wm_items.append((dv(WB["wo"], m_ * 1024, [[8192, 128], [1, 2048]]), 2048, R_cv["wo"]))
        wdn_items = [(dv(WB["wdn"], m_ * 2816, [[8 * 2816, 128], [1, 2816]]), 2816, R_cv["wdn"])
                     for _ in range(5 if ui == NUNIT - 1 else 4) for m_ in range(8)]
        wm_stream = WStream(wm_items, [(C_WM, R_wm[0], ch_wm[0]), (C_WM + 5632, R_wm[1], ch_wm[1])])
        wdn_stream = WStream(wdn_items, [(C_WDN, R_wdn[0], ch_wdn[0]), (C_WDN + 5632, R_wdn[1], ch_wdn[1])])

        def T(i):
            return C_TMP + i * 2048

        MT = [(C_OUTS, [R_outsA]), (C_OUTS + 2048, [R_outsB]), (C_SQ, [R_sq[0], R_sq[1]]), (C_SQ + 2048, [R_rb[0]])]

        def x1t(m, j0, nn, p0=0, np_=128):
            return v32(C_X1T + 4 * (m * 516 + j0), [[1, nn]], p0, np_)

        def rms_bc(j0, nn, rbi):
            bk = next_bank()
            for m in range(8):
                sq = m % 2
                if m % 2 == 0:
                    S.op("act", lambda e, m=m, sq=sq: e.activation(out=v16(C_SQ + sq * 1024, [[1, nn]]),
                                                                   in_=x1t(m, j0, nn), func=AF.Square),
                         reads=[R_x1t[m]], writes=[R_sq[sq]])
                else:
                    S.op("dve", lambda e, m=m, sq=sq: e.tensor_tensor(out=v16(C_SQ + sq * 1024, [[1, nn]]),
                                                                      in0=x1t(m, j0, nn), in1=x1t(m, j0, nn),
                                                                      op=ALU.mult),
                         reads=[R_x1t[m]], writes=[R_sq[sq]])
                S.op("pe", lambda e, m=m, sq=sq, bk=bk: e.matmul(pv32(bk * 512, [[1, nn]]), v16(ONESB, [[1, 128]]),
                                                              v16(C_SQ + sq * 1024, [[1, nn]]), start=(m == 0),
                                                              stop=(m == 7)),
                     reads=[R_sq[sq], R_const], writes=[banks[bk]], signal=True)
            rb = C_RB + rbi * 2048
            S.op("act", lambda e, bk=bk: e.activation(out=v32(rb, [[1, nn]]), in_=pv32(bk * 512, [[1, nn]]),
                                                      func=AF.Ln, bias=v32(EPSC, [[1, 1]]), scale=1.0 / D),
                 reads=[banks[bk], R_const], writes=[R_rb[rbi]])
            S.op("act", lambda e: e.activation(out=v32(rb, [[1, nn]]), in_=v32(rb, [[1, nn]]), func=AF.Exp,
                                               scale=-0.5),
                 reads=[R_rb[rbi]], writes=[R_rb[rbi]])

        def down_part(j0, nn, rhs_fn):
            for m in range(8):
                wb, R_w = wdn_stream.take()
                bk = next_bank()
                mm_chain(S, cx, [(pv32(bk * 512, [[1, nn]]), v16(wb + 2 * (k * 128), [[1, 128]]),
                                  rhs_fn(k)) for k in range(NCP)], [R_w], [banks[bk]],
                         step_reads=[[R_ht[k]] for k in range(NCP)])
                S.op("dve", lambda e, m=m, bk=bk: e.scalar_tensor_tensor(
                    out=x1t(m, j0, nn), in0=pv32(bk * 512, [[1, nn]]), scalar=0.5, in1=x1t(m, j0, nn),
                    op0=ALU.mult, op1=ALU.add), reads=[banks[bk], R_x1t[m]], writes=[R_x1t[m]])

        def final_part(j0, nn, tok0):
            R_rt = Res("rt")
            blocks = []
            st = 0
            while st < nn:
                ln = min(128, nn - st)
                blocks.append((st, ln))
                st += ln

            def do_t(bi):
                st, ln = blocks[bi]
                bp = next_bank_pair()
                mm_chain(S, cx, [(pv32(bp * 512 + m * 128, [[1, 128]], 0, ln), x1t(m, j0 + st, ln),
                                  v32(IDF, [[1, 128]])) for m in range(8)], R_x1t + [R_const],
                         [banks[bp], banks[bp + 1]], transpose=True)
                return bp

            def do_e(bi, bp):
                st, ln = blocks[bi]
                osl = outs_i[0] % 2
                outs_i[0] += 1
                ob = C_OUTS if osl == 0 else C_SQ
                ores = [R_outsA, R_outsB] if osl == 0 else [R_sq[0], R_sq[1], R_rb[0]]
                S.op("dve", lambda e: e.scalar_tensor_tensor(
                    out=v32(ob, [[1, 1024]], 0, ln), in0=pv32(bp * 512, [[1, 1024]], 0, ln),
                    scalar=v32(RT + 4 * bi, [[1, 1]], 0, ln), in1=v32(GFINB, [[1, 1024]], 0, ln),
                    op0=ALU.mult, op1=ALU.mult),
                    reads=[banks[bp], banks[bp + 1], R_rt, R_const], writes=ores)
                S.op("sp", lambda e, t0=tok0 + st: e.dma_start(out=yout[t0:t0 + ln, :],
                                                               in_=v32(ob, [[1, 1024]], 0, ln)),
                     reads=ores, writes=[], chan=ch_out)

            nb = len(blocks)
            pre = [do_t(bi) for bi in range(min(2, nb))]
            rms_bc(j0, nn, 1)
            bk = next_bank()
            for bi, (st, ln) in enumerate(blocks):
                mm_chain(S, cx, [(pv32(bk * 512 + bi * 128, [[1, 128]], 0, ln), v32(C_RB + 2048 + 4 * st, [[1, ln]]),
                                  v32(IDF, [[1, 128]]))], [R_rb[1], R_const], [banks[bk]], transpose=True)
                S.op("act", lambda e, bi=bi, ln=ln, bk=bk: e.activation(
                    out=v32(RT + 4 * bi, [[1, 1]], 0, ln), in_=pv32(bk * 512 + bi * 128, [[1, 1]], 0, ln),
                    func=AF.Copy), reads=[banks[bk], R_rt], writes=[R_rt])
            for bi in range(len(pre)):
                do_e(bi, pre[bi])
            for bi in range(len(pre), nb):
                do_e(bi, do_t(bi))

        def merge_gen(tt):
            ocol = ownc + tt * 512
            for m in range(8):
                wb, R_w = wm_stream.take()
                ba, bb, bga, bgb = next_bank(), next_bank(), next_bank(), next_bank()
                mm_chain(S, cx, [(pv32(ba * 512, [[1, 512]]), v16(wb + 2 * (k * 128), [[1, 128]]),
                                  v16(OAB_OFF + 2 * (k * 2048 + tt * 512), [[1, 512]])) for k in range(2)],
                         [R_w, R_oab], [banks[ba]])
                mm_chain(S, cx, [(pv32(bb * 512, [[1, 512]]), v16(wb + 2 * ((2 + k) * 128), [[1, 128]]),
                                  v16(OAB_OFF + 2 * ((2 + k) * 2048 + tt * 512), [[1, 512]])) for k in range(4)],
                         [R_w, R_oab], [banks[bb]])
                mm_chain(S, cx, [(pv32(bga * 512, [[1, 512]]), v16(wb + 2 * ((6 + k) * 128), [[1, 128]]),
                                  xn_ap(k, ocol, 512)) for k in range(8)], [R_w, R_xn], [banks[bga]])
                mm_chain(S, cx, [(pv32(bgb * 512, [[1, 512]]), v16(wb + 2 * ((14 + k) * 128), [[1, 128]]),
                                  xn_ap(k, ocol, 512)) for k in range(8)], [R_w, R_xn], [banks[bgb]])
                (ta_b, ta_r), (tb_b, tb_r) = MT[2 * (m % 2)], MT[2 * (m % 2) + 1]
                S.op("act", lambda e, m=m, bga=bga, ta_b=ta_b: e.activation(
                    out=v32(ta_b, [[1, 512]]), in_=pv32(bga * 512, [[1, 512]]), func=AF.Tanh,
                    bias=v32(BGH + 4 * m, [[1, 1]]), scale=0.5), reads=[banks[bga], R_const], writes=ta_r)
                S.op("act", lambda e, m=m, bgb=bgb, tb_b=tb_b: e.activation(
                    out=v32(tb_b, [[1, 512]]), in_=pv32(bgb * 512, [[1, 512]]), func=AF.Tanh,
                    bias=v32(BGH + 4 * (8 + m), [[1, 1]]), scale=0.5), reads=[banks[bgb], R_const],
                    writes=tb_r)
                S.op("dve", lambda e, ba=ba, ta_b=ta_b: e.scalar_tensor_tensor(
                    out=v32(ta_b, [[1, 512]]), in0=v32(ta_b, [[1, 512]]), scalar=1.0, in1=pv32(ba * 512, [[1, 512]]),
                    op0=ALU.add, op1=ALU.mult), reads=[banks[ba]] + ta_r, writes=ta_r)
                S.op("dve", lambda e, bb=bb, tb_b=tb_b: e.scalar_tensor_tensor(
                    out=v32(tb_b, [[1, 512]]), in0=v32(tb_b, [[1, 512]]), scalar=1.0, in1=pv32(bb * 512, [[1, 512]]),
                    op0=ALU.add, op1=ALU.mult), reads=[banks[bb]] + tb_r, writes=tb_r)
                S.op("dve", lambda e, m=m, ta_b=ta_b, tb_b=tb_b: e.tensor_tensor(
                    out=v16(C_MG + 2 * (m * 512), [[1, 512]]), in0=v32(ta_b, [[1, 512]]), in1=v32(tb_b, [[1, 512]]),
                    op=ALU.add), reads=ta_r + tb_r, writes=[R_mg])
                yield
            chk('merge_%d_%d' % (ui, tt))

        xslots = [(C_XR, R_xr[0], ch_xr[0]), (C_XR + 4096, R_xr[1], ch_xr[1]),
                  (C_WUP, R_wup[0], ch_wup[0]), (C_WUP + 4096, R_wup[1], ch_wup[1])]

        def load_x_tile(tt):
            c0_ = own0 + tt * 512
            for tb in range(4):
                xb_, R_x, ch_x = xslots[tb]
                S.op("sp", lambda e, xb_=xb_, t0=c0_ + tb * 128: e.dma_start(
                    out=v32(xb_, [[1, 1024]]), in_=xin[t0:t0 + 128, :]),
                    writes=[R_x], chan=ch_x)

        def mid_phase(tt):
            gt = ui * 4 + tt
            c0 = own0 + tt * 512
            S.op("dve", lambda e: e.tensor_copy(out=v32(C_X1T, [[516, 8]]), in_=v32(X1C, [[1, 8]])),
                 reads=[R_x1c] + R_x1t, writes=R_x1t)
            for tb in range(4):
                xb_, R_x, ch_x = xslots[tb]
                bp = next_bank_pair()
                mm_chain(S, cx, [(pv32(bp * 512 + m * 128, [[1, 128]]), v32(xb_ + 4 * (m * 128), [[1, 128]]),
                                  v32(IDF, [[1, 128]])) for m in range(8)], [R_x, R_const],
                         [banks[bp], banks[bp + 1]], transpose=True)
                S.op("act", lambda e, bp=bp, tb=tb: e.activation(
                    out=v32(C_X1T + 4 * (1 + tb * 128), [[516, 8], [1, 128]]),
                    in_=pv32(bp * 512, [[128, 8], [1, 128]]), func=AF.Copy),
                    reads=[banks[bp], banks[bp + 1]] + R_x1t, writes=R_x1t)
            for m in range(8):
                if m % 2 == 0:
                    wb, R_w = wm_stream.take()
                else:
                    wb = wb + 2048
                bk = next_bank()
                mm_chain(S, cx, [(pv32(bk * 512, [[1, 512]]), v16(wb + 2 * (k * 128), [[1, 128]]),
                                  v16(C_MG + 2 * (k * 512), [[1, 512]])) for k in range(8)], [R_w, R_mg],
                         [banks[bk]])
                S.op("dve", lambda e, m=m, bk=bk: e.scalar_tensor_tensor(
                    out=x1t(m, 1, 512), in0=pv32(bk * 512, [[1, 512]]), scalar=0.5, in1=x1t(m, 1, 512),
                    op0=ALU.mult, op1=ALU.add), reads=[banks[bk], R_x1t[m]], writes=[R_x1t[m]])
            chk('y_%d_%d' % (ui, tt))
            rms_bc(1, 512, 0)
            for m in range(8):
                S.op("dve", lambda e, m=m: e.scalar_tensor_tensor(
                    out=v16(C_XN1 + 2 * (m * 512), [[1, 512]]), in0=x1t(m, 1, 512), scalar=sm(SM_GF + m),
                    in1=v32(C_RB, [[1, 512]]), op0=ALU.mult, op1=ALU.mult),
                    reads=[R_x1t[m], R_rb[0], R_const], writes=[R_xn1[m]])
            chk('xn1_%d_%d' % (ui, tt))
            wup_slots = [(C_WUP, R_wup[0], ch_wup[0]), (C_WUP + 4096, R_wup[1], ch_wup[1]),
                         (C_XR, R_xr[0], ch_xr[0]), (C_XR + 4096, R_xr[1], ch_xr[1])]

            def load_wup(cp):
                wb, R_w, ch_w = wup_slots[cp % 4]
                S.op("pool", lambda e, cp=cp, wb=wb: e.dma_start(
                    out=v16(wb, [[1, 2048]]), in_=dv(WB["wup"], cp * 2048, [[NCP * 2048, 128], [1, 2048]])),
                    reads=[R_cv["wup"]], writes=[R_w], chan=ch_w)

            def chain_cp(cp, st_):
                wb, R_w, ch_w = wup_slots[cp % 4]
                UCG_, UCV_ = (C_UCG, C_UCV) if st_ == 0 else (C_UCG2, C_UCV2)
                R_g, R_v = (R_ucg, R_ucv) if st_ == 0 else (R_ucg2, R_ucv2)
                iG, iV, iX = 3 * st_, 3 * st_ + 1, 3 * st_ + 2
                if cp + 3 < NCP:
                    load_wup(cp + 3)
                bg, bv = next_bank(), next_bank()
                mm_chain(S, cx, [(pv32(bg * 512, [[1, 512]]), v16(wb + 2 * (k * 256), [[1, 128]]),
                                  v16(C_XN1 + 2 * (k * 512), [[1, 512]])) for k in range(8)], [R_w],
                         [banks[bg]], step_reads=[[R_xn1[k]] for k in range(8)])
                mm_chain(S, cx, [(pv32(bv * 512, [[1, 512]]), v16(wb + 2 * (k * 256 + 128), [[1, 128]]),
                                  v16(C_XN1 + 2 * (k * 512), [[1, 512]])) for k in range(8)], [R_w],
                         [banks[bv]], step_reads=[[R_xn1[k]] for k in range(8)])
                yield
                for (UC, R_uc, bk, ch_i, ti, ceng) in ((UCG_, R_g, bg, cp, iG, "dve"), (UCV_, R_v, bv, NCP + cp, iV, "dve")):
                    S.op("act", lambda e, UC=UC, ch_i=ch_i: e.activation(
                        out=v32(UC, [[1, 2]]), in_=v32(SAVE + 8 * ch_i, [[1, 2]]), func=AF.Copy),
                        reads=[R_save, R_uc], writes=[R_uc])
                    S.op("act", lambda e, UC=UC, bk=bk: e.activation(
                        out=v32(UC + 8, [[1, 512]]), in_=pv32(bk * 512, [[1, 512]]), func=AF.Copy),
                        reads=[banks[bk], R_uc], writes=[R_uc])
                    yield
                    S.op("act", lambda e, UC=UC, ch_i=ch_i: e.activation(
                        out=v32(SAVE + 8 * ch_i, [[1, 2]]), in_=v32(UC + 4 * 512, [[1, 2]]), func=AF.Copy),
                        reads=[R_uc, R_save], writes=[R_save])
                    S.op("act", lambda e, UC=UC, ch_i=ch_i, ti=ti: e.activation(
                        out=v32(T(ti), [[1, 512]]), in_=v32(UC + 4, [[1, 512]]), func=AF.Identity,
                        bias=sm(SM_CB + ch_i), scale=sm(SM_CW + 44 + ch_i)),
                        reads=[R_uc, R_const], writes=[R_tmp[ti]])
                    yield
                    S.op(ceng, lambda e, UC=UC, ch_i=ch_i, ti=ti: e.scalar_tensor_tensor(
                        out=v32(T(ti), [[1, 512]]), in0=v32(UC, [[1, 512]]), scalar=sm(SM_CW + ch_i),
                        in1=v32(T(ti), [[1, 512]]), op0=ALU.mult, op1=ALU.add),
                        reads=[R_uc, R_const, R_tmp[ti]], writes=[R_tmp[ti]])
                    yield
                    S.op(ceng, lambda e, UC=UC, ch_i=ch_i, ti=ti: e.scalar_tensor_tensor(
                        out=v32(T(ti), [[1, 512]]), in0=v32(UC + 8, [[1, 512]]), scalar=sm(SM_CW + 88 + ch_i),
                        in1=v32(T(ti), [[1, 512]]), op0=ALU.mult, op1=ALU.add),
                        reads=[R_uc, R_const, R_tmp[ti]], writes=[R_tmp[ti]])
                    if tt == 0 and ui > 0:
                        w2x = W2NM if ui == 1 else W2N1
                        w0x = W0NM if ui == 1 else W0N1
                        S.op(ceng, lambda e, UC=UC, ch_i=ch_i, ti=ti, w2x=w2x: e.scalar_tensor_tensor(
                            out=v32(T(ti), [[1, 1]]), in0=v32(UC + 8, [[1, 1]]), scalar=v32(w2x + 4 * ch_i, [[1, 1]]),
                            in1=v32(T(ti), [[1, 1]]), op0=ALU.mult, op1=ALU.add),
                            reads=[R_uc, R_const, R_tmp[ti]], writes=[R_tmp[ti]])
                        S.op(ceng, lambda e, UC=UC, ch_i=ch_i, ti=ti, w0x=w0x: e.scalar_tensor_tensor(
                            out=v32(T(ti) + 4, [[1, 1]]), in0=v32(UC + 4, [[1, 1]]),
                            scalar=v32(w0x + 4 * ch_i, [[1, 1]]), in1=v32(T(ti) + 4, [[1, 1]]),
                            op0=ALU.mult, op1=ALU.add),
                            reads=[R_uc, R_const, R_tmp[ti]], writes=[R_tmp[ti]])
                    yield
                S.op("act", lambda e: e.activation(out=v32(T(iX), [[1, 512]]), in_=v32(T(iG), [[1, 512]]),
                                                   func=AF.Square, scale=math.sqrt(0.044715)),
                     reads=[R_tmp[iG]], writes=[R_tmp[iX]])
                yield
                S.op("dve", lambda e: e.scalar_tensor_tensor(
                    out=v32(T(iX), [[1, 512]]), in0=v32(T(iX), [[1, 512]]), scalar=1.0, in1=v32(T(iG), [[1, 512]]),
                    op0=ALU.add, op1=ALU.mult), reads=[R_tmp[iX], R_tmp[iG]], writes=[R_tmp[iX]])
                yield
                S.op("act", lambda e: e.activation(out=v32(T(iX), [[1, 512]]), in_=v32(T(iX), [[1, 512]]),
                                                   func=AF.Tanh, scale=GELU_K), reads=[R_tmp[iX]], writes=[R_tmp[iX]])
                yield
                S.op("dve", lambda e: e.scalar_tensor_tensor(
                    out=v32(T(iX), [[1, 512]]), in0=v32(T(iX), [[1, 512]]), scalar=1.0, in1=v32(T(iG), [[1, 512]]),
                    op0=ALU.add, op1=ALU.mult), reads=[R_tmp[iX], R_tmp[iG]], writes=[R_tmp[iX]])
                yield
                S.op("dve", lambda e, cp=cp: e.tensor_tensor(
                    out=v16(C_HT + 2 * (cp * 512), [[1, 512]]), in0=v32(T(iX), [[1, 512]]), in1=v32(T(iV), [[1, 512]]),
                    op=ALU.mult), reads=[R_tmp[iX], R_tmp[iV]], writes=[R_ht[cp]])

            load_wup(0)
            load_wup(1)
            load_wup(2)
            mg = merge_gen(tt + 1) if tt + 1 < 4 else None
            for cp0 in range(0, NCP, 2):
                if mg is not None and cp0 >= 10:
                    next(mg, None)
                gens = [chain_cp(cp0, 0), chain_cp(cp0 + 1, 1)]
                alive = [True, True]
                step = 0
                while any(alive):
                    for gi_ in range(2):
                        if not alive[gi_]:
                            continue
                        if gi_ == 1 and step < 0:
                            continue
                        try:
                            next(gens[gi_])
                        except StopIteration:
                            alive[gi_] = False
                    step += 1
            if mg is not None:
                for _ in mg:
                    pass
            chk('up_%d_%d' % (ui, tt))
            j0 = 1 if gt == 0 else 0
            down_part(j0, 512 - j0, lambda k, j0=j0: v16(C_HT + 2 * (k * 512 + j0), [[1, 512 - j0]]))
            S.op("dve", lambda e: e.tensor_copy(out=v32(X1C, [[1, 8]]),
                                                in_=v32(C_X1T + 4 * 512, [[516, 8]])),
                 reads=R_x1t + [R_x1c], writes=[R_x1c])
            return j0, c0

        load_x_tile(0)
        for _ in merge_gen(0):
            pass
        for tt in range(4):
            j0, c0 = mid_phase(tt)
            if tt + 1 < 4:
                load_x_tile(tt + 1)
            final_part(j0, 512 - j0, c0 - 1 + j0)
            chk('down_%d_%d' % (ui, tt))

        if ui == NUNIT - 1:
            cv, t1 = FL_CV, FL_CV + 176
            R_fl = Res("flush")
            S.op("dve", lambda e: e.tensor_copy(out=v32(C_X1T, [[516, 8]]), in_=v32(X1C, [[1, 8]])),
                 reads=[R_x1c] + R_x1t, writes=R_x1t)
            S.op("dve", lambda e: e.tensor_tensor(out=v32(cv, [[1, 44]]), in0=v32(SAVE, [[2, 44]]),
                                                  in1=sm(SM_CW, 44), op=ALU.mult),
                 reads=[R_save, R_const], writes=[R_fl])
            S.op("dve", lambda e: e.tensor_tensor(out=v32(t1, [[1, 44]]), in0=v32(SAVE + 4, [[2, 44]]),
                                                  in1=sm(SM_CW + 44, 44), op=ALU.mult),
                 reads=[R_save, R_const, R_fl], writes=[R_fl])
            S.op("dve", lambda e: e.tensor_tensor(out=v32(cv, [[1, 44]]), in0=v32(cv, [[1, 44]]),
                                                  in1=v32(t1, [[1, 44]]), op=ALU.add), reads=[R_fl], writes=[R_fl])
            S.op("dve", lambda e: e.tensor_tensor(out=v32(cv, [[1, 44]]), in0=v32(cv, [[1, 44]]),
                                                  in1=sm(SM_CB, 44), op=ALU.add), reads=[R_fl, R_const],
                 writes=[R_fl])
            S.op("dve", lambda e: e.tensor_tensor(out=v32(t1, [[1, 22]]), in0=v32(cv, [[1, 22]]),
                                                  in1=v32(cv, [[1, 22]]), op=ALU.mult), reads=[R_fl], writes=[R_fl])
            S.op("dve", lambda e: e.tensor_scalar(out=v32(t1, [[1, 22]]), in0=v32(t1, [[1, 22]]), scalar1=0.044715,
                                                  scalar2=1.0, op0=ALU.mult, op1=ALU.add), reads=[R_fl],
                 writes=[R_fl])
            S.op("dve", lambda e: e.tensor_tensor(out=v32(t1, [[1, 22]]), in0=v32(t1, [[1, 22]]),
                                                  in1=v32(cv, [[1, 22]]), op=ALU.mult), reads=[R_fl], writes=[R_fl])
            S.op("act", lambda e: e.activation(out=v32(t1, [[1, 22]]), in_=v32(t1, [[1, 22]]), func=AF.Tanh,
                                               scale=GELU_K), reads=[R_fl], writes=[R_fl])
            S.op("dve", lambda e: e.scalar_tensor_tensor(out=v32(t1, [[1, 22]]), in0=v32(t1, [[1, 22]]), scalar=1.0,
                                                         in1=v32(cv, [[1, 22]]), op0=ALU.add, op1=ALU.mult),
                 reads=[R_fl], writes=[R_fl])
            S.op("dve", lambda e: e.tensor_tensor(out=v16(FL_H, [[1, 22]]), in0=v32(t1, [[1, 22]]),
                                                  in1=v32(cv + 88, [[1, 22]]), op=ALU.mult), reads=[R_fl],
                 writes=R_ht)
            down_part(0, 1, lambda k: v16(FL_H + 2 * k, [[1, 1]]))
            final_part(0, 1, TOK - 1)

    S.stopped = False
    S.barrier()

    with nc.Block() as block:
        @block.sync
        def _(e):
            S.emit("sp", e)

        @block.scalar
        def _(e):
            S.emit("act", e)

        @block.vector
        def _(e):
            S.emit("dve", e)

        @block.gpsimd
        def _(e):
            S.emit("pool", e)

        @block.tensor
        def _(e):
            S.emit("pe", e)
    stack.close()
    return nc


def _kp(w):
    K = w.shape[0] // 128
    return np.ascontiguousarray(w.reshape(K, 128, w.shape[1]).transpose(1, 0, 2))


_PROG = None


def kernel(x_prompt, x_sample, g_attn, w_in, b_gate, rel_bias, sink, w_a_out, w_b_out, w_o,
           g_ffn, w_up, conv_w, conv_b, w_down, g_final):
    global _PROG
    f = np.float32
    x_prompt = np.asarray(x_prompt, f)
    x_sample = np.asarray(x_sample, f)
    w_in = np.asarray(w_in, f)[0]
    w_a_out = np.asarray(w_a_out, f)[0]
    w_b_out = np.asarray(w_b_out, f)[0]
    w_o = np.asarray(w_o, f)[0]
    w_up = np.asarray(w_up, f)[0]
    w_down = np.asarray(w_down, f)[0]
    conv_w = np.asarray(conv_w, f)[0]
    conv_b = np.asarray(conv_b, f)[0]
    g_attn = np.asarray(g_attn, f)[0]
    g_ffn = np.asarray(g_ffn, f)[0]
    b_gate = np.asarray(b_gate, f)[0]
    sink = np.asarray(sink, f)[0]
    g_final = np.asarray(g_final, f)
    rel_bias = np.asarray(rel_bias, f)

    winp = _kp(w_in)
    wqkv = np.zeros((128, 6, 8, 384), f)
    for g in range(3):
        for hp in range(2):
            c = (4 * g + 2 * hp) * 64
            ph = g * 2 + hp
            wqkv[:, ph, :, 0:128] = winp[:, :, c:c + 128]
            wqkv[:, ph, :, 128:256] = winp[:, :, 768 + c:768 + c + 128]
            wqkv[:, ph, :, 256:384] = winp[:, :, 1536 + c:1536 + c + 128]
    QB0 = 2304
    KB0 = QB0 + 512
    VB0 = KB0 + 128
    G0 = VB0 + 128
    wbkv = np.zeros((128, 8, 256), f)
    wbkv[:, :, 0:128] = winp[:, :, KB0:KB0 + 128]
    wbkv[:, :, 128:256] = winp[:, :, VB0:VB0 + 128]
    wbq = np.zeros((128, 4, 8, 128), f)
    for ci in range(4):
        wbq[:, ci, :, 0:64] = winp[:, :, QB0 + 64 * ci:QB0 + 64 * ci + 64]
        wbq[:, ci, :, 64:128] = winp[:, :, QB0 + 64 * (4 + ci):QB0 + 64 * (4 + ci) + 64]
    wap = _kp(w_a_out)
    wbo_perm = np.zeros((4, 128, 1024), f)
    for ci in range(4):
        wbo_perm[ci, 0:64] = w_b_out[64 * ci:64 * ci + 64]
        wbo_perm[ci, 64:128] = w_b_out[64 * (4 + ci):64 * (4 + ci) + 64]
    wbop = np.ascontiguousarray(wbo_perm.transpose(1, 0, 2))
    wm = np.zeros((128, 8, 22, 128), f)
    for m in range(8):
        ms = slice(m * 128, (m + 1) * 128)
        wm[:, m, 0:2] = wap[:, :, ms]
        wm[:, m, 2:6] = wbop[:, :, ms]
        wm[:, m, 6:14] = winp[:, :, G0 + m * 128:G0 + (m + 1) * 128]
        wm[:, m, 14:22] = winp[:, :, G0 + 1024 + m * 128:G0 + 1024 + (m + 1) * 128]
    wop = _kp(w_o)
    wo = np.ascontiguousarray(wop.reshape(128, 8, 8, 128).transpose(0, 2, 1, 3))
    wupp = _kp(w_up)
    wup = np.zeros((128, NCP, 8, 256), f)
    for cp in range(NCP):
        wup[:, cp, :, 0:128] = wupp[:, :, cp * 128:(cp + 1) * 128]
        wup[:, cp, :, 128:256] = wupp[:, :, DFF + cp * 128:DFF + (cp + 1) * 128]
    wdnp = _kp(w_down)
    wdn = np.ascontiguousarray(wdnp.reshape(128, NCP, 8, 128).transpose(0, 2, 1, 3))

    small = np.zeros((128, 224), f)
    small[:, 0:8] = g_attn.reshape(8, 128).T
    small[:, 8:16] = g_ffn.reshape(8, 128).T
    small[:, 16:24] = g_final.reshape(8, 128).T
    small[:, 24:40] = b_gate.reshape(16, 128).T
    for k in range(3):
        small[:, 40 + 44 * k:40 + 44 * (k + 1)] = conv_w[k].reshape(44, 128).T
    small[:, 172:216] = conv_b.reshape(44, 128).T
    small[:, 216:224] = sink[None, :]
    oh = _onehot_tables()
    ident = np.eye(128, dtype=f)

    common = dict(wqkv=wqkv.reshape(128, -1), wbkv=wbkv.reshape(128, -1), wbq=wbq.reshape(128, -1),
                  wm=wm.reshape(128, -1), wo=wo.reshape(128, -1), wup=wup.reshape(128, -1),
                  wdn=wdn.reshape(128, -1), small=small, relb=rel_bias, oh=oh, ident=ident,
                  gfinb=np.ascontiguousarray(np.broadcast_to(g_final[None, :], (128, 1024))))
    in_maps = []
    for c in range(NCORES):
        if c < 4:
            xs = np.concatenate([x_prompt[c], x_sample[c]], axis=0)
            fl = 1.0
        else:
            b = 4 + 3 * (c - 4)
            xs = np.concatenate([x_sample[b], x_sample[b + 1], x_sample[b + 2]], axis=0)
            fl = 0.0
        flagv = np.zeros((128, 4), f)
        flagv[:, 0] = fl
        flagv[:, 1] = 1.0
        flagv[:64, 1] = fl
        flagv[:, 2] = fl - 1.0
        m = dict(common)
        m["xin"] = np.ascontiguousarray(xs)
        m["flagv"] = flagv
        in_maps.append(m)

    if _PROG is None:
        _PROG = build_program()
    res = run_bass_kernel_spmd(_PROG, in_maps, core_ids=list(range(NCORES)))
    y_prompt = np.zeros_like(x_prompt)
    y_sample = np.zeros_like(x_sample)
    for c in range(NCORES):
        y = res.results[c]["yout"]
        if c < 4:
            y_prompt[c] = y[:4096]
            y_sample[c] = y[4096:]
        else:
            b = 4 + 3 * (c - 4)
            for i in range(3):
                y_sample[b + i] = y[2048 * i:2048 * (i + 1)]
    return y_prompt, y_sample
```

```python
import math
from contextlib import ExitStack

import numpy as np

import concourse.bass as bass
import concourse.mybir as mybir
from concourse.bass_utils import run_bass_kernel_spmd

F32 = mybir.dt.float32
BF16 = mybir.dt.bfloat16
ALU = mybir.AluOpType
AF = mybir.ActivationFunctionType

NCORES = 8
TOK = 6144
D = 1024
UNIT = 2048
NUNIT = 3
DFF = 2816
NCP = 22
EPS = 1e-6
PAD = 256
GELU_K = 0.7978845608028654
DILS = (1, 4, 16)

ENGS = ("sp", "act", "dve", "pool", "pe")


class Res:
    __slots__ = ("name", "w", "r")

    def __init__(self, name=""):
        self.name = name
        self.w = None
        self.r = {}


class Sched:
    def __init__(self, nc, stack):
        self.nc = nc
        self.stack = stack
        self.q = {e: [] for e in ENGS}
        self.sem = {}
        self.val = {}
        for e in ENGS:
            self.sem[e] = stack.enter_context(nc.semaphore("sem_" + e))
            self.val[e] = 0
        self.waited = {e: {} for e in ENGS}
        self.nchan = 0
        self.stopped = False

    def chan(self):
        sid = "dma%d" % self.nchan
        self.nchan += 1
        self.sem[sid] = self.stack.enter_context(self.nc.semaphore(sid))
        self.val[sid] = 0
        return sid

    def op(self, eng, fn, reads=(), writes=(), extra=(), chan=None, signal=True, self_wait=False):
        if self.stopped:
            return None
        deps = {}

        def need(ev):
            if ev is None:
                return
            if deps.get(ev[0], 0) < ev[1]:
                deps[ev[0]] = ev[1]

        for r in reads:
            need(r.w)
        for w in writes:
            need(w.w)
            for sid, v in w.r.items():
                need((sid, v))
        for e in extra:
            need(e)
        waits = []
        wd = self.waited[eng]
        for sid, v in deps.items():
            if sid == eng and eng == "pe" and not self_wait:
                continue
            if wd.get(sid, 0) < v:
                wd[sid] = v
                waits.append((sid, v))
        if chan is not None:
            self.val[chan] += 16
            ev = (chan, self.val[chan])
            inc = (chan, 16)
        elif signal:
            self.val[eng] += 1
            ev = (eng, self.val[eng])
            inc = (eng, 1)
        else:
            ev = None
            inc = None
        self.q[eng].append((waits, fn, inc))
        if ev is not None:
            for r in reads:
                if r.r.get(ev[0], 0) < ev[1]:
                    r.r[ev[0]] = ev[1]
            for w in writes:
                w.w = ev
                w.r = {}
        return ev

    def barrier(self):
        if self.stopped:
            return
        snap = dict(self.val)
        for e in ENGS:
            waits = []
            for sid, v in snap.items():
                if v == 0 or sid == e:
                    continue
                if self.waited[e].get(sid, 0) < v:
                    self.waited[e][sid] = v
                    waits.append((sid, v))
            if waits:
                self.q[e].append((waits, None, None))

    def emit(self, eng_name, e):
        for waits, fn, inc in self.q[eng_name]:
            for sid, v in waits:
                e.wait_ge(self.sem[sid], v)
            if fn is None:
                continue
            inst = fn(e)
            if inc is not None:
                inst.then_inc(self.sem[inc[0]], inc[1])


def _rel_bucket(rel):
    rel = np.asarray(rel, dtype=np.int64)
    nb = 16
    max_exact = 8
    ret = np.where(rel > 0, nb, 0)
    n = np.abs(rel)
    nf = np.maximum(n, 1).astype(np.float32)
    large = max_exact + (np.log(nf / np.float32(max_exact)) / np.float32(math.log(1024 / max_exact))
                         * np.float32(nb - max_exact)).astype(np.int32)
    large = np.minimum(large, nb - 1)
    return ret + np.where(n < max_exact, n, large)


def _onehot_tables():
    oh = np.zeros((32, 4 * 512), np.float32)
    for t in range(4):
        d = DILS[t] if t < 3 else 1
        R = 64 if t < 3 else 128
        for delta in range(-R, R + 1):
            b = int(_rel_bucket(delta * d))
            oh[b, t * 512 + delta + PAD] = 1.0
    return oh


def geom(d, left, right):
    n = UNIT // d
    klo = -64 if left else 0
    khi = n + (64 if right else 0)
    nk = khi - klo
    kblocks = []
    s = klo
    while s < khi:
        kblocks.append((s, min(128, khi - s)))
        s += 128
    qblocks = []
    i = -1
    while True:
        qs = klo + 64 + 128 * i
        if qs >= n:
            break
        v0, v1 = max(qs, 0), min(qs + 128, n)
        if v1 > v0:
            kbs = []
            for side, bi in ((0, i), (1, i + 1)):
                if 0 <= bi < len(kblocks):
                    kbs.append((side, bi))
            qblocks.append((qs, v0, v1, kbs))
        i += 1
    return n, klo, khi, nk, kblocks, qblocks


def geom_b(left, right):
    n = UNIT
    klo = -128 if left else 0
    khi = n + (128 if right else 0)
    nk = khi - klo
    kblocks = [(s, 128) for s in range(klo, khi, 128)]
    qblocks = []
    for i in range(n // 128):
        qs = 128 * i
        kbs = []
        for m in range(3):
            st = qs - 128 + 128 * m
            if klo <= st < khi:
                kbs.append((m, (st - klo) // 128))
        qblocks.append((qs, qs, qs + 128, kbs))
    return n, klo, khi, nk, kblocks, qblocks


class Ctx:
    pass


def mm_chain(S, cx, steps, reads, wres, transpose=False, step_reads=None):
    n = len(steps)
    ev = None
    for i, (o, a, b) in enumerate(steps):
        first, last = (i == 0), (i == n - 1)
        if transpose:
            fn = (lambda e, o=o, a=a, b=b: e.transpose(o, a, b))
        else:
            fn = (lambda e, o=o, a=a, b=b, first=first, last=last: e.matmul(o, a, b, start=first, stop=last))
        rd = list(reads) + (list(step_reads[i]) if step_reads is not None else [])
        ev = S.op("pe", fn, reads=rd, writes=wres if (first or last) else (), signal=last)
    return ev


STOP = [None]
ATT_CUT = [5]


class _Stop(Exception):
    pass


def build_program():
    nc = bass.Bass("TRN2", target_bir_lowering=False)
    stack = ExitStack()

    def dram_in(name, shape, dt=F32):
        return nc.dram_tensor(name, list(shape), dt, kind="ExternalInput").ap()

    xin = dram_in("xin", [TOK, D])
    yout = nc.dram_tensor("yout", [TOK, D], F32, kind="ExternalOutput").ap()
    flagv_d = dram_in("flagv", [128, 4])
    wqkv_d = dram_in("wqkv", [128, 6 * 3072])
    wbkv_d = dram_in("wbkv", [128, 2048])
    wbq_d = dram_in("wbq", [128, 4 * 1024])
    wm_d = dram_in("wm", [128, 8 * 2816])
    wo_d = dram_in("wo", [128, 8 * 1024])
    wup_d = dram_in("wup", [128, NCP * 2048])
    wdn_d = dram_in("wdn", [128, 8 * 2816])
    small_d = dram_in("small", [128, 224])
    gfinb_d = dram_in("gfinb", [128, 1024])
    relb_d = dram_in("relb", [32, 20])
    oh_d = dram_in("oh", [32, 2048])
    ident_d = dram_in("ident", [128, 128])
    er_d = nc.dram_tensor("er_scr", [4 * 20, 512], F32, kind="Internal").ap()
    esave_d = nc.dram_tensor("esave", [128, 6144], F32, kind="Internal").ap()
    dscr_d = nc.dram_tensor("dscr", [2, 2048], F32, kind="Internal").ap()
    rscr_d = nc.dram_tensor("rscr", [2, 2048], F32, kind="Internal").ap()

    S = Sched(nc, stack)
    cx = Ctx()

    XN_OFF = 0
    OAB_OFF = 49152
    CONST_OFF = 73728
    R_OFF = 81920
    R_SIZE = 124 * 1024
    TOTAL = R_OFF + R_SIZE
    arena = stack.enter_context(nc.sbuf_tensor("arena", [128, TOTAL // 2], BF16))
    A16 = arena[:]
    A32 = arena[:].bitcast(F32)
    PS16 = A16.ap[0][0]
    PS32 = A32.ap[0][0]
    psum = stack.enter_context(nc.psum_tensor("psum", [128, 4096], F32))
    P32 = psum[:]
    P16 = psum[:].bitcast(BF16)
    PP32 = P32.ap[0][0]
    PP16 = P16.ap[0][0]

    def v16(boff, dims, p0=0, np_=128):
        assert boff % 2 == 0
        return bass.AP(A16.tensor, p0 * PS16 + boff // 2, [[PS16, np_]] + [list(x) for x in dims])

    def v32(boff, dims, p0=0, np_=128):
        assert boff % 4 == 0
        return bass.AP(A32.tensor, p0 * PS32 + boff // 4, [[PS32, np_]] + [list(x) for x in dims])

    def pv32(col, dims, p0=0, np_=128):
        return bass.AP(P32.tensor, p0 * PP32 + col, [[PP32, np_]] + [list(x) for x in dims])

    def pv16(col, dims, p0=0, np_=128):
        return bass.AP(P16.tensor, p0 * PP16 + col, [[PP16, np_]] + [list(x) for x in dims])

    def dv(ap, off, dims):
        return bass.AP(ap.tensor, ap.offset + off, [list(x) for x in dims])

    banks = [Res("bank%d" % i) for i in range(8)]
    bank_rr = [0]

    def next_bank():
        b = bank_rr[0]
        bank_rr[0] = (b + 1) % 8
        return b

    def next_bank_pair():
        if bank_rr[0] % 2:
            bank_rr[0] = (bank_rr[0] + 1) % 8
        b = bank_rr[0]
        bank_rr[0] = (b + 2) % 8
        return b

    c_cur = [CONST_OFF]

    def calloc(nbytes):
        o = c_cur[0]
        c_cur[0] += (nbytes + 31) // 32 * 32
        assert c_cur[0] <= R_OFF
        return o

    IDB = calloc(256)
    IDF = calloc(512)
    ONESB = calloc(256)
    SMALL = calloc(896)
    BGH = calloc(64)
    ESINK = calloc(32)
    FLAG = calloc(16)
    SSQ = calloc(128)
    RSTD = calloc(128)
    SAVE = calloc(44 * 2 * 4)
    W2NM = calloc(44 * 4)
    W0NM = calloc(44 * 4)
    W2N1 = calloc(44 * 4)
    W0N1 = calloc(44 * 4)
    FL_CV = calloc(44 * 4 * 2)
    FL_H = calloc(64)
    X1C = calloc(32)
    RDS = calloc(128)
    EPSC = calloc(32)
    ZEROC = calloc(32)
    GFINB = calloc(4096)
    RT = SSQ + 64
    SM_GA, SM_GF, SM_GFIN, SM_BG, SM_CW, SM_CB, SM_SINK = 0, 8, 16, 24, 40, 172, 216
    R_const = Res("const")

    def sm(off, n=1, p0=0, np_=128):
        return v32(SMALL + 4 * off, [[1, n]], p0, np_)

    B_ACC = R_OFF
    B_QT = B_ACC + 16384
    B_KT = B_QT + 4096
    B_VA = B_KT + 6144
    B_EA = B_VA + 12288
    B_EB = B_EA + 12288
    B_ES = B_EB + 12288
    B_PT = B_ES + 4096
    B_RD = B_PT + 2048
    B_RS = B_RD + 4096
    B_WQ = B_RS + 4096
    B_ACC2 = B_WQ + 12288
    B_END = B_ACC2 + 16384
    assert B_END <= TOTAL
    A_XB = R_OFF
    A_XS = A_XB + 49152
    A_JUNK = A_XS + 4096
    assert A_JUNK + 2048 <= TOTAL
    C_X1T = R_OFF
    C_XN1 = C_X1T + 16512
    C_HT = C_XN1 + 8192
    C_UCG = C_HT + 22528
    C_UCV = C_UCG + 2064
    C_UCG2 = C_UCV + 2064
    C_UCV2 = C_UCG2 + 2064
    C_TMP = C_UCV2 + 2064
    C_SQ = C_TMP + 12288
    C_RB = C_SQ + 2048
    C_XR = C_RB + 4096
    C_OUTS = C_XR + 8192
    C_MG = C_OUTS + 4096
    C_WM = C_MG + 8192
    C_WUP = C_WM + 11264
    C_WDN = C_WUP + 8192
    C_END = C_WDN + 11264
    assert C_END <= TOTAL, C_END - TOTAL

    ch_c = S.chan()
    S.op("sp", lambda e: e.dma_start(out=v32(SMALL, [[1, 224]]), in_=small_d), writes=[R_const], chan=ch_c)
    S.op("sp", lambda e: e.dma_start(out=v32(GFINB, [[1, 1024]]), in_=gfinb_d), writes=[R_const], chan=ch_c)
    S.op("sp", lambda e: e.dma_start(out=v32(IDF, [[1, 128]]), in_=ident_d), writes=[R_const], chan=ch_c)
    S.op("sp", lambda e: e.dma_start(out=v32(FLAG, [[1, 4]]), in_=flagv_d), writes=[R_const], chan=ch_c)
    S.op("dve", lambda e: e.tensor_copy(out=v16(IDB, [[1, 128]]), in_=v32(IDF, [[1, 128]])),
         reads=[R_const], writes=[R_const])
    S.op("pool", lambda e: e.memset(v16(ONESB, [[1, 128]]), 1.0), writes=[R_const])
    S.op("pool", lambda e: e.memset(v32(SAVE, [[1, 88]]), 0.0), writes=[R_const])
    S.op("pool", lambda e: e.memset(v32(X1C, [[1, 8]]), 0.0), writes=[R_const])
    S.op("pool", lambda e: e.memset(v32(EPSC, [[1, 8]]), EPS), writes=[R_const])
    S.op("pool", lambda e: e.memset(v32(ZEROC, [[1, 8]]), 0.0), writes=[R_const])
    S.op("dve", lambda e: e.tensor_scalar(out=v32(BGH, [[1, 16]]), in0=sm(SM_BG, 16), scalar1=0.5, scalar2=None,
                                          op0=ALU.mult), reads=[R_const], writes=[R_const])
    S.op("act", lambda e: e.activation(out=v32(ESINK, [[1, 8]]), in_=sm(SM_SINK, 8), func=AF.Exp),
         reads=[R_const], writes=[R_const])
    for dst, src, scal in ((W2NM, SM_CW + 88, None), (W0NM, SM_CW, None), (W2N1, SM_CW + 88, -1.0),
                           (W0N1, SM_CW, -1.0)):
        if scal is None:
            S.op("dve", lambda e, dst=dst, src=src: e.tensor_scalar(
                out=v32(dst, [[1, 44]]), in0=sm(src, 44), scalar1=v32(FLAG + 8, [[1, 1]]), scalar2=None,
                op0=ALU.mult), reads=[R_const], writes=[R_const])
        else:
            S.op("dve", lambda e, dst=dst, src=src, scal=scal: e.tensor_scalar(
                out=v32(dst, [[1, 44]]), in0=sm(src, 44), scalar1=scal, scalar2=None, op0=ALU.mult),
                reads=[R_const], writes=[R_const])

    R_scr = Res("scratch_R")
    ch_e = S.chan()
    OHS = R_OFF + 90112
    RELS = OHS + 8192
    ONES32 = OHS + 8192 + 128
    ERT = OHS + 16384
    S.op("sp", lambda e: e.dma_start(out=v32(OHS, [[1, 2048]], 0, 32), in_=oh_d), writes=[R_scr], chan=ch_e)
    S.op("sp", lambda e: e.dma_start(out=v32(RELS, [[1, 20]], 0, 32), in_=relb_d), writes=[R_scr], chan=ch_e)
    S.op("pool", lambda e: e.memset(v32(ONES32, [[1, 20]], 0, 32), 1.0), writes=[R_scr])
    R_er = Res("er")
    ch_er = S.chan()
    R_ert = [Res("ert0"), Res("ert1")]
    for t in range(4):
        bv = next_bank()
        bm = next_bank()
        mm_chain(S, cx, [(pv32(bv * 512, [[1, 512]], 0, 20), v32(RELS, [[1, 20]], 0, 32),
                          v32(OHS + t * 2048, [[1, 512]], 0, 32))], [R_scr], [banks[bv]])
        mm_chain(S, cx, [(pv32(bm * 512, [[1, 512]], 0, 20), v32(ONES32, [[1, 20]], 0, 32),
                          v32(OHS + t * 2048, [[1, 512]], 0, 32))], [R_scr], [banks[bm]])
        tmp = ERT + (t % 2) * 2048
        R_t = R_ert[t % 2]
        S.op("act", lambda e, bv=bv, tmp=tmp: e.activation(out=v32(tmp, [[1, 512]], 0, 20),
                                                            in_=pv32(bv * 512, [[1, 512]], 0, 20), func=AF.Exp),
             reads=[banks[bv]], writes=[R_t])
        S.op("dve", lambda e, bm=bm, tmp=tmp: e.tensor_tensor(out=v32(tmp, [[1, 512]], 0, 20),
                                                               in0=v32(tmp, [[1, 512]], 0, 20),
                                                               in1=pv32(bm * 512, [[1, 512]], 0, 20), op=ALU.mult),
             reads=[banks[bm], R_t], writes=[R_t])
        S.op("sp", lambda e, t=t, tmp=tmp: e.dma_start(out=er_d[t * 20:(t + 1) * 20, :],
                                                       in_=v32(tmp, [[1, 512]], 0, 20)),
             reads=[R_t], writes=[R_er], chan=ch_er)

    CV_OFF = R_OFF + 110592
    cv_names = [("wm", wm_d, 8 * 2816), ("wo", wo_d, 8192), ("wup", wup_d, NCP * 2048), ("wdn", wdn_d, 8 * 2816),
                ("wqkv", wqkv_d, 6 * 3072), ("wbkv", wbkv_d, 2048), ("wbq", wbq_d, 4096)]
    WB = {}
    R_cv = {}
    R_cvs = [Res("cvs0"), Res("cvs1")]
    ch_cvi = [S.chan(), S.chan()]
    ch_cvo = [S.chan(), S.chan()]
    cv_chunks = []
    for (nm, src_ap, ncols) in cv_names:
        WB[nm] = nc.dram_tensor(nm + "_b", [128, ncols], BF16, kind="Internal").ap()
        R_cv[nm] = Res("cv_" + nm)
        c = 0
        while c < ncols:
            w_ = min(4096, ncols - c)
            cv_chunks.append((nm, src_ap, ncols, c, w_))
            c += w_

    def conv_gen():
        def emit_in(k):
            nm, src_ap, ncols, c, w_ = cv_chunks[k]
            sl = k % 2
            S.op("pool", lambda e: e.dma_start(out=v16(CV_OFF + sl * 8192, [[1, w_]]),
                                               in_=dv(src_ap, c, [[ncols, 128], [1, w_]])),
                 writes=[R_cvs[sl]], chan=ch_cvi[sl])

        def emit_out(k):
            nm, src_ap, ncols, c, w_ = cv_chunks[k]
            sl = k % 2
            S.op("pool", lambda e: e.dma_start(out=dv(WB[nm], c, [[ncols, 128], [1, w_]]),
                                               in_=v16(CV_OFF + sl * 8192, [[1, w_]])),
                 reads=[R_cvs[sl]], writes=[R_cv[nm]], chan=ch_cvo[sl])

        for k in range(len(cv_chunks)):
            emit_in(k)
            if k >= 1:
                emit_out(k - 1)
            yield
        emit_out(len(cv_chunks) - 1)
        yield

    cvg = conv_gen()

    def conv_step(n):
        for _ in range(n):
            next(cvg, None)


    def chk(tag):
        if STOP[0] == tag:
            S.stopped = True

    units = [(0, False, True), (2048, True, False), (4096, False, False)]
    R_xn = Res("xn")
    R_oab = Res("oab")
    XNW = 3072

    def xn_ap(k, col, n, step=1, p0=0, np_=128):
        return v16(XN_OFF + 2 * (k * XNW + col), [[step, n]], p0, np_)

    R_save = Res("save")
    R_x1t = [Res("x1t%d" % i) for i in range(8)]
    R_x1c = Res("x1c")
    R_esave = Res("esave")
    R_dscr = Res("dscr")
    R_rscr = Res("rscr")
    R_rds = Res("rds")
    ch_out = S.chan()
    R_outsA = Res("outsA")
    R_outsB = Res("outsB")

    chk('prologue')
    for ui, (own0, left, right) in enumerate(units):
        ext0 = own0 - (1024 if left else 0)
        next_ = 2048 + (1024 if (left or right) else 0)
        ownc = own0 - ext0
        nblk = next_ // 128

        if ui > 0:
            S.barrier()
        R_wq = [Res("wq0"), Res("wq1")]
        ch_wq = [S.chan(), S.chan()]
        wq_i = [0]
        def load_wslab(src_ap, ncols, R_src):
            s = wq_i[0] % 2
            wq_i[0] += 1
            S.op("pool", lambda e, s=s: e.dma_start(out=v16(B_WQ + s * 6144, [[1, ncols]]), in_=src_ap),
                 reads=[R_src], writes=[R_wq[s]], chan=ch_wq[s])
            return s

        slab_specs = []
        R_nodep = Res("nodep")
        if ui == 0:
            for hp_ in range(2):
                for g_ in range(3):
                    slab_specs.append((dv(wqkv_d, (g_ * 2 + hp_) * 3072, [[6 * 3072, 128], [1, 3072]]), 3072, R_nodep))
            slab_specs.append((dv(wbkv_d, 0, [[2048, 128], [1, 2048]]), 2048, R_nodep))
            for ci_ in range(4):
                slab_specs.append((dv(wbq_d, ci_ * 1024, [[4096, 128], [1, 1024]]), 1024, R_nodep))
        else:
            for hp_ in range(2):
                for g_ in range(3):
                    slab_specs.append((dv(WB["wqkv"], (g_ * 2 + hp_) * 3072, [[6 * 3072, 128], [1, 3072]]), 3072,
                                       R_cv["wqkv"]))
            slab_specs.append((dv(WB["wbkv"], 0, [[2048, 128], [1, 2048]]), 2048, R_cv["wbkv"]))
            for ci_ in range(4):
                slab_specs.append((dv(WB["wbq"], ci_ * 1024, [[4096, 128], [1, 1024]]), 1024, R_cv["wbq"]))
        slab_state = {"i": 0, "slot": load_wslab(*slab_specs[0])}

        R_xb = [Res("xb%d" % i) for i in range(12)]
        ch_xb = [S.chan() for _ in range(12)]
        R_ssqs = [Res("ssq0"), Res("ssq1")]
        R_xs = [Res("xs%d" % i) for i in range(2)]
        R_junk = Res("junk")
        for bi_, b0 in enumerate(range(0, nblk, 6)):
            nb = min(6, nblk - b0)
            hf = bi_ % 2
            R_ssq = R_ssqs[hf]
            SSQ_ = SSQ + 32 * hf
            RSTD_ = RSTD + 32 * hf
            S.op("act", lambda e, SSQ_=SSQ_: e.activation(out=v32(SSQ_, [[1, 8]]), in_=v32(ZEROC, [[1, 8]]), func=AF.Copy),
                 reads=[R_const], writes=[R_ssq])
            for gi_, j0_ in enumerate(range(0, nb, 3)):
                ng = min(3, nb - j0_)
                sl0 = hf * 6 + j0_
                r0 = ext0 + (b0 + j0_) * 128
                S.op(("sp", "act")[gi_ % 2], lambda e, sl0=sl0, r0=r0, ng=ng: e.dma_start(
                    out=v32(A_XB + sl0 * 4096, [[1024, ng], [1, 1024]]),
                    in_=bass.AP(xin.tensor, xin.offset + r0 * D, [[D, 128], [128 * D, ng], [1, D]])),
                     writes=[R_xb[sl0 + q] for q in range(ng)], chan=ch_xb[sl0])
            for j in range(nb):
                b = b0 + j
                sl = hf * 6 + j
                S.op("act", lambda e, sl=sl, j=j, SSQ_=SSQ_: e.activation(out=v16(A_JUNK, [[1, 1024]]),
                                                        in_=v32(A_XB + sl * 4096, [[1, 1024]]), func=AF.Square,
                                                        accum_out=v32(SSQ_ + 4 * j, [[1, 1]])),
                     reads=[R_xb[sl]], writes=[R_junk, R_ssq])
            S.op("dve", lambda e, nb=nb, SSQ_=SSQ_, RSTD_=RSTD_: e.tensor_scalar(
                out=v32(RSTD_, [[1, nb]]), in0=v32(SSQ_, [[1, nb]]),
                scalar1=1.0 / D, scalar2=EPS, op0=ALU.mult, op1=ALU.add),
                 reads=[R_ssq], writes=[R_ssq])
            S.op("act", lambda e, nb=nb, RSTD_=RSTD_: e.activation(out=v32(RSTD_, [[1, nb]]), in_=v32(RSTD_, [[1, nb]]),
                                                      func=AF.Sqrt), reads=[R_ssq], writes=[R_ssq])
            S.op("dve", lambda e, nb=nb, RSTD_=RSTD_: e.reciprocal(out=v32(RSTD_, [[1, nb]]), in_=v32(RSTD_, [[1, nb]])),
                 reads=[R_ssq], writes=[R_ssq])
            pend_ev = None
            for j in range(nb):
                b = b0 + j
                s = b % 2
                sl = hf * 6 + j
                S.op("dve", lambda e, j=j, s=s, sl=sl, RSTD_=RSTD_: e.tensor_scalar(out=v16(A_XS + s * 2048, [[1, 1024]]),
                                                                in0=v32(A_XB + sl * 4096, [[1, 1024]]),
                                                                scalar1=v32(RSTD_ + 4 * j, [[1, 1]]), scalar2=None,
                                                                op0=ALU.mult),
                     reads=[R_xb[sl], R_ssq], writes=[R_xs[s]])
                bk = next_bank()
                mm_chain(S, cx, [(pv16(bk * 1024 + c * 128, [[1, 128]]), v16(A_XS + s * 2048 + c * 256, [[1, 128]]),
                                  v16(IDB, [[1, 128]])) for c in range(8)], [R_xs[s], R_const], [banks[bk]],
                         transpose=True)
                if pend_ev is not None:
                    pend_ev()
                pend_ev = (lambda b=b, bk=bk: S.op("dve", lambda e: e.tensor_tensor(
                    out=v16(XN_OFF + 2 * (b * 128), [[XNW, 8], [1, 128]]),
                    in0=pv16(bk * 1024, [[128, 8], [1, 128]]),
                    in1=v32(SMALL + 4 * SM_GA, [[1, 8], [0, 128]]), op=ALU.mult),
                    reads=[banks[bk], R_const], writes=[R_xn]))
            if pend_ev is not None:
                pend_ev()

        chk('ph1_%d' % ui)
        S.barrier()
        R_E = Res("E")
        ch_E = S.chan()
        R_es = [Res("es0"), Res("es1")]
        R_pt = [Res("pt0"), Res("pt1")]
        if ui == 0:
            R_stg = Res("stage")
            STG = B_ACC
            for g in range(3):
                dst = STG + g * 1024 * 4
                src = bass.AP(er_d.tensor, (g * 20 + 4 * g) * 512 - 64 + PAD - 127,
                              [[1, 128], [512, 4], [128, 2], [1, 128]])
                S.op(("sp", "act")[g % 2], lambda e, dst=dst, src=src: e.dma_start(
                    out=v32(dst, [[256, 4], [128, 2], [1, 128]]), in_=src),
                     reads=[R_er], writes=[R_stg], chan=ch_E)
            src = bass.AP(er_d.tensor, (3 * 20 + 12) * 512 - 128 + PAD - 127, [[1, 128], [512, 8], [128, 3], [1, 128]])
            S.op("act", lambda e, src=src: e.dma_start(out=v32(STG + 12288, [[384, 8], [128, 3], [1, 128]]), in_=src),
                 reads=[R_er], writes=[R_stg], chan=ch_E)
            S.op("dve", lambda e: e.tensor_copy(out=v32(B_EA, [[128, 48], [1, 128]]),
                                                in_=v32(STG + 127 * 4, [[128, 48], [-1, 128]])),
                 reads=[R_stg], writes=[R_E])
            ch_Es = S.chan()
            S.op("sp", lambda e: e.dma_start(out=esave_d, in_=v32(B_EA, [[1, 6144]])), reads=[R_E],
                 writes=[R_esave], chan=ch_Es)
            S.barrier()
        else:
            S.op("sp", lambda e: e.dma_start(out=v32(B_EA, [[1, 6144]]), in_=esave_d), reads=[R_esave],
                 writes=[R_E], chan=ch_E)
        chk('etab_%d' % ui)
        R_accs = [Res("acc0"), Res("acc1")]
        ACCB = [B_ACC, B_ACC2]
        acc_cur = {"i": 0}
        R_qt = Res("qt")
        R_kt = Res("kt")
        R_va = Res("va")
        R_rd = Res("rd")
        R_rs = Res("rs")
        ch_rs = S.chan()

        def proj_fm(s, wcol, wk, evac, col0, n):
            bk = next_bank()
            mm_chain(S, cx, [(pv32(bk * 512, [[1, n]]), v16(B_WQ + s * 6144 + 2 * (k * wk + wcol), [[1, 128]]),
                              xn_ap(k, col0, n)) for k in range(8)], [R_wq[s], R_xn], [banks[bk]])
            evac(bk)

        def attn(rq_list, kblocks_all, klo, nk, n, d, nside, e_offs, hh_list, hooks=None):
            SW = nside * 128
            groups = [hh_list] if nside == 2 else [[hh] for hh in hh_list]
            items = []
            for (r, qblocks) in rq_list:
                for (qs, v0, v1, kbs) in qblocks:
                    for gidx, grp in enumerate(groups):
                        items.append((r, qs, v0, v1, kbs, gidx, grp))

            def stage_s(idx):
                r, qs, v0, v1, kbs, gidx, grp = items[idx]
                w = v1 - v0
                c0 = v0 - qs
                bs = next_bank()
                es = idx % 2
                steps = []
                for gi, hh in enumerate(grp):
                    for (side, gb) in kbs:
                        st, ln = kblocks_all[gb]
                        steps.append((pv32(bs * 512 + gi * SW + side * 128 + c0, [[1, w]], 0, ln),
                                      v16(B_KT + 2 * (r * nk + (st - klo)), [[1, ln]], 64 * hh, 64),
                                      v16(B_QT + 2 * (r * n + v0), [[1, w]], 64 * hh, 64)))
                nst = len(steps)
                nper = len(kbs)
                prev_ev = None
                for i, (o, a, b_) in enumerate(steps):
                    boundary_next = (nside == 2 and (i + 1) % nper == 0 and i != nst - 1)
                    boundary_here = (nside == 2 and i % nper == 0 and i != 0)
                    ev_ = S.op("pe", lambda e, o=o, a=a, b_=b_: e.matmul(o, a, b_, start=True, stop=True),
                               reads=[R_kt, R_qt], writes=[banks[bs]] if i in (0, nst - 1) else (),
                               extra=[prev_ev] if (boundary_here and prev_ev) else (),
                               signal=(i == nst - 1) or boundary_next, self_wait=boundary_here)
                    if boundary_next:
                        prev_ev = ev_
                wd = len(grp) * SW
                S.op("act", lambda e, bs=bs, es=es, wd=wd: e.activation(
                    out=v32(B_ES + es * 2048, [[1, wd]]), in_=pv32(bs * 512, [[1, wd]]), func=AF.Exp,
                    scale=0.125), reads=[banks[bs]], writes=[R_es[es]])
                eoff = e_offs[gidx]
                S.op("dve", lambda e, es=es, wd=wd, eoff=eoff: e.tensor_tensor(
                    out=v16(B_PT + es * 1024, [[1, wd]]), in0=v32(B_ES + es * 2048, [[1, wd]]),
                    in1=v32(eoff, [[1, wd]]), op=ALU.mult), reads=[R_es[es], R_E], writes=[R_pt[es]])

            def stage_pv(idx):
                r, qs, v0, v1, kbs, gidx, grp = items[idx]
                w = v1 - v0
                c0 = v0 - qs
                es = idx % 2
                bo = next_bank()
                nh = len(grp)
                h0 = grp[0]
                cnt = 0
                tot = nh * len(kbs)
                for gi, hh in enumerate(grp):
                    for j, (side, gb) in enumerate(kbs):
                        st, ln = kblocks_all[gb]
                        cnt += 1
                        o = pv32(bo * 512 + hh * 128 + c0, [[1, w]])
                        a = v16(B_VA + 2 * (gb * 192 + hh * 64), [[1, 128]], 0, ln)
                        b_ = v16(B_PT + es * 1024 + 2 * (gi * SW + side * 128 + c0), [[1, w]], 0, ln)
                        S.op("pe", lambda e, o=o, a=a, b_=b_, j=j, nk_=len(kbs): e.matmul(
                            o, a, b_, start=(j == 0), stop=(j == nk_ - 1)),
                            reads=[R_va, R_pt[es]], writes=[banks[bo]] if cnt in (1, tot) else (),
                            signal=(cnt == tot))
                ab = ACCB[acc_cur["i"]]
                R_acc = R_accs[acc_cur["i"]]
                S.op("dve", lambda e, bo=bo, nh=nh, h0=h0, v0=v0, w=w, c0=c0, r=r, ab=ab: e.tensor_tensor(
                    out=v32(ab + 4 * (h0 * 2048 + v0 * d + r), [[2048, nh], [d, w]]),
                    in0=pv32(bo * 512 + h0 * 128 + c0, [[128, nh], [1, w]]),
                    in1=v32(ab + 4 * (h0 * 2048 + v0 * d + r), [[2048, nh], [d, w]]), op=ALU.add),
                    reads=[banks[bo], R_acc], writes=[R_acc])

            for idx in range(len(items) + 1):
                if idx < len(items):
                    stage_s(idx)
                if hooks and idx in hooks:
                    hooks[idx]()
                if idx >= 1:
                    stage_pv(idx - 1)

        def set_va_ones(nb_, halo_list):
            S.op("pool", lambda e: e.memset(v16(B_VA + 2 * 64, [[192, nb_], [1, 64]]), 1.0), writes=[R_va])
            for gb, hm in halo_list:
                S.op("pool", lambda e, gb=gb, hm=hm: e.tensor_copy(
                    out=v16(B_VA + 2 * (gb * 192 + 64), [[1, 64]]),
                    in_=v32(FLAG + 4 * hm, [[0, 64]])), reads=[R_const, R_va], writes=[R_va])

        def finalize_a(ai, sink_cols=None):
            ab = ACCB[ai]
            R_acc = R_accs[ai]
            S.op("sp", lambda e: e.dma_start(out=dscr_d[0:1, :], in_=v32(ab, [[1, 2048]], 64, 1)),
                 reads=[R_acc], writes=[R_dscr], chan=ch_rs)
            S.op("sp", lambda e: e.dma_start(out=dscr_d[1:2, :], in_=v32(ab + 8192, [[1, 2048]], 0, 1)),
                 reads=[R_acc], writes=[R_dscr], chan=ch_rs)
            S.op("sp", lambda e: e.dma_start(out=v32(RDS, [[1, 32]], 0, 64),
                                             in_=bass.AP(dscr_d.tensor, 0, [[32, 64], [1, 32]])),
                 reads=[R_dscr], writes=[R_rds], chan=ch_rs)
            S.op("sp", lambda e: e.dma_start(out=v32(RDS, [[1, 32]], 64, 64),
                                             in_=bass.AP(dscr_d.tensor, 2048, [[32, 64], [1, 32]])),
                 reads=[R_dscr], writes=[R_rds], chan=ch_rs)

        def finalize_b(ai, dst_chunk, sink_cols=None):
            ab = ACCB[ai]
            R_acc = R_accs[ai]
            if sink_cols is not None:
                for half, col in enumerate(sink_cols):
                    S.op("dve", lambda e, half=half, col=col: e.tensor_scalar(
                        out=v32(RDS, [[1, 32]], 64 * half, 64), in0=v32(RDS, [[1, 32]], 64 * half, 64),
                        scalar1=v32(ESINK + 4 * col, [[1, 1]], 64 * half, 64), scalar2=None, op0=ALU.add),
                        reads=[R_rds, R_const], writes=[R_rds])
            S.op("dve", lambda e: e.reciprocal(out=v32(RDS, [[1, 32]]), in_=v32(RDS, [[1, 32]])),
                 reads=[R_rds], writes=[R_rds])
            S.op("sp", lambda e: e.dma_start(out=bass.AP(rscr_d.tensor, 0, [[32, 128], [1, 32]]),
                                             in_=v32(RDS, [[1, 32]])),
                 reads=[R_rds], writes=[R_rscr], chan=ch_rs)
            S.op("sp", lambda e: e.dma_start(out=v32(B_RD, [[1, 2048]], 0, 64),
                                             in_=bass.AP(rscr_d.tensor, 0, [[0, 64], [1, 2048]])),
                 reads=[R_rscr], writes=[R_rs], chan=ch_rs)
            S.op("sp", lambda e: e.dma_start(out=v32(B_RD, [[1, 2048]], 64, 64),
                                             in_=bass.AP(rscr_d.tensor, 2048, [[0, 64], [1, 2048]])),
                 reads=[R_rscr], writes=[R_rs], chan=ch_rs)

        def finalize_c(ai, dst_chunk):
            ab = ACCB[ai]
            R_acc = R_accs[ai]
            S.op("dve", lambda e: e.tensor_tensor(
                out=v16(OAB_OFF + 2 * (dst_chunk * 2048), [[1, 2048]], 0, 64),
                in0=v32(ab, [[1, 2048]], 0, 64), in1=v32(B_RD, [[1, 2048]], 0, 64), op=ALU.mult),
                reads=[R_acc, R_rs], writes=[R_oab])
            S.op("dve", lambda e: e.tensor_tensor(
                out=v16(OAB_OFF + 2 * (dst_chunk * 2048), [[1, 2048]], 64, 64),
                in0=v32(ab + 8192, [[1, 2048]], 64, 64), in1=v32(B_RD, [[1, 2048]], 64, 64), op=ALU.mult),
                reads=[R_acc, R_rs], writes=[R_oab])

        pend = {"f": None}

        def fin_start(ai, dst_chunk, sink_cols=None):
            finalize_a(ai, sink_cols)
            pend["f"] = (ai, dst_chunk, sink_cols, 0)

        def fin_step():
            f = pend["f"]
            if f is None:
                return
            ai, dst_chunk, sink_cols, stage = f
            if stage == 0:
                finalize_b(ai, dst_chunk, sink_cols)
                pend["f"] = (ai, dst_chunk, sink_cols, 1)
            else:
                finalize_c(ai, dst_chunk)
                pend["f"] = None

        def fin_flush():
            while pend["f"] is not None:
                fin_step()

        def project_kv(s, wk, kcol, vcol, d, n, klo, khi, nk, kblocks):
            nbr = len(kblocks)
            ranges = []
            if left:
                ranges.append((klo * d, 0))
            ranges += [(o, o + 512) for o in range(0, 2048, 512)]
            if right:
                ranges.append((2048, 2048 + (khi - n) * d))
            tiles = []
            for (a, b_) in ranges:
                o = a
                while o < b_:
                    nn = min(512, b_ - o)
                    tiles.append((o, nn))
                    o += nn
            for (o, nn) in tiles:
                def evac(bk, o=o, nn=nn):
                    S.op("act", lambda e: e.activation(
                        out=v16(B_KT + 2 * (o // d - klo), [[1, nn // d], [nk, d]]),
                        in_=pv32(bk * 512, [[d, nn // d], [1, d]]), func=AF.Copy),
                        reads=[banks[bk]], writes=[R_kt])
                proj_fm(s, kcol, wk, evac, ownc + o, nn)
            for r in range(d):
                for bi, (st, ln) in enumerate(kblocks):
                    gb = r * nbr + bi
                    bk = next_bank()
                    col = ownc + st * d + r
                    mm_chain(S, cx, [(pv32(bk * 512, [[1, 128]], 0, ln), xn_ap(k, col, ln, d),
                                      v16(B_WQ + s * 6144 + 2 * (k * wk + vcol), [[1, 128]])) for k in range(8)],
                             [R_wq[s], R_xn], [banks[bk]])
                    hm = None
                    if st < 0:
                        hm = 1 if ln == 128 and st == -64 else 0
                    elif st >= n:
                        hm = 0
                    if hm is None:
                        S.op("act", lambda e, gb=gb, bk=bk, ln=ln: e.activation(
                            out=v16(B_VA + 2 * (gb * 192), [[128, 2], [1, 64]], 0, ln),
                            in_=pv32(bk * 512, [[64, 2], [1, 64]], 0, ln), func=AF.Copy),
                            reads=[banks[bk], R_va], writes=[R_va])
                    else:
                        S.op("dve", lambda e, gb=gb, bk=bk, ln=ln, hm=hm: e.tensor_scalar(
                            out=v16(B_VA + 2 * (gb * 192), [[128, 2], [1, 64]], 0, ln),
                            in0=pv32(bk * 512, [[64, 2], [1, 64]], 0, ln),
                            scalar1=v32(FLAG + 4 * hm, [[1, 1]], 0, ln), scalar2=None, op0=ALU.mult),
                            reads=[banks[bk], R_const, R_va], writes=[R_va])

        def take_slab():
            conv_step(3)
            cur = slab_state["slot"]
            slab_state["i"] += 1
            if slab_state["i"] < len(slab_specs):
                slab_state["slot"] = load_wslab(*slab_specs[slab_state["i"]])
            return cur

        for hp in range(2):
            acc_cur["i"] = hp % 2
            S.op("pool", lambda e, ab=ACCB[hp % 2]: e.memset(v32(ab, [[1, 4096]]), 0.0), writes=[R_accs[hp % 2]])
            for g in range(3):
                d = DILS[g]
                n, klo, khi, nk, kblocks, qblocks = geom(d, left, right)
                nbr = len(kblocks)
                ph = g * 2 + hp
                s = take_slab()
                halo_list = []
                for r in range(d):
                    for bi, (st, ln) in enumerate(kblocks):
                        if st < 0:
                            halo_list.append((r * nbr + bi, 1))
                        elif st >= n:
                            halo_list.append((r * nbr + bi, 0))
                allblocks = [(st, ln) for r in range(d) for (st, ln) in kblocks]
                set_va_ones(len(allblocks), halo_list)
                chk('va1_%d_%d_%d' % (ui, hp, g))
                for o in range(0, 2048, 512):
                    def evq(bk, o=o, d=d, n=n):
                        S.op("act", lambda e: e.activation(
                            out=v16(B_QT + 2 * (o // d), [[1, 512 // d], [n, d]]),
                            in_=pv32(bk * 512, [[d, 512 // d], [1, d]]), func=AF.Copy),
                            reads=[banks[bk]], writes=[R_qt])
                    proj_fm(s, 0, 384, evq, ownc + o, 512)
                chk('qproj_%d_%d_%d' % (ui, hp, g))
                fin_step()
                project_kv(s, 384, 128, 256, d, n, klo, khi, nk, kblocks)
                fin_step()
                chk('kvproj_%d_%d_%d' % (ui, hp, g))
                rq = []
                for r in range(d):
                    rq.append((r, [(qs, v0, v1, [(side, r * nbr + bi) for (side, bi) in kbs])
                                   for (qs, v0, v1, kbs) in qblocks]))
                attn(rq, allblocks, klo, nk, n, d, 2, [B_EA + ph * 2048], [0, 1])
                chk('attng_%d_%d_%d' % (ui, hp, g))
            fin_flush()
            fin_start(hp % 2, hp)
            chk('attnA%d_%d' % (hp, ui))

        n, klo, khi, nk, kblocks, qblocks = geom_b(left, right)
        s = take_slab()
        set_va_ones(len(kblocks), [(bi, 0) for bi, (st, ln) in enumerate(kblocks) if st < 0 or st >= n])
        fin_step()
        project_kv(s, 256, 0, 128, 1, n, klo, khi, nk, kblocks)
        fin_step()
        for ci in range(4):
            acc_cur["i"] = ci % 2
            S.op("pool", lambda e, ab=ACCB[ci % 2]: e.memset(v32(ab, [[1, 4096]]), 0.0), writes=[R_accs[ci % 2]])
            s = take_slab()
            for o in range(0, 2048, 512):
                def evq(bk, o=o):
                    S.op("act", lambda e: e.activation(out=v16(B_QT + 2 * o, [[1, 512]]),
                                                       in_=pv32(bk * 512, [[1, 512]]), func=AF.Copy),
                         reads=[banks[bk]], writes=[R_qt])
                proj_fm(s, 0, 128, evq, ownc + o, 512)
            fin_step()
            attn([(0, qblocks)], kblocks, klo, nk, n, 1, 3,
                 [B_EB + ci * 1536, B_EB + (4 + ci) * 1536], [0, 1], hooks={8: fin_step})
            fin_flush()
            fin_start(ci % 2, 2 + ci, (ci, 4 + ci))
        fin_flush()
        conv_step(40)

        chk('attn_%d' % ui)
        S.barrier()
        R_xr = [Res("xr0"), Res("xr1")]
        ch_xr = [S.chan(), S.chan()]
        R_wm = [Res("wm0"), Res("wm1")]
        ch_wm = [S.chan(), S.chan()]
        R_wup = [Res("wup0"), Res("wup1")]
        ch_wup = [S.chan(), S.chan()]
        R_wdn = [Res("wdn0"), Res("wdn1")]
        ch_wdn = [S.chan(), S.chan()]
        R_mg = Res("mg")
        R_xn1 = [Res("xn1_%d" % i) for i in range(8)]
        R_ht = [Res("ht%d" % i) for i in range(NCP)]
        R_ucg = Res("ucg")
        R_ucv = Res("ucv")
        R_ucg2 = Res("ucg2")
        R_ucv2 = Res("ucv2")
        R_tmp = [Res("tmp%d" % i) for i in range(6)]
        R_sq = [Res("sq0"), Res("sq1")]
        R_rb = [Res("rb0"), Res("rb1")]
        wmi = [0]
        outs_i = [0]
        wupi = [0]
        wdni = [0]
        xri = [0]

        class WStream:
            def __init__(self, items, slots):
                self.items = items
                self.slots = slots
                self.i = 0
                self._load(0)

            def _load(self, i):
                if i >= len(self.items):
                    return
                src, ncols, R_src = self.items[i]
                boff, R_w, ch_w = self.slots[i % len(self.slots)]
                S.op("pool", lambda e: e.dma_start(out=v16(boff, [[1, ncols]]), in_=src), reads=[R_src],
                     writes=[R_w], chan=ch_w)

            def take(self):
                i = self.i
                self.i += 1
                self._load(i + 1)
                boff, R_w, ch_w = self.slots[i % len(self.slots)]
                return boff, R_w

        wm_items = []
        for tt_ in range(4):
            for m_ in range(8):
                wm_items.append((dv(WB["wm"], m_ * 2816, [[8 * 2816, 128], [1, 2816]]), 2816, R_cv["wm"]))
            for m_ in range(0, 8, 2):
                wm_items.append((dv(WB["wo"], m_ * 1024, [[8192, 128], [1, 2048]]), 2048, R_cv["wo"]))
        wdn_items = [(dv(WB["wdn"], m_ * 2816, [[8 * 2816, 128], [1, 2816]]), 2816, R_cv["wdn"])
                     for _ in range(5 if ui == NUNIT - 1 else 4) for m_ in range(8)]
        wm_stream = WStream(wm_items, [(C_WM, R_wm[0], ch_wm[0]), (C_WM + 5632, R_wm[1], ch_wm[1])])
        wdn_stream = WStream(wdn_items, [(C_WDN, R_wdn[0], ch_wdn[0]), (C_WDN + 5632, R_wdn[1], ch_wdn[1])])

        def T(i):
            return C_TMP + i * 2048

        MT = [(C_OUTS, [R_outsA]), (C_OUTS + 2048, [R_outsB]), (C_SQ, [R_sq[0], R_sq[1]]), (C_SQ + 2048, [R_rb[0]])]

        def x1t(m, j0, nn, p0=0, np_=128):
            return v32(C_X1T + 4 * (m * 516 + j0), [[1, nn]], p0, np_)

        def rms_bc(j0, nn, rbi):
            bk = next_bank()
            for m in range(8):
                sq = m % 2
                if m % 2 == 0:
                    S.op("act", lambda e, m=m, sq=sq: e.activation(out=v16(C_SQ + sq * 1024, [[1, nn]]),
                                                                   in_=x1t(m, j0, nn), func=AF.Square),
                         reads=[R_x1t[m]], writes=[R_sq[sq]])
                else:
                    S.op("dve", lambda e, m=m, sq=sq: e.tensor_tensor(out=v16(C_SQ + sq * 1024, [[1, nn]]),
                                                                      in0=x1t(m, j0, nn), in1=x1t(m, j0, nn),
                                                                      op=ALU.mult),
                         reads=[R_x1t[m]], writes=[R_sq[sq]])
                S.op("pe", lambda e, m=m, sq=sq, bk=bk: e.matmul(pv32(bk * 512, [[1, nn]]), v16(ONESB, [[1, 128]]),
                                                              v16(C_SQ + sq * 1024, [[1, nn]]), start=(m == 0),
                                                              stop=(m == 7)),
                     reads=[R_sq[sq], R_const], writes=[banks[bk]], signal=True)
            rb = C_RB + rbi * 2048
            S.op("act", lambda e, bk=bk: e.activation(out=v32(rb, [[1, nn]]), in_=pv32(bk * 512, [[1, nn]]),
                                                      func=AF.Ln, bias=v32(EPSC, [[1, 1]]), scale=1.0 / D),
                 reads=[banks[bk], R_const], writes=[R_rb[rbi]])
            S.op("act", lambda e: e.activation(out=v32(rb, [[1, nn]]), in_=v32(rb, [[1, nn]]), func=AF.Exp,
                                               scale=-0.5),
                 reads=[R_rb[rbi]], writes=[R_rb[rbi]])

        def down_part(j0, nn, rhs_fn):
            for m in range(8):
                wb, R_w = wdn_stream.take()
                bk = next_bank()
                mm_chain(S, cx, [(pv32(bk * 512, [[1, nn]]), v16(wb + 2 * (k * 128), [[1, 128]]),
                                  rhs_fn(k)) for k in range(NCP)], [R_w], [banks[bk]],
                         step_reads=[[R_ht[k]] for k in range(NCP)])
                S.op("dve", lambda e, m=m, bk=bk: e.scalar_tensor_tensor(
                    out=x1t(m, j0, nn), in0=pv32(bk * 512, [[1, nn]]), scalar=0.5, in1=x1t(m, j0, nn),
                    op0=ALU.mult, op1=ALU.add), reads=[banks[bk], R_x1t[m]], writes=[R_x1t[m]])

        def final_part(j0, nn, tok0):
            R_rt = Res("rt")
            blocks = []
            st = 0
            while st < nn:
                ln = min(128, nn - st)
                blocks.append((st, ln))
                st += ln

            def do_t(bi):
                st, ln = blocks[bi]
                bp = next_bank_pair()
                mm_chain(S, cx, [(pv32(bp * 512 + m * 128, [[1, 128]], 0, ln), x1t(m, j0 + st, ln),
                                  v32(IDF, [[1, 128]])) for m in range(8)], R_x1t + [R_const],
                         [banks[bp], banks[bp + 1]], transpose=True)
                return bp

            def do_e(bi, bp):
                st, ln = blocks[bi]
                osl = outs_i[0] % 2
                outs_i[0] += 1
                ob = C_OUTS if osl == 0 else C_SQ
                ores = [R_outsA, R_outsB] if osl == 0 else [R_sq[0], R_sq[1], R_rb[0]]
                S.op("dve", lambda e: e.scalar_tensor_tensor(
                    out=v32(ob, [[1, 1024]], 0, ln), in0=pv32(bp * 512, [[1, 1024]], 0, ln),
                    scalar=v32(RT + 4 * bi, [[1, 1]], 0, ln), in1=v32(GFINB, [[1, 1024]], 0, ln),
                    op0=ALU.mult, op1=ALU.mult),
                    reads=[banks[bp], banks[bp + 1], R_rt, R_const], writes=ores)
                S.op("sp", lambda e, t0=tok0 + st: e.dma_start(out=yout[t0:t0 + ln, :],
                                                               in_=v32(ob, [[1, 1024]], 0, ln)),
                     reads=ores, writes=[], chan=ch_out)

            nb = len(blocks)
            pre = [do_t(bi) for bi in range(min(2, nb))]
            rms_bc(j0, nn, 1)
            bk = next_bank()
            for bi, (st, ln) in enumerate(blocks):
                mm_chain(S, cx, [(pv32(bk * 512 + bi * 128, [[1, 128]], 0, ln), v32(C_RB + 2048 + 4 * st, [[1, ln]]),
                                  v32(IDF, [[1, 128]]))], [R_rb[1], R_const], [banks[bk]], transpose=True)
                S.op("act", lambda e, bi=bi, ln=ln, bk=bk: e.activation(
                    out=v32(RT + 4 * bi, [[1, 1]], 0, ln), in_=pv32(bk * 512 + bi * 128, [[1, 1]], 0, ln),
                    func=AF.Copy), reads=[banks[bk], R_rt], writes=[R_rt])
            for bi in range(len(pre)):
                do_e(bi, pre[bi])
            for bi in range(len(pre), nb):
                do_e(bi, do_t(bi))

        def merge_gen(tt):
            ocol = ownc + tt * 512
            for m in range(8):
                wb, R_w = wm_stream.take()
                ba, bb, bga, bgb = next_bank(), next_bank(), next_bank(), next_bank()
                mm_chain(S, cx, [(pv32(ba * 512, [[1, 512]]), v16(wb + 2 * (k * 128), [[1, 128]]),
                                  v16(OAB_OFF + 2 * (k * 2048 + tt * 512), [[1, 512]])) for k in range(2)],
                         [R_w, R_oab], [banks[ba]])
                mm_chain(S, cx, [(pv32(bb * 512, [[1, 512]]), v16(wb + 2 * ((2 + k) * 128), [[1, 128]]),
                                  v16(OAB_OFF + 2 * ((2 + k) * 2048 + tt * 512), [[1, 512]])) for k in range(4)],
                         [R_w, R_oab], [banks[bb]])
                mm_chain(S, cx, [(pv32(bga * 512, [[1, 512]]), v16(wb + 2 * ((6 + k) * 128), [[1, 128]]),
                                  xn_ap(k, ocol, 512)) for k in range(8)], [R_w, R_xn], [banks[bga]])
                mm_chain(S, cx, [(pv32(bgb * 512, [[1, 512]]), v16(wb + 2 * ((14 + k) * 128), [[1, 128]]),
                                  xn_ap(k, ocol, 512)) for k in range(8)], [R_w, R_xn], [banks[bgb]])
                (ta_b, ta_r), (tb_b, tb_r) = MT[2 * (m % 2)], MT[2 * (m % 2) + 1]
                S.op("act", lambda e, m=m, bga=bga, ta_b=ta_b: e.activation(
                    out=v32(ta_b, [[1, 512]]), in_=pv32(bga * 512, [[1, 512]]), func=AF.Tanh,
                    bias=v32(BGH + 4 * m, [[1, 1]]), scale=0.5), reads=[banks[bga], R_const], writes=ta_r)
                S.op("act", lambda e, m=m, bgb=bgb, tb_b=tb_b: e.activation(
                    out=v32(tb_b, [[1, 512]]), in_=pv32(bgb * 512, [[1, 512]]), func=AF.Tanh,
                    bias=v32(BGH + 4 * (8 + m), [[1, 1]]), scale=0.5), reads=[banks[bgb], R_const],
                    writes=tb_r)
                S.op("dve", lambda e, ba=ba, ta_b=ta_b: e.scalar_tensor_tensor(
                    out=v32(ta_b, [[1, 512]]), in0=v32(ta_b, [[1, 512]]), scalar=1.0, in1=pv32(ba * 512, [[1, 512]]),
                    op0=ALU.add, op1=ALU.mult), reads=[banks[ba]] + ta_r, writes=ta_r)
                S.op("dve", lambda e, bb=bb, tb_b=tb_b: e.scalar_tensor_tensor(
                    out=v32(tb_b, [[1, 512]]), in0=v32(tb_b, [[1, 512]]), scalar=1.0, in1=pv32(bb * 512, [[1, 512]]),
                    op0=ALU.add, op1=ALU.mult), reads=[banks[bb]] + tb_r, writes=tb_r)
                S.op("dve", lambda e, m=m, ta_b=ta_b, tb_b=tb_b: e.tensor_tensor(
                    out=v16(C_MG + 2 * (m * 512), [[1, 512]]), in0=v32(ta_b, [[1, 512]]), in1=v32(tb_b, [[1, 512]]),
                    op=ALU.add), reads=ta_r + tb_r, writes=[R_mg])
                yield
            chk('merge_%d_%d' % (ui, tt))

        xslots = [(C_XR, R_xr[0], ch_xr[0]), (C_XR + 4096, R_xr[1], ch_xr[1]),
                  (C_WUP, R_wup[0], ch_wup[0]), (C_WUP + 4096, R_wup[1], ch_wup[1])]

        def load_x_tile(tt):
            c0_ = own0 + tt * 512
            for pi_ in range(2):
                xb_, R_xa, ch_x = xslots[2 * pi_]
                R_xb2 = xslots[2 * pi_ + 1][1]
                t0 = c0_ + pi_ * 256
                S.op(("sp", "act")[pi_], lambda e, xb_=xb_, t0=t0: e.dma_start(
                    out=v32(xb_, [[1024, 2], [1, 1024]]),
                    in_=bass.AP(xin.tensor, xin.offset + t0 * D, [[D, 128], [128 * D, 2], [1, D]])),
                    writes=[R_xa, R_xb2], chan=ch_x)

        def mid_phase(tt):
            gt = ui * 4 + tt
            c0 = own0 + tt * 512
            S.op("dve", lambda e: e.tensor_copy(out=v32(C_X1T, [[516, 8]]), in_=v32(X1C, [[1, 8]])),
                 reads=[R_x1c] + R_x1t, writes=R_x1t)
            for tb in range(4):
                xb_, R_x, ch_x = xslots[tb]
                bp = next_bank_pair()
                mm_chain(S, cx, [(pv32(bp * 512 + m * 128, [[1, 128]]), v32(xb_ + 4 * (m * 128), [[1, 128]]),
                                  v32(IDF, [[1, 128]])) for m in range(8)], [R_x, R_const],
                         [banks[bp], banks[bp + 1]], transpose=True)
                S.op("act", lambda e, bp=bp, tb=tb: e.activation(
                    out=v32(C_X1T + 4 * (1 + tb * 128), [[516, 8], [1, 128]]),
                    in_=pv32(bp * 512, [[128, 8], [1, 128]]), func=AF.Copy),
                    reads=[banks[bp], banks[bp + 1]] + R_x1t, writes=R_x1t)
            for m in range(8):
                if m % 2 == 0:
                    wb, R_w = wm_stream.take()
                else:
                    wb = wb + 2048
                bk = next_bank()
                mm_chain(S, cx, [(pv32(bk * 512, [[1, 512]]), v16(wb + 2 * (k * 128), [[1, 128]]),
                                  v16(C_MG + 2 * (k * 512), [[1, 512]])) for k in range(8)], [R_w, R_mg],
                         [banks[bk]])
                S.op("dve", lambda e, m=m, bk=bk: e.scalar_tensor_tensor(
                    out=x1t(m, 1, 512), in0=pv32(bk * 512, [[1, 512]]), scalar=0.5, in1=x1t(m, 1, 512),
                    op0=ALU.mult, op1=ALU.add), reads=[banks[bk], R_x1t[m]], writes=[R_x1t[m]])
            chk('y_%d_%d' % (ui, tt))
            rms_bc(1, 512, 0)
            for m in range(8):
                S.op("dve", lambda e, m=m: e.scalar_tensor_tensor(
                    out=v16(C_XN1 + 2 * (m * 512), [[1, 512]]), in0=x1t(m, 1, 512), scalar=sm(SM_GF + m),
                    in1=v32(C_RB, [[1, 512]]), op0=ALU.mult, op1=ALU.mult),
                    reads=[R_x1t[m], R_rb[0], R_const], writes=[R_xn1[m]])
            chk('xn1_%d_%d' % (ui, tt))
            wup_slots = [(C_WUP, R_wup[0], ch_wup[0]), (C_WUP + 4096, R_wup[1], ch_wup[1]),
                         (C_XR, R_xr[0], ch_xr[0]), (C_XR + 4096, R_xr[1], ch_xr[1])]

            def load_wup(cp):
                wb, R_w, ch_w = wup_slots[cp % 4]
                S.op("pool", lambda e, cp=cp, wb=wb: e.dma_start(
                    out=v16(wb, [[1, 2048]]), in_=dv(WB["wup"], cp * 2048, [[NCP * 2048, 128], [1, 2048]])),
                    reads=[R_cv["wup"]], writes=[R_w], chan=ch_w)

            def chain_cp(cp, st_):
                wb, R_w, ch_w = wup_slots[cp % 4]
                UCG_, UCV_ = (C_UCG, C_UCV) if st_ == 0 else (C_UCG2, C_UCV2)
                R_g, R_v = (R_ucg, R_ucv) if st_ == 0 else (R_ucg2, R_ucv2)
                iG, iV, iX = 3 * st_, 3 * st_ + 1, 3 * st_ + 2
                if cp + 3 < NCP:
                    load_wup(cp + 3)
                bg, bv = next_bank(), next_bank()
                mm_chain(S, cx, [(pv32(bg * 512, [[1, 512]]), v16(wb + 2 * (k * 256), [[1, 128]]),
                                  v16(C_XN1 + 2 * (k * 512), [[1, 512]])) for k in range(8)], [R_w],
                         [banks[bg]], step_reads=[[R_xn1[k]] for k in range(8)])
                mm_chain(S, cx, [(pv32(bv * 512, [[1, 512]]), v16(wb + 2 * (k * 256 + 128), [[1, 128]]),
                                  v16(C_XN1 + 2 * (k * 512), [[1, 512]])) for k in range(8)], [R_w],
                         [banks[bv]], step_reads=[[R_xn1[k]] for k in range(8)])
                yield
                for (UC, R_uc, bk, ch_i, ti, ceng) in ((UCG_, R_g, bg, cp, iG, "dve"), (UCV_, R_v, bv, NCP + cp, iV, "dve")):
                    S.op("act", lambda e, UC=UC, ch_i=ch_i: e.activation(
                        out=v32(UC, [[1, 2]]), in_=v32(SAVE + 8 * ch_i, [[1, 2]]), func=AF.Copy),
                        reads=[R_save, R_uc], writes=[R_uc])
                    S.op("act", lambda e, UC=UC, bk=bk: e.activation(
                        out=v32(UC + 8, [[1, 512]]), in_=pv32(bk * 512, [[1, 512]]), func=AF.Copy),
                        reads=[banks[bk], R_uc], writes=[R_uc])
                    yield
                    S.op("act", lambda e, UC=UC, ch_i=ch_i: e.activation(
                        out=v32(SAVE + 8 * ch_i, [[1, 2]]), in_=v32(UC + 4 * 512, [[1, 2]]), func=AF.Copy),
                        reads=[R_uc, R_save], writes=[R_save])
                    S.op("act", lambda e, UC=UC, ch_i=ch_i, ti=ti: e.activation(
                        out=v32(T(ti), [[1, 512]]), in_=v32(UC + 4, [[1, 512]]), func=AF.Identity,
                        bias=sm(SM_CB + ch_i), scale=sm(SM_CW + 44 + ch_i)),
                        reads=[R_uc, R_const], writes=[R_tmp[ti]])
                    yield
                    S.op(ceng, lambda e, UC=UC, ch_i=ch_i, ti=ti: e.scalar_tensor_tensor(
                        out=v32(T(ti), [[1, 512]]), in0=v32(UC, [[1, 512]]), scalar=sm(SM_CW + ch_i),
                        in1=v32(T(ti), [[1, 512]]), op0=ALU.mult, op1=ALU.add),
                        reads=[R_uc, R_const, R_tmp[ti]], writes=[R_tmp[ti]])
                    yield
                    S.op(ceng, lambda e, UC=UC, ch_i=ch_i, ti=ti: e.scalar_tensor_tensor(
                        out=v32(T(ti), [[1, 512]]), in0=v32(UC + 8, [[1, 512]]), scalar=sm(SM_CW + 88 + ch_i),
                        in1=v32(T(ti), [[1, 512]]), op0=ALU.mult, op1=ALU.add),
                        reads=[R_uc, R_const, R_tmp[ti]], writes=[R_tmp[ti]])
                    if tt == 0 and ui > 0:
                        w2x = W2NM if ui == 1 else W2N1
                        w0x = W0NM if ui == 1 else W0N1
                        S.op(ceng, lambda e, UC=UC, ch_i=ch_i, ti=ti, w2x=w2x: e.scalar_tensor_tensor(
                            out=v32(T(ti), [[1, 1]]), in0=v32(UC + 8, [[1, 1]]), scalar=v32(w2x + 4 * ch_i, [[1, 1]]),
                            in1=v32(T(ti), [[1, 1]]), op0=ALU.mult, op1=ALU.add),
                            reads=[R_uc, R_const, R_tmp[ti]], writes=[R_tmp[ti]])
                        S.op(ceng, lambda e, UC=UC, ch_i=ch_i, ti=ti, w0x=w0x: e.scalar_tensor_tensor(
                            out=v32(T(ti) + 4, [[1, 1]]), in0=v32(UC + 4, [[1, 1]]),
                            scalar=v32(w0x + 4 * ch_i, [[1, 1]]), in1=v32(T(ti) + 4, [[1, 1]]),
                            op0=ALU.mult, op1=ALU.add),
                            reads=[R_uc, R_const, R_tmp[ti]], writes=[R_tmp[ti]])
                    yield
                S.op("act", lambda e: e.activation(out=v32(T(iX), [[1, 512]]), in_=v32(T(iG), [[1, 512]]),
                                                   func=AF.Square, scale=math.sqrt(0.044715)),
                     reads=[R_tmp[iG]], writes=[R_tmp[iX]])
                yield
                S.op("dve", lambda e: e.scalar_tensor_tensor(
                    out=v32(T(iX), [[1, 512]]), in0=v32(T(iX), [[1, 512]]), scalar=1.0, in1=v32(T(iG), [[1, 512]]),
                    op0=ALU.add, op1=ALU.mult), reads=[R_tmp[iX], R_tmp[iG]], writes=[R_tmp[iX]])
                yield
                S.op("act", lambda e: e.activation(out=v32(T(iX), [[1, 512]]), in_=v32(T(iX), [[1, 512]]),
                                                   func=AF.Tanh, scale=GELU_K), reads=[R_tmp[iX]], writes=[R_tmp[iX]])
                yield
                S.op("dve", lambda e: e.scalar_tensor_tensor(
                    out=v32(T(iX), [[1, 512]]), in0=v32(T(iX), [[1, 512]]), scalar=1.0, in1=v32(T(iG), [[1, 512]]),
                    op0=ALU.add, op1=ALU.mult), reads=[R_tmp[iX], R_tmp[iG]], writes=[R_tmp[iX]])
                yield
                S.op("dve", lambda e, cp=cp: e.tensor_tensor(
                    out=v16(C_HT + 2 * (cp * 512), [[1, 512]]), in0=v32(T(iX), [[1, 512]]), in1=v32(T(iV), [[1, 512]]),
                    op=ALU.mult), reads=[R_tmp[iX], R_tmp[iV]], writes=[R_ht[cp]])

            load_wup(0)
            load_wup(1)
            load_wup(2)
            mg = merge_gen(tt + 1) if tt + 1 < 4 else None
            for cp0 in range(0, NCP, 2):
                if mg is not None and cp0 >= 10:
                    next(mg, None)
                gens = [chain_cp(cp0, 0), chain_cp(cp0 + 1, 1)]
                alive = [True, True]
                step = 0
                while any(alive):
                    for gi_ in range(2):
                        if not alive[gi_]:
                            continue
                        if gi_ == 1 and step < 0:
                            continue
                        try:
                            next(gens[gi_])
                        except StopIteration:
                            alive[gi_] = False
                    step += 1
            if mg is not None:
                for _ in mg:
                    pass
            chk('up_%d_%d' % (ui, tt))
            j0 = 1 if gt == 0 else 0
            down_part(j0, 512 - j0, lambda k, j0=j0: v16(C_HT + 2 * (k * 512 + j0), [[1, 512 - j0]]))
            S.op("dve", lambda e: e.tensor_copy(out=v32(X1C, [[1, 8]]),
                                                in_=v32(C_X1T + 4 * 512, [[516, 8]])),
                 reads=R_x1t + [R_x1c], writes=[R_x1c])
            return j0, c0

        load_x_tile(0)
        for _ in merge_gen(0):
            pass
        for tt in range(4):
            j0, c0 = mid_phase(tt)
            if tt + 1 < 4:
                load_x_tile(tt + 1)
            final_part(j0, 512 - j0, c0 - 1 + j0)
            chk('down_%d_%d' % (ui, tt))

        if ui == NUNIT - 1:
            cv, t1 = FL_CV, FL_CV + 176
            R_fl = Res("flush")
            S.op("dve", lambda e: e.tensor_copy(out=v32(C_X1T, [[516, 8]]), in_=v32(X1C, [[1, 8]])),
                 reads=[R_x1c] + R_x1t, writes=R_x1t)
            S.op("dve", lambda e: e.tensor_tensor(out=v32(cv, [[1, 44]]), in0=v32(SAVE, [[2, 44]]),
                                                  in1=sm(SM_CW, 44), op=ALU.mult),
                 reads=[R_save, R_const], writes=[R_fl])
            S.op("dve", lambda e: e.tensor_tensor(out=v32(t1, [[1, 44]]), in0=v32(SAVE + 4, [[2, 44]]),
                                                  in1=sm(SM_CW + 44, 44), op=ALU.mult),
                 reads=[R_save, R_const, R_fl], writes=[R_fl])
            S.op("dve", lambda e: e.tensor_tensor(out=v32(cv, [[1, 44]]), in0=v32(cv, [[1, 44]]),
                                                  in1=v32(t1, [[1, 44]]), op=ALU.add), reads=[R_fl], writes=[R_fl])
            S.op("dve", lambda e: e.tensor_tensor(out=v32(cv, [[1, 44]]), in0=v32(cv, [[1, 44]]),
                                                  in1=sm(SM_CB, 44), op=ALU.add), reads=[R_fl, R_const],
                 writes=[R_fl])
            S.op("dve", lambda e: e.tensor_tensor(out=v32(t1, [[1, 22]]), in0=v32(cv, [[1, 22]]),
                                                  in1=v32(cv, [[1, 22]]), op=ALU.mult), reads=[R_fl], writes=[R_fl])
            S.op("dve", lambda e: e.tensor_scalar(out=v32(t1, [[1, 22]]), in0=v32(t1, [[1, 22]]), scalar1=0.044715,
                                                  scalar2=1.0, op0=ALU.mult, op1=ALU.add), reads=[R_fl],
                 writes=[R_fl])
            S.op("dve", lambda e: e.tensor_tensor(out=v32(t1, [[1, 22]]), in0=v32(t1, [[1, 22]]),
                                                  in1=v32(cv, [[1, 22]]), op=ALU.mult), reads=[R_fl], writes=[R_fl])
            S.op("act", lambda e: e.activation(out=v32(t1, [[1, 22]]), in_=v32(t1, [[1, 22]]), func=AF.Tanh,
                                               scale=GELU_K), reads=[R_fl], writes=[R_fl])
            S.op("dve", lambda e: e.scalar_tensor_tensor(out=v32(t1, [[1, 22]]), in0=v32(t1, [[1, 22]]), scalar=1.0,
                                                         in1=v32(cv, [[1, 22]]), op0=ALU.add, op1=ALU.mult),
                 reads=[R_fl], writes=[R_fl])
            S.op("dve", lambda e: e.tensor_tensor(out=v16(FL_H, [[1, 22]]), in0=v32(t1, [[1, 22]]),
                                                  in1=v32(cv + 88, [[1, 22]]), op=ALU.mult), reads=[R_fl],
                 writes=R_ht)
            down_part(0, 1, lambda k: v16(FL_H + 2 * k, [[1, 1]]))
            final_part(0, 1, TOK - 1)

    S.stopped = False
    S.barrier()

    with nc.Block() as block:
        @block.sync
        def _(e):
            S.emit("sp", e)

        @block.scalar
        def _(e):
            S.emit("act", e)

        @block.vector
        def _(e):
            S.emit("dve", e)

        @block.gpsimd
        def _(e):
            S.emit("pool", e)

        @block.tensor
        def _(e):
            S.emit("pe", e)
    stack.close()
    return nc


def _kp(w):
    K = w.shape[0] // 128
    return np.ascontiguousarray(w.reshape(K, 128, w.shape[1]).transpose(1, 0, 2))


_PROG = None


def kernel(x_prompt, x_sample, g_attn, w_in, b_gate, rel_bias, sink, w_a_out, w_b_out, w_o,
           g_ffn, w_up, conv_w, conv_b, w_down, g_final):
    global _PROG
    f = np.float32
    x_prompt = np.asarray(x_prompt, f)
    x_sample = np.asarray(x_sample, f)
    w_in = np.asarray(w_in, f)[0]
    w_a_out = np.asarray(w_a_out, f)[0]
    w_b_out = np.asarray(w_b_out, f)[0]
    w_o = np.asarray(w_o, f)[0]
    w_up = np.asarray(w_up, f)[0]
    w_down = np.asarray(w_down, f)[0]
    conv_w = np.asarray(conv_w, f)[0]
    conv_b = np.asarray(conv_b, f)[0]
    g_attn = np.asarray(g_attn, f)[0]
    g_ffn = np.asarray(g_ffn, f)[0]
    b_gate = np.asarray(b_gate, f)[0]
    sink = np.asarray(sink, f)[0]
    g_final = np.asarray(g_final, f)
    rel_bias = np.asarray(rel_bias, f)

    winp = _kp(w_in)
    wqkv = np.zeros((128, 6, 8, 384), f)
    for g in range(3):
        for hp in range(2):
            c = (4 * g + 2 * hp) * 64
            ph = g * 2 + hp
            wqkv[:, ph, :, 0:128] = winp[:, :, c:c + 128]
            wqkv[:, ph, :, 128:256] = winp[:, :, 768 + c:768 + c + 128]
            wqkv[:, ph, :, 256:384] = winp[:, :, 1536 + c:1536 + c + 128]
    QB0 = 2304
    KB0 = QB0 + 512
    VB0 = KB0 + 128
    G0 = VB0 + 128
    wbkv = np.zeros((128, 8, 256), f)
    wbkv[:, :, 0:128] = winp[:, :, KB0:KB0 + 128]
    wbkv[:, :, 128:256] = winp[:, :, VB0:VB0 + 128]
    wbq = np.zeros((128, 4, 8, 128), f)
    for ci in range(4):
        wbq[:, ci, :, 0:64] = winp[:, :, QB0 + 64 * ci:QB0 + 64 * ci + 64]
        wbq[:, ci, :, 64:128] = winp[:, :, QB0 + 64 * (4 + ci):QB0 + 64 * (4 + ci) + 64]
    wap = _kp(w_a_out)
    wbo_perm = np.zeros((4, 128, 1024), f)
    for ci in range(4):
        wbo_perm[ci, 0:64] = w_b_out[64 * ci:64 * ci + 64]
        wbo_perm[ci, 64:128] = w_b_out[64 * (4 + ci):64 * (4 + ci) + 64]
    wbop = np.ascontiguousarray(wbo_perm.transpose(1, 0, 2))
    wm = np.zeros((128, 8, 22, 128), f)
    for m in range(8):
        ms = slice(m * 128, (m + 1) * 128)
        wm[:, m, 0:2] = wap[:, :, ms]
        wm[:, m, 2:6] = wbop[:, :, ms]
        wm[:, m, 6:14] = winp[:, :, G0 + m * 128:G0 + (m + 1) * 128]
        wm[:, m, 14:22] = winp[:, :, G0 + 1024 + m * 128:G0 + 1024 + (m + 1) * 128]
    wop = _kp(w_o)
    wo = np.ascontiguousarray(wop.reshape(128, 8, 8, 128).transpose(0, 2, 1, 3))
    wupp = _kp(w_up)
    wup = np.zeros((128, NCP, 8, 256), f)
    for cp in range(NCP):
        wup[:, cp, :, 0:128] = wupp[:, :, cp * 128:(cp + 1) * 128]
        wup[:, cp, :, 128:256] = wupp[:, :, DFF + cp * 128:DFF + (cp + 1) * 128]
    wdnp = _kp(w_down)
    wdn = np.ascontiguousarray(wdnp.reshape(128, NCP, 8, 128).transpose(0, 2, 1, 3))

    small = np.zeros((128, 224), f)
    small[:, 0:8] = g_attn.reshape(8, 128).T
    small[:, 8:16] = g_ffn.reshape(8, 128).T
    small[:, 16:24] = g_final.reshape(8, 128).T
    small[:, 24:40] = b_gate.reshape(16, 128).T
    for k in range(3):
        small[:, 40 + 44 * k:40 + 44 * (k + 1)] = conv_w[k].reshape(44, 128).T
    small[:, 172:216] = conv_b.reshape(44, 128).T
    small[:, 216:224] = sink[None, :]
    oh = _onehot_tables()
    ident = np.eye(128, dtype=f)

    common = dict(wqkv=wqkv.reshape(128, -1), wbkv=wbkv.reshape(128, -1), wbq=wbq.reshape(128, -1),
                  wm=wm.reshape(128, -1), wo=wo.reshape(128, -1), wup=wup.reshape(128, -1),
                  wdn=wdn.reshape(128, -1), small=small, relb=rel_bias, oh=oh, ident=ident,
                  gfinb=np.ascontiguousarray(np.broadcast_to(g_final[None, :], (128, 1024))))
    in_maps = []
    for c in range(NCORES):
        if c < 4:
            xs = np.concatenate([x_prompt[c], x_sample[c]], axis=0)
            fl = 1.0
        else:
            b = 4 + 3 * (c - 4)
            xs = np.concatenate([x_sample[b], x_sample[b + 1], x_sample[b + 2]], axis=0)
            fl = 0.0
        flagv = np.zeros((128, 4), f)
        flagv[:, 0] = fl
        flagv[:, 1] = 1.0
        flagv[:64, 1] = fl
        flagv[:, 2] = fl - 1.0
        m = dict(common)
        m["xin"] = np.ascontiguousarray(xs)
        m["flagv"] = flagv
        in_maps.append(m)

    if _PROG is None:
        _PROG = build_program()
    res = run_bass_kernel_spmd(_PROG, in_maps, core_ids=list(range(NCORES)))
    y_prompt = np.zeros_like(x_prompt)
    y_sample = np.zeros_like(x_sample)
    for c in range(NCORES):
        y = res.results[c]["yout"]
        if c < 4:
            y_prompt[c] = y[:4096]
            y_sample[c] = y[4096:]
        else:
            b = 4 + 3 * (c - 4)
            for i in range(3):
                y_sample[b + i] = y[2048 * i:2048 * (i + 1)]
    return y_prompt, y_sample
```
